# Optimizing a Trainium2 kernel written in Bass

```python
import math
import jax, jax.numpy as jnp
from jax import lax
import numpy as np

D_MODEL = 1024
BATCH = 8
SEQ = 2048
DEPTH = 2
DEC_BATCH = 128
DEC_SEQ = 1
PAST_LEN = 16384
PAGE_SIZE = 128

MIX_WIDTH = D_MODEL
N_MIXERS = 4
GW = MIX_WIDTH // N_MIXERS
HEAD_DIM = 64
NH = GW // HEAD_DIM
LORA_W = 32
LORA_A = 32
LORA_G = 64
RWKV_COLS = 3 * GW + LORA_W + LORA_A + LORA_G
RWKV_SPLITS = [GW, 2 * GW, 3 * GW, 3 * GW + LORA_W, 3 * GW + LORA_W + LORA_A]
RWKV_GN_EPS = 64e-5
HGRN_COLS = 4 * GW
SSM_STATE = 128
SSM_GROUPS = 2
HEADS_PER_GROUP = NH // SSM_GROUPS
CONV_WIDTH = 4
SSM_CONV_CH = GW + 2 * SSM_GROUPS * SSM_STATE
SSM_COLS = GW + SSM_CONV_CH + NH
MLSTM_COLS = 3 * GW + 2 * NH + GW
MLSTM_SPLITS = [GW, 2 * GW, 3 * GW, 3 * GW + NH, 3 * GW + 2 * NH]
IN_COLS = RWKV_COLS + HGRN_COLS + SSM_COLS + MLSTM_COLS
GROUP_SPLITS = [RWKV_COLS, RWKV_COLS + HGRN_COLS, RWKV_COLS + HGRN_COLS + SSM_COLS]
CHUNK = 64
N_MEM = 256
XATTN_HEADS = 4
XATTN_HEAD_DIM = D_MODEL // XATTN_HEADS
D_FF = 4 * D_MODEL
NORM_EPS = 1e-6

kernel_name = 'hybrid_rwkv7_hgrn2_mamba2_mlstm_decode_step'

f32 = jnp.float32


def rmsnorm(x, g):
    xf = x.astype(f32)
    y = xf * lax.rsqrt(jnp.mean(xf * xf, axis=-1, keepdims=True) + NORM_EPS)
    return (y * g.astype(f32)).astype(x.dtype)


def head_rmsnorm(o, g):
    b, t = o.shape[:2]
    o = o * lax.rsqrt(jnp.mean(o * o, axis=-1, keepdims=True) + NORM_EPS)
    return o.reshape(b, t, GW) * g.astype(f32)


def causal_conv(u, buf, w, bias):
    t = u.shape[1]
    cat = jnp.concatenate([buf.astype(f32), u], axis=1)
    out = bias.astype(f32) + sum(cat[:, j:j + t] * w[j].astype(f32) for j in range(CONV_WIDTH))
    return out, cat[:, -(CONV_WIDTH - 1):]


def rwkv7_mix(p, shift_prev, S0, lp):
    p = p.astype(f32)
    bsz, t, _ = p.shape
    prev_rows = jnp.concatenate([shift_prev.astype(f32)[:, None], p[:, :-1]], axis=1)
    pm = p + (prev_rows - p) * lp['rwkv_mu'].astype(f32)
    r, k, v, dw, da, dg = jnp.split(pm, RWKV_SPLITS, axis=-1)
    w_log = -jax.nn.softplus(-(lp['rwkv_w0'] + jnp.tanh(dw) @ lp['rwkv_w_up'].astype(f32))) - 0.5
    decay = jnp.exp(-jnp.exp(w_log))
    a = jax.nn.sigmoid(lp['rwkv_a0'] + da @ lp['rwkv_a_up'].astype(f32))
    g = jax.nn.sigmoid(dg) @ lp['rwkv_g_up'].astype(f32)
    heads = lambda z: z.reshape(bsz, t, NH, HEAD_DIM)
    kk = heads(k * lp['rwkv_k_k'])
    kk = kk / jnp.maximum(jnp.linalg.norm(kk, axis=-1, keepdims=True), 1e-12)
    k = k * (1.0 + (a - 1.0) * lp['rwkv_k_a'])
    r_h, k_h, v_h, w_h, a_h = heads(r), heads(k), heads(v), heads(decay), heads(a)

    def step(S, inp):
        r_t, k_t, v_t, w_t, kk_t, a_t = inp
        S = (S * w_t[:, :, None, :]
             - jnp.einsum('bhvk,bhk->bhv', S, kk_t)[..., None] * (kk_t * a_t)[:, :, None, :]
             + v_t[..., None] * k_t[:, :, None, :])
        return S, jnp.einsum('bhvk,bhk->bhv', S, r_t)

    xs = tuple(jnp.moveaxis(z, 1, 0) for z in (r_h, k_h, v_h, w_h, kk, a_h))
    S_T, o = lax.scan(step, S0.astype(f32), xs)
    o = jnp.moveaxis(o, 0, 1)
    mu = jnp.mean(o, axis=-1, keepdims=True)
    var = jnp.mean(jnp.square(o - mu), axis=-1, keepdims=True)
    o = ((o - mu) * lax.rsqrt(var + RWKV_GN_EPS)).reshape(bsz, t, GW) * lp['rwkv_ln_g'] + lp['rwkv_ln_b']
    bonus = (jnp.sum(r_h * k_h * lp['rwkv_r_k'].astype(f32), axis=-1, keepdims=True) * v_h).reshape(bsz, t, GW)
    return (o + bonus) * g, S_T, p[:, -1]


def hgrn2_mix(p, S0, lb, norm_g):
    p = p.astype(f32)
    bsz, t, _ = p.shape
    lb = lb.astype(f32)
    q_pre, f_pre, i_in, g = jnp.split(p, 4, axis=-1)
    q = jax.nn.silu(q_pre)
    log_f = jnp.log(lb + (1.0 - lb) * jax.nn.sigmoid(f_pre))
    k = (1.0 - lb) * jax.nn.sigmoid(-f_pre)
    L = math.gcd(t, CHUNK)
    nc = t // L
    chunks = lambda z: z.reshape(bsz, nc, L, NH, HEAD_DIM).transpose(1, 0, 3, 2, 4)
    causal = jnp.tril(jnp.ones((L, L), dtype=bool))[:, :, None]

    def step(S, inp):
        q_c, k_c, lf_c, v_c = inp
        b = jnp.cumsum(lf_c, axis=2)
        inter = jnp.einsum('bhtk,bhkv->bhtv', q_c * jnp.exp(b), S)
        diff = b[:, :, :, None, :] - b[:, :, None, :, :]
        dec = jnp.where(causal, jnp.exp(jnp.where(causal, diff, 0.0)), 0.0)
        A = jnp.einsum('bhtk,bhtsk,bhsk->bhts', q_c, dec, k_c)
        intra = jnp.einsum('bhts,bhsv->bhtv', A, v_c)
        b_end = b[:, :, -1]
        S = jnp.exp(b_end)[..., None] * S + jnp.einsum('bhsk,bhsv->bhkv', k_c * jnp.exp(b_end[:, :, None] - b), v_c)
        return S, inter + intra

    S_T, o = lax.scan(step, S0.astype(f32), tuple(map(chunks, (q, k, log_f, i_in))))
    o = o.transpose(1, 0, 3, 2, 4).reshape(bsz, t, NH, HEAD_DIM)
    return head_rmsnorm(o, norm_g) * jax.nn.silu(g), S_T


def mamba2_mix(p, conv_buf, h0, conv_w, conv_b, dt_bias, A_log, D, norm_g):
    p = p.astype(f32)
    bsz, t, _ = p.shape
    z, xbc, dt_pre = jnp.split(p, [GW, GW + SSM_CONV_CH], axis=-1)
    xbc, new_buf = causal_conv(xbc, conv_buf, conv_w, conv_b)
    xbc = jax.nn.silu(xbc)
    xs, Bm, Cm = jnp.split(xbc, [GW, GW + SSM_GROUPS * SSM_STATE], axis=-1)
    x_h = xs.reshape(bsz, t, NH, HEAD_DIM)
    B_h = jnp.repeat(Bm.reshape(bsz, t, SSM_GROUPS, SSM_STATE), HEADS_PER_GROUP, axis=2)
    C_h = jnp.repeat(Cm.reshape(bsz, t, SSM_GROUPS, SSM_STATE), HEADS_PER_GROUP, axis=2)
    dt = jax.nn.softplus(dt_pre + dt_bias.astype(f32))
    la = dt * (-jnp.exp(A_log.astype(f32)))
    L = math.gcd(t, CHUNK)
    nc = t // L
    chunks = lambda u: u.reshape(bsz, nc, L, *u.shape[2:]).swapaxes(0, 1)
    causal = jnp.tril(jnp.ones((L, L), dtype=bool))[None, :, :, None]

    def step(h, inp):
        x_c, dt_c, la_c, B_c, C_c = inp
        cum = jnp.cumsum(la_c, axis=1)
        seg = cum[:, :, None, :] - cum[:, None, :, :]
        Lm = jnp.where(causal, jnp.exp(jnp.where(causal, seg, 0.0)), 0.0)
        scores = jnp.einsum('bthn,bshn->btsh', C_c, B_c) * Lm
        y = jnp.einsum('btsh,bsh,bshp->bthp', scores, dt_c, x_c)
        y = y + jnp.einsum('bthn,bhpn->bthp', C_c, h) * jnp.exp(cum)[..., None]
        w_end = jnp.exp(cum[:, -1:] - cum) * dt_c
        h = jnp.exp(cum[:, -1])[:, :, None, None] * h + jnp.einsum('bsh,bshp,bshn->bhpn', w_end, x_c, B_c)
        return h, y

    h_T, y = lax.scan(step, h0.astype(f32), tuple(map(chunks, (x_h, dt, la, B_h, C_h))))
    y = y.swapaxes(0, 1).reshape(bsz, t, NH, HEAD_DIM) + x_h * D.astype(f32)[:, None]
    y = (y.reshape(bsz, t, GW) * jax.nn.silu(z)).reshape(bsz, t, SSM_GROUPS, GW // SSM_GROUPS)
    y = y * lax.rsqrt(jnp.mean(y * y, axis=-1, keepdims=True) + NORM_EPS)
    return y.reshape(bsz, t, GW) * norm_g.astype(f32), h_T, new_buf


def mlstm_mix(p, C0, n0, m0, i_bias, f_bias, norm_g):
    p = p.astype(f32)
    bsz, t, _ = p.shape
    q, k, v, i_pre, f_pre, o_pre = jnp.split(p, MLSTM_SPLITS, axis=-1)
    heads = lambda z: z.reshape(bsz, t, NH, HEAD_DIM)
    q, k, v = heads(q), heads(k) * (HEAD_DIM ** -0.5), heads(v)
    log_i = i_pre + i_bias.astype(f32)
    log_f = jax.nn.log_sigmoid(f_pre + f_bias.astype(f32))
    L = math.gcd(t, CHUNK)
    nc = t // L
    chunks = lambda u: u.reshape(bsz, nc, L, *u.shape[2:]).swapaxes(0, 1)
    causal = jnp.tril(jnp.ones((L, L), dtype=bool))[None, :, :, None]

    def step(carry, inp):
        C, n, m = carry
        q_c, k_c, v_c, li_c, lf_c = inp
        b = jnp.cumsum(lf_c, axis=1)
        log_inter = b + m[:, None, :]
        log_intra = jnp.where(causal, b[:, :, None, :] - b[:, None, :, :] + li_c[:, None, :, :], -jnp.inf)
        m_t = jnp.maximum(log_inter, jnp.max(log_intra, axis=2))
        w_inter = jnp.exp(log_inter - m_t)
        w_intra = jnp.exp(log_intra - m_t[:, :, None, :])
        qk = jnp.einsum('bthd,bshd->btsh', q_c, k_c) * w_intra
        num = jnp.einsum('btsh,bshd->bthd', qk, v_c) + w_inter[..., None] * jnp.einsum('bhvk,bthk->bthv', C, q_c)
        den = jnp.sum(qk, axis=2) + w_inter * jnp.einsum('bhk,bthk->bth', n, q_c)
        h = num / jnp.maximum(jnp.abs(den), jnp.exp(-m_t))[..., None]
        m_end = m_t[:, -1]
        g_prev = jnp.exp(b[:, -1] + m - m_end)
        g_s = jnp.exp(b[:, -1:] - b + li_c - m_end[:, None])
        C = g_prev[..., None, None] * C + jnp.einsum('bsh,bshv,bshk->bhvk', g_s, v_c, k_c)
        n = g_prev[..., None] * n + jnp.einsum('bsh,bshk->bhk', g_s, k_c)
        return (C, n, m_end), h

    (C_T, n_T, m_T), h = lax.scan(step, (C0.astype(f32), n0.astype(f32), m0.astype(f32)),
                                  tuple(map(chunks, (q, k, v, log_i, log_f))))
    h = h.swapaxes(0, 1).reshape(bsz, t, NH, HEAD_DIM)
    return head_rmsnorm(h, norm_g) * jax.nn.sigmoid(o_pre), C_T, n_T, m_T


def hgrn_lower_bounds(logits):
    pr = jax.nn.softmax(logits.astype(f32), axis=0)
    return jnp.cumsum(pr, axis=0) - pr[0:1]


def memory_kv(mem, g, wk, wv):
    mn = rmsnorm(mem, g)
    b = mem.shape[0]
    return (mn @ wk).reshape(b, N_MEM, XATTN_HEADS, XATTN_HEAD_DIM), (mn @ wv).reshape(b, N_MEM, XATTN_HEADS, XATTN_HEAD_DIM)


def cross_attend(xn, mem_k, mem_v, wq, wo):
    b, t, _ = xn.shape
    q = (xn @ wq).reshape(b, t, XATTN_HEADS, XATTN_HEAD_DIM)
    s = jnp.einsum('bthd,bmhd->bhtm', q, mem_k.astype(q.dtype)).astype(f32) * (XATTN_HEAD_DIM ** -0.5)
    pr = jax.nn.softmax(s, axis=-1).astype(xn.dtype)
    o = jnp.einsum('bhtm,bmhd->bthd', pr, mem_v.astype(xn.dtype)).reshape(b, t, D_MODEL)
    return o @ wo


def decoder_layer(x, st, mem_k, mem_v, lp):
    wkv, shift, hgrn_S, ssm_h, conv_buf, mC, mn, mm = st
    proj = rmsnorm(x, lp['norm_mix_g']) @ lp['w_in']
    p_r, p_h, p_s, p_m = jnp.split(proj, GROUP_SPLITS, axis=-1)
    o_r, wkv, shift = rwkv7_mix(p_r, shift, wkv, lp)
    o_h, hgrn_S = hgrn2_mix(p_h, hgrn_S, lp['hgrn_lb'], lp['hgrn_norm_g'])
    o_s, ssm_h, conv_buf = mamba2_mix(p_s, conv_buf, ssm_h, lp['ssm_conv_w'], lp['ssm_conv_b'],
                                      lp['ssm_dt_bias'], lp['ssm_A_log'], lp['ssm_D'], lp['ssm_norm_g'])
    o_m, mC, mn, mm = mlstm_mix(p_m, mC, mn, mm, lp['mlstm_i_bias'], lp['mlstm_f_bias'], lp['mlstm_norm_g'])
    mixed = jnp.concatenate([o_r, o_h, o_s, o_m], axis=-1).astype(x.dtype)
    x = x + mixed @ lp['w_out']
    x = x + cross_attend(rmsnorm(x, lp['norm_x_g']), mem_k, mem_v, lp['xattn_wq'], lp['xattn_wo'])
    hf = rmsnorm(x, lp['norm_ff_g']) @ lp['ff_w1']
    x = x + jnp.square(jax.nn.relu(hf)) @ lp['ff_w2']
    return x, (wkv, shift, hgrn_S, ssm_h, conv_buf, mC, mn, mm)


def zero_state(bsz):
    return (jnp.zeros((bsz, NH, HEAD_DIM, HEAD_DIM), f32), jnp.zeros((bsz, RWKV_COLS), f32),
            jnp.zeros((bsz, NH, HEAD_DIM, HEAD_DIM), f32), jnp.zeros((bsz, NH, HEAD_DIM, SSM_STATE), f32),
            jnp.zeros((bsz, CONV_WIDTH - 1, SSM_CONV_CH), f32), jnp.zeros((bsz, NH, HEAD_DIM, HEAD_DIM), f32),
            jnp.zeros((bsz, NH, HEAD_DIM), f32), jnp.zeros((bsz, NH), f32))


def setup_inputs(seed: int = 0) -> dict:
    key = jax.random.key(seed)
    ks = iter(jax.random.split(key, 64))
    nrm = lambda shape, scale=1.0: jax.random.normal(next(ks), shape, f32) * scale
    unif = lambda shape, lo, hi: jax.random.uniform(next(ks), shape, f32, lo, hi)
    gain = lambda shape: 1.0 + nrm(shape, 0.02)
    dt0 = jnp.exp(unif((DEPTH, NH), math.log(1e-3), math.log(1e-1)))
    return {
        'x_prompt': nrm((BATCH, SEQ, D_MODEL)),
        'x_sample': nrm((DEC_BATCH, DEC_SEQ, D_MODEL)),
        'state_rwkv_wkv': nrm((DEPTH, DEC_BATCH, NH, HEAD_DIM, HEAD_DIM), 0.3),
        'state_rwkv_shift': nrm((DEPTH, DEC_BATCH, RWKV_COLS)),
        'state_hgrn': nrm((DEPTH, DEC_BATCH, NH, HEAD_DIM, HEAD_DIM), 0.3),
        'state_ssm': nrm((DEPTH, DEC_BATCH, NH, HEAD_DIM, SSM_STATE), 0.3),
        'state_ssm_conv': nrm((DEPTH, DEC_BATCH, CONV_WIDTH - 1, SSM_CONV_CH)),
        'state_mlstm_C': nrm((DEPTH, DEC_BATCH, NH, HEAD_DIM, HEAD_DIM), 0.3),
        'state_mlstm_n': nrm((DEPTH, DEC_BATCH, NH, HEAD_DIM), 0.3),
        'state_mlstm_m': nrm((DEPTH, DEC_BATCH, NH)),
        'cache_mem_k': nrm((DEPTH, DEC_BATCH, N_MEM, XATTN_HEADS, XATTN_HEAD_DIM)),
        'cache_mem_v': nrm((DEPTH, DEC_BATCH, N_MEM, XATTN_HEADS, XATTN_HEAD_DIM)),
        'mem_prompt': nrm((BATCH, N_MEM, D_MODEL)),
        'norm_mix_g': gain((DEPTH, D_MODEL)),
        'w_in': nrm((DEPTH, D_MODEL, IN_COLS), D_MODEL ** -0.5),
        'w_out': nrm((DEPTH, MIX_WIDTH, D_MODEL), MIX_WIDTH ** -0.5),
        'rwkv_mu': unif((DEPTH, RWKV_COLS), 0.0, 1.0),
        'rwkv_w0': unif((DEPTH, GW), -5.0, 1.0),
        'rwkv_w_up': nrm((DEPTH, LORA_W, GW), 0.3 * LORA_W ** -0.5),
        'rwkv_a0': nrm((DEPTH, GW), 0.1),
        'rwkv_a_up': nrm((DEPTH, LORA_A, GW), 0.3 * LORA_A ** -0.5),
        'rwkv_g_up': nrm((DEPTH, LORA_G, GW), LORA_G ** -0.5),
        'rwkv_k_k': 0.85 + nrm((DEPTH, GW), 0.02),
        'rwkv_k_a': gain((DEPTH, GW)),
        'rwkv_r_k': nrm((DEPTH, NH, HEAD_DIM), 0.1),
        'rwkv_ln_g': gain((DEPTH, GW)),
        'rwkv_ln_b': nrm((DEPTH, GW), 0.02),
        'hgrn_lb_logits': nrm((DEPTH, GW), 0.5),
        'hgrn_norm_g': gain((DEPTH, GW)),
        'ssm_conv_w': nrm((DEPTH, CONV_WIDTH, SSM_CONV_CH), 0.5),
        'ssm_conv_b': nrm((DEPTH, SSM_CONV_CH), 0.02),
        'ssm_dt_bias': dt0 + jnp.log(-jnp.expm1(-dt0)),
        'ssm_A_log': jnp.log(unif((DEPTH, NH), 1.0, 16.0)),
        'ssm_D': gain((DEPTH, NH)),
        'ssm_norm_g': gain((DEPTH, GW)),
        'mlstm_i_bias': nrm((DEPTH, NH), 0.1),
        'mlstm_f_bias': unif((DEPTH, NH), 3.0, 6.0),
        'mlstm_norm_g': gain((DEPTH, GW)),
        'norm_x_g': gain((DEPTH, D_MODEL)),
        'norm_mem_g': gain((DEPTH, D_MODEL)),
        'xattn_wq': nrm((DEPTH, D_MODEL, D_MODEL), D_MODEL ** -0.5),
        'xattn_wk': nrm((DEPTH, D_MODEL, D_MODEL), D_MODEL ** -0.5),
        'xattn_wv': nrm((DEPTH, D_MODEL, D_MODEL), D_MODEL ** -0.5),
        'xattn_wo': nrm((DEPTH, D_MODEL, D_MODEL), D_MODEL ** -0.5),
        'norm_ff_g': gain((DEPTH, D_MODEL)),
        'ff_w1': nrm((DEPTH, D_MODEL, D_FF), D_MODEL ** -0.5),
        'ff_w2': nrm((DEPTH, D_FF, D_MODEL), D_FF ** -0.5),
        'final_norm_g': gain((D_MODEL,)),
    }


def reference(x_prompt, x_sample, state_rwkv_wkv, state_rwkv_shift, state_hgrn, state_ssm, state_ssm_conv,
              state_mlstm_C, state_mlstm_n, state_mlstm_m, cache_mem_k, cache_mem_v, mem_prompt,
              norm_mix_g, w_in, w_out, rwkv_mu, rwkv_w0, rwkv_w_up, rwkv_a0, rwkv_a_up, rwkv_g_up,
              rwkv_k_k, rwkv_k_a, rwkv_r_k, rwkv_ln_g, rwkv_ln_b, hgrn_lb_logits, hgrn_norm_g,
              ssm_conv_w, ssm_conv_b, ssm_dt_bias, ssm_A_log, ssm_D, ssm_norm_g,
              mlstm_i_bias, mlstm_f_bias, mlstm_norm_g, norm_x_g, norm_mem_g,
              xattn_wq, xattn_wk, xattn_wv, xattn_wo, norm_ff_g, ff_w1, ff_w2, final_norm_g):
    lb_all = hgrn_lower_bounds(hgrn_lb_logits)
    hp, hs = x_prompt, x_sample
    prompt_states, sample_states, prompt_mem = [], [], []
    for l in range(DEPTH):
        lp = {
            'norm_mix_g': norm_mix_g[l], 'w_in': w_in[l], 'w_out': w_out[l],
            'rwkv_mu': rwkv_mu[l], 'rwkv_w0': rwkv_w0[l], 'rwkv_w_up': rwkv_w_up[l], 'rwkv_a0': rwkv_a0[l],
            'rwkv_a_up': rwkv_a_up[l], 'rwkv_g_up': rwkv_g_up[l], 'rwkv_k_k': rwkv_k_k[l], 'rwkv_k_a': rwkv_k_a[l],
            'rwkv_r_k': rwkv_r_k[l], 'rwkv_ln_g': rwkv_ln_g[l], 'rwkv_ln_b': rwkv_ln_b[l],
            'hgrn_lb': lb_all[l], 'hgrn_norm_g': hgrn_norm_g[l],
            'ssm_conv_w': ssm_conv_w[l], 'ssm_conv_b': ssm_conv_b[l], 'ssm_dt_bias': ssm_dt_bias[l],
            'ssm_A_log': ssm_A_log[l], 'ssm_D': ssm_D[l], 'ssm_norm_g': ssm_norm_g[l],
            'mlstm_i_bias': mlstm_i_bias[l], 'mlstm_f_bias': mlstm_f_bias[l], 'mlstm_norm_g': mlstm_norm_g[l],
            'norm_x_g': norm_x_g[l], 'xattn_wq': xattn_wq[l], 'xattn_wo': xattn_wo[l],
            'norm_ff_g': norm_ff_g[l], 'ff_w1': ff_w1[l], 'ff_w2': ff_w2[l],
        }
        mk, mv = memory_kv(mem_prompt, norm_mem_g[l], xattn_wk[l], xattn_wv[l])
        hp, st_p = decoder_layer(hp, zero_state(hp.shape[0]), mk, mv, lp)
        prompt_states.append(st_p)
        prompt_mem.append((mk, mv))
        st_in = (state_rwkv_wkv[l], state_rwkv_shift[l], state_hgrn[l], state_ssm[l], state_ssm_conv[l],
                 state_mlstm_C[l], state_mlstm_n[l], state_mlstm_m[l])
        hs, st_s = decoder_layer(hs, st_in, cache_mem_k[l], cache_mem_v[l], lp)
        sample_states.append(st_s)
    y_prompt = rmsnorm(hp, final_norm_g)
    y_sample = rmsnorm(hs, final_norm_g)
    p_wkv, p_shift, p_hgrn, p_ssm, p_conv, p_C, p_n, p_m = [jnp.stack(s) for s in zip(*prompt_states)]
    s_wkv, s_shift, s_hgrn, s_ssm, s_conv, s_C, s_n, s_m = [jnp.stack(s) for s in zip(*sample_states)]
    p_mem_k = jnp.stack([kv[0] for kv in prompt_mem])
    p_mem_v = jnp.stack([kv[1] for kv in prompt_mem])
    return (y_prompt, y_sample, p_wkv, p_shift, p_hgrn, p_ssm, p_conv, p_C, p_n, p_m, p_mem_k, p_mem_v,
            s_wkv, s_shift, s_hgrn, s_ssm, s_conv, s_C, s_n, s_m)
```

```python
import numpy as np
import concourse.bass as bass
import concourse.mybir as mybir
from concourse.bass_utils import run_bass_kernel_spmd
from contextlib import ExitStack

F32 = mybir.dt.float32
BF16 = mybir.dt.bfloat16
AF = mybir.ActivationFunctionType
ALU = mybir.AluOpType
AX = mybir.AxisListType


class Region:
    __slots__ = ("name", "lw", "rd", "excl")

    def __init__(self, name):
        self.name = name
        self.excl = False
        self.lw = None
        self.rd = {}


class V:
    __slots__ = ("ap", "regs")

    def __init__(self, ap, regs):
        self.ap = ap
        self.regs = regs

    def __getitem__(self, idx):
        return V(self.ap[idx], self.regs)

    def with_ap(self, ap):
        return V(ap, self.regs)


class EngQ:
    def __init__(self, name, sem):
        self.name = name
        self.sem = sem
        self.count = 0
        self.ops = []
        self.known = {}
        self.known_dma = {}
        self.marks = set()


class Prog:
    NDMA = 12

    def __init__(self, nc, es):
        self.nc = nc
        self.es = es
        self.q = {}
        for n in ("pe", "dve", "act", "pool", "sp"):
            self.q[n] = EngQ(n, es.enter_context(nc.semaphore("s_" + n)))
        self.dsem = {}
        for n in ("sp", "pool", "act"):
            self.dsem[n] = [[es.enter_context(nc.semaphore("d_%s%d" % (n, i))), 0]
                            for i in range(self.NDMA)]
        self.dnext = {"sp": 0, "pool": 0, "act": 0}
        self.nregion = 0
        self.ninstr = 0

    def sb(self, name, shape, dt):
        t = self.es.enter_context(self.nc.sbuf_tensor("sb_" + name, list(shape), dt))
        return V(t[:], (Region(name),))

    def ps(self, name, shape, dt=F32):
        t = self.es.enter_context(self.nc.psum_tensor("ps_" + name, list(shape), dt))
        r = Region(name)
        r.excl = True
        return V(t[:], (r,))

    def dram(self, ap, name):
        return V(ap, (Region(name),))

    def _deps(self, eng, reads, writes):
        deps = []
        for v in reads:
            for r in v.regs:
                if r.lw is not None:
                    deps.append(r.lw)
        for v in writes:
            for r in v.regs:
                if r.lw is not None:
                    deps.append(r.lw)
                for en, tk in r.rd.items():
                    if en != eng:
                        deps.append(tk)
        return deps

    def emit(self, eng, fn, reads, writes, dma=False):
        Q = self.q[eng]
        ex = [v for v in reads if any(r.excl for r in v.regs)]
        if ex:
            writes = list(writes) + ex
        deps = self._deps(eng, reads, writes)
        waits = {}
        for tk in deps:
            if tk[0] == "c":
                _, en, idx = tk
                if en == eng and eng == "pe":
                    continue
                if en == eng and not dma:
                    pass
                if Q.known.get(en, 0) >= idx:
                    continue
                key = ("c", en)
                if waits.get(key, 0) < idx:
                    waits[key] = idx
            else:
                _, qn, si, val = tk
                if Q.known_dma.get((qn, si), 0) >= val:
                    continue
                key = ("d", qn, si)
                if waits.get(key, 0) < val:
                    waits[key] = val
        wl = []
        for key, val in waits.items():
            if key[0] == "c":
                wl.append(("c", key[1], val))
                self.q[key[1]].marks.add(val)
                Q.known[key[1]] = val
            else:
                wl.append((self.dsem[key[1]][key[2]][0], val))
                Q.known_dma[(key[1], key[2])] = val
        if dma:
            i = self.dnext[eng]
            self.dnext[eng] = (i + 1) % self.NDMA
            ent = self.dsem[eng][i]
            prev = ent[1]
            if prev > 0 and Q.known_dma.get((eng, i), 0) < prev:
                wl.append((ent[0], prev))
                Q.known_dma[(eng, i)] = prev
            ent[1] = prev + 16
            tok = ("d", eng, i, ent[1])
            sem, inc = ent[0], 16
        else:
            Q.count += 1
            tok = ("c", eng, Q.count)
            sem, inc = Q.sem, 1

        myidx = None if dma else Q.count

        def op(e, wl=wl, fn=fn, sem=sem, inc=inc, myidx=myidx, Q=Q):
            for w in wl:
                if w[0] == "c":
                    e.wait_ge(self.q[w[1]].sem, self.q[w[1]].rank[w[2]])
                else:
                    e.wait_ge(w[0], w[1])
            ins = fn(e)
            if myidx is None or myidx in Q.rank:
                ins.then_inc(sem, inc)

        Q.ops.append(op)
        self.ninstr += 1
        for v in reads:
            for r in v.regs:
                r.rd[eng if not dma else "dma_" + eng + str(tok[2])] = tok
        for v in writes:
            for r in v.regs:
                r.lw = tok
                r.rd = {}
        return tok

    def barrier(self):
        snap = {n: q.count for n, q in self.q.items()}
        dsn = {(qn, i): ent[1] for qn, lst in self.dsem.items() for i, ent in enumerate(lst) if ent[1] > 0}
        for n, Q in self.q.items():
            wl = []
            for en, cnt in snap.items():
                if en != n and cnt > 0 and Q.known.get(en, 0) < cnt:
                    wl.append(("c", en, cnt)); self.q[en].marks.add(cnt); Q.known[en] = cnt
            for (qn, i), val in dsn.items():
                if Q.known_dma.get((qn, i), 0) < val:
                    wl.append((self.dsem[qn][i][0], val)); Q.known_dma[(qn, i)] = val

            def op(e, wl=wl):
                for w in wl:
                    if w[0] == "c":
                        e.wait_ge(self.q[w[1]].sem, self.q[w[1]].rank[w[2]])
                    else:
                        e.wait_ge(w[0], w[1])
            Q.ops.append(op)

    def arena_init(self, nbytes):
        self._arena = self.es.enter_context(self.nc.sbuf_tensor("sb_arena", [128, nbytes // 4], F32))
        self._aoff = 0
        self._awords = nbytes // 4
        self._an = 0

    def arena_reset(self):
        self.barrier()
        self._aoff = 0

    def al(self, name, shape, dt):
        shape = list(shape)
        n = 1
        for s_ in shape[1:]:
            n *= s_
        words = n if dt == F32 else (n + 1) // 2
        assert self._aoff + words <= self._awords, ("arena overflow", name, self._aoff, words, self._awords)
        ap = self._arena[0:shape[0], self._aoff:self._aoff + words]
        self._aoff += words
        if dt != F32:
            ap = ap.bitcast(dt)[:, 0:n]
        if len(shape) > 2:
            names = " ".join("d%d" % i for i in range(1, len(shape)))
            kw = {"d%d" % i: shape[i] for i in range(1, len(shape))}
            ap = ap.rearrange("p (%s) -> p %s" % (names, names), **kw)
        self._an += 1
        return V(ap, (Region(name),))

    def dma(self, out, in_, eng="sp", **kw):
        kw.setdefault("allow_slow_non_contiguous", True)
        return self.emit(eng, lambda e: e.dma_start(out=out.ap, in_=in_.ap, **kw),
                         [in_], [out], dma=True)

    def mm(self, out, lhsT, rhs, start=True, stop=True, **kw):
        return self.emit("pe", lambda e: e.matmul(out.ap, lhsT.ap, rhs.ap, start=start, stop=stop, **kw),
                         [lhsT, rhs], [out])

    def tr(self, out, in_, ident):
        return self.emit("pe", lambda e: e.transpose(out.ap, in_.ap, ident.ap), [in_, ident], [out])

    def act(self, out, in_, func, bias=None, scale=None, accum_out=None, eng="act"):
        st = getattr(self, "actset", None)
        if False:
            pass
        elif func == AF.Sigmoid:
            self.actset = "SIG"
        elif func == AF.Silu:
            self.actset = "SILU"
        elif func == AF.Tanh:
            if st not in ("SIG", "SILU", "EXPO"):
                self.actset = "EXPO"
        elif func == AF.Sqrt:
            self.actset = "SQRT"
        reads = [in_]
        kw = {}
        if bias is not None:
            if isinstance(bias, V):
                reads.append(bias); kw["bias"] = bias.ap
            else:
                kw["bias"] = bias
        if scale is not None:
            if isinstance(scale, V):
                reads.append(scale); kw["scale"] = scale.ap
            else:
                kw["scale"] = scale
        writes = [out]
        if accum_out is not None:
            writes.append(accum_out); kw["accum_out"] = accum_out.ap
        return self.emit(eng, lambda e: e.activation(out.ap, in_.ap, func, **kw), reads, writes)

    def tt(self, out, a, b, op, eng="dve"):
        return self.emit(eng, lambda e: e.tensor_tensor(out.ap, a.ap, b.ap, op), [a, b], [out])

    def ts(self, out, a, s1, s2, op0, op1=None, eng="dve", accum_out=None):
        reads = [a]
        s1a = s1.ap if isinstance(s1, V) else s1
        s2a = s2.ap if isinstance(s2, V) else s2
        if isinstance(s1, V): reads.append(s1)
        if isinstance(s2, V): reads.append(s2)
        writes = [out]
        kw = {}
        if accum_out is not None:
            writes.append(accum_out); kw["accum_out"] = accum_out.ap
        if op1 is None:
            return self.emit(eng, lambda e: e.tensor_single_scalar(out.ap, a.ap, s1a, op0), reads, writes)
        return self.emit(eng, lambda e: e.tensor_scalar(out.ap, a.ap, s1a, s2a, op0, op1, **kw), reads, writes)

    def stt(self, out, a, s, b, op0, op1, eng="dve"):
        if eng == "pool":
            eng = "dve"
        reads = [a, b]
        sa = s.ap if isinstance(s, V) else s
        if isinstance(s, V): reads.append(s)
        return self.emit(eng, lambda e: e.scalar_tensor_tensor(out.ap, a.ap, sa, b.ap, op0, op1), reads, [out])

    def copy(self, out, in_, eng="dve"):
        if eng == "act":
            return self.emit("act", lambda e: e.copy(out.ap, in_.ap), [in_], [out])
        return self.emit(eng, lambda e: e.tensor_copy(out.ap, in_.ap), [in_], [out])

    def memset(self, out, val, eng="dve"):
        return self.emit(eng, lambda e: e.memset(out.ap, val), [], [out])

    def scan(self, out, d0, d1, init, op0, op1, eng="dve"):
        reads = [d0, d1]
        ia = init.ap if isinstance(init, V) else init
        if isinstance(init, V): reads.append(init)
        return self.emit(eng, lambda e: e.tensor_tensor_scan(out.ap, d0.ap, d1.ap, ia, op0, op1), reads, [out])

    def reduce(self, out, in_, op, axis=None, eng="dve"):
        axis = axis or AX.X
        return self.emit(eng, lambda e: e.tensor_reduce(out.ap, in_.ap, axis, op), [in_], [out])

    def recip(self, out, in_):
        return self.emit("dve", lambda e: e.reciprocal(out.ap, in_.ap), [in_], [out])

    def build(self):
        nc = self.nc
        for Q in self.q.values():
            Q.rank = {idx: i + 1 for i, idx in enumerate(sorted(Q.marks))}
        print("marked incs", {n: len(Q.rank) for n, Q in self.q.items()})
        finals = []
        for qn, lst in self.dsem.items():
            for s, val in lst:
                if val > 0:
                    finals.append((s, val))
        with nc.Block() as block:
            @block.tensor
            def _(e):
                for op in self.q["pe"].ops:
                    op(e)

            @block.vector
            def _(e):
                for op in self.q["dve"].ops:
                    op(e)

            @block.scalar
            def _(e):
                for op in self.q["act"].ops:
                    op(e)

            @block.gpsimd
            def _(e):
                for op in self.q["pool"].ops:
                    op(e)

            @block.sync
            def _(e):
                for op in self.q["sp"].ops:
                    op(e)
                for s, val in finals:
                    e.wait_ge(s, val)

NL = 2
RW0, HG0, SS0, ML0 = 0, 896, 1920, 2948
TPB = 256
CH = 64
NF, NB = 24, 14


def _mkcols():
    cols = {}
    cur = [0]

    def add(n, k):
        cols[n] = cur[0]
        cur[0] += k
    for n in ("g_mix", "g_x", "g_ff", "g_mem", "g_fin"):
        add(n, 8)
    add("mu", 7)
    for n in ("w0", "a0", "k_k", "k_a", "r_k", "ln_g", "ln_b", "lbl0", "lbl1", "hg_g", "ssm_g", "ml_g",
              "ssm_D", "ml_ib", "ml_fb"):
        add(n, 2)
    add("conv_w", 24)
    add("conv_b", 6)
    add("dt_b", 4)
    add("A_log", 4)
    return cols, cur[0]


PC, NPC = _mkcols()


def pack_params(inp):
    f = np.float32
    pp = np.zeros((NL, 128, NPC), f)
    for l in range(NL):
        def put(name, arr, k):
            pp[l, :, PC[name]:PC[name] + k] = np.asarray(arr, f).reshape(k, 128).T
        put("g_mix", inp["norm_mix_g"][l], 8)
        put("g_x", inp["norm_x_g"][l], 8)
        put("g_ff", inp["norm_ff_g"][l], 8)
        put("g_mem", inp["norm_mem_g"][l], 8)
        put("g_fin", inp["final_norm_g"], 8)
        put("mu", inp["rwkv_mu"][l], 7)
        put("w0", inp["rwkv_w0"][l], 2)
        put("a0", inp["rwkv_a0"][l], 2)
        put("k_k", inp["rwkv_k_k"][l], 2)
        put("k_a", inp["rwkv_k_a"][l], 2)
        put("r_k", inp["rwkv_r_k"][l], 2)
        put("ln_g", inp["rwkv_ln_g"][l], 2)
        put("ln_b", inp["rwkv_ln_b"][l], 2)
        put("lbl0", inp["hgrn_lb_logits"][0], 2)
        put("lbl1", inp["hgrn_lb_logits"][1], 2)
        put("hg_g", inp["hgrn_norm_g"][l], 2)
        put("ssm_g", inp["ssm_norm_g"][l], 2)
        put("ml_g", inp["mlstm_norm_g"][l], 2)
        put("ssm_D", np.repeat(inp["ssm_D"][l], 64), 2)
        put("ml_ib", np.repeat(inp["mlstm_i_bias"][l], 64), 2)
        put("ml_fb", np.repeat(inp["mlstm_f_bias"][l], 64), 2)
        cw = np.asarray(inp["ssm_conv_w"][l], f)
        for j in range(4):
            pp[l, :, PC["conv_w"] + j * 6:PC["conv_w"] + j * 6 + 6] = cw[j].reshape(6, 128).T
        put("conv_b", inp["ssm_conv_b"][l], 6)
        pp[l, :, PC["dt_b"]:PC["dt_b"] + 4] = np.asarray(inp["ssm_dt_bias"][l], f)[None, :]
        pp[l, :, PC["A_log"]:PC["A_log"] + 4] = np.asarray(inp["ssm_A_log"][l], f)[None, :]
    return pp


def make_consts():
    f = np.float32
    c = {}
    c["c_ident"] = np.eye(128, dtype=f)
    bd = np.zeros((128, 128), f)
    bd[:64, :64] = 1
    bd[64:, 64:] = 1
    c["c_bd"] = bd
    s = np.arange(64)[:, None]
    t = np.arange(64)[None, :]
    incl = (s <= t).astype(f)
    strict = (s < t).astype(f)
    strictT = (s > t).astype(f)
    neg = np.where(s <= t, 0.0, -30000.0).astype(f)
    c["c_mask"] = np.stack([np.tile(m[:, None, :], (1, 4, 1)).reshape(64, 256) for m in (neg,)])
    mp = lambda m: np.tile(m, (2, 2))
    c["c_rmask"] = np.concatenate([mp(strict), mp(strictT), mp(strict), mp(incl)], axis=1)
    c["c_eyeP"] = mp(np.eye(64, dtype=f))
    c["c_maskP"] = np.stack([np.tile(m, (2, 2)) for m in (incl,)])
    rs = np.ones((128, TPB), f)
    rs[:, ::CH] = 0
    c["c_reset"] = rs
    sel = np.zeros((4, 6, 128), f)
    for j in range(2):
        for p in range(128):
            sel[2 * j + p // 64, j, p] = 1
    for h in range(4):
        sel[h, 2 + h, :] = 1
    c["c_sel"] = sel.reshape(4, 768)
    r2 = np.zeros((2, 4), f)
    r2[:, 0] = (0, 1)
    r2[:, 1] = (1, 0)
    r2[:, 2] = (1, 0)
    r2[:, 3] = (0, 1)
    c["c_r2"] = r2
    c["c_eye16"] = np.tile(np.eye(16, dtype=f)[None], (128, 1, 1)).reshape(128, 256)
    return c

def rwkv_stub(X):
    P = X["P"]; dr = X["dr"]; DV = X["DV"]; F = X["F"]; bk = X["bk"]; proj = X["proj"]; ident_f = X["ident_f"]

    def rwkv(l, g, W, wo):
        tb = g.TB
        for b in range(g.nblk):
            last = (b == g.nblk - 1)
            if not (g.sample or last):
                continue
            for i in range(7):
                ps = proj(g, b, W, i * 128, 128, bk[i % 2])
                P.copy(F[i][:, 1:1 + tb], ps, eng=("act" if i % 2 else "dve"))
                if g.sample:
                    P.tr(bk[2][0:16, 0:128], F[i][:, 1:17], ident_f)
                    o_ = F[8][0:16, 0:128]
                    P.copy(o_, bk[2][0:16, 0:128])
                    P.dma(DV(dr["s_shift"][l, :, i * 128:(i + 1) * 128]), o_)
                else:
                    P.tr(bk[2][0:1, 0:128], F[i][:, tb:tb + 1], ident_f)
                    o_ = F[8][0:1, 0:128]
                    P.copy(o_, bk[2][0:1, 0:128])
                    P.dma(DV(dr["p_shift"][l:l + 1, i * 128:(i + 1) * 128]), o_)
    X["rwkv"] = rwkv


def rwkv_full(X):
    P = X["P"]; dr = X["dr"]; DV = X["DV"]; F = X["F"]; B = X["B"]; bk = X["bk"]; b4 = X["b4"]
    bO = X["bO"]; tok = X["tok"]; col = X["col"]; pp = X["pp"]; lora = X["lora"]
    ident_f = X["ident_f"]; ident_b = X["ident_b"]; bd_f = X["bd_f"]; ones_f = X["ones_f"]
    reset = X["reset"]; hp = X["hp"]; proj = X["proj"]; add_out = X["add_out"]; rms_rstd = X["rms_rstd"]
    state_io_T = X["state_io_T"]; omka = X["omka"]; rmask = X["rmask"]; eyeP = X["eyeP"]; mkP = X["mkP"]
    gp = X["gp"]; gs = X["gs"]
    flush_out = X["flush_out"]; defer_out = X["defer_out"]
    rms_rstd_g = X["rms_rstd_g"]; interleave = X["interleave"]
    b0, b1, b2, b3, _, b5, b6, b7 = bk
    import math

    def rwkv(l, g, W, wo):
        tb, C = g.TB, g.C
        S, Sb = g.S_r, g.Sb_r
        nlev = int(round(math.log2(C)))
        if g.sample:
            state_io_T(g, l, S, "st_wkv", True)
            for i in range(7):
                st = F[i][0:16, 0:128]
                P.dma(st, DV(dr["st_shift"][l, :, i * 128:(i + 1) * 128]))
                P.tr(b0[:, 0:16], st, ident_f[0:16, 0:16])
                P.copy(gs.prev_r[:, i, :], b0[:, 0:16])
        else:
            P.memset(S, 0.0)
            P.memset(Sb, 0.0)
            P.memset(gp.hist_r, 0.0)
        p4 = lambda t_: t_.with_ap(t_.ap.rearrange("p a j c -> p (a j c)"))
        for b in range(g.nblk):
            last = (b == g.nblk - 1)
            for i in range(7):
                ps = proj(g, b, W, i * 128, 128, bk[i % 2])
                PPi = F[i]
                P.copy(PPi[:, 1:1 + tb], ps, eng=("act" if i % 2 else "dve"))
                if g.sample:
                    prev = gs.prev_r[:, i, :]
                else:
                    P.copy(PPi[:, 0:1], gp.hist_r[:, i:i + 1], eng="pool")
                    P.copy(gp.hist_r[:, i:i + 1], PPi[:, tb:tb + 1], eng="pool")
                    prev = PPi[:, 0:tb]
                pm = F[7 + i][:, :tb]
                P.tt(pm, prev, PPi[:, 1:1 + tb], ALU.subtract)
                P.stt(pm, pm, col(l, "mu", i), PPi[:, 1:1 + tb], ALU.mult, ALU.add)
                if g.sample:
                    P.tr(b2[0:16, 0:128], PPi[:, 1:17], ident_f)
                    o_ = F[23][0:16, 0:128]
                    P.copy(o_, b2[0:16, 0:128])
                    P.dma(DV(dr["s_shift"][l, :, i * 128:(i + 1) * 128]), o_)
                elif last:
                    P.tr(b2[0:1, 0:128], PPi[:, tb:tb + 1], ident_f)
                    o_ = F[23][0:1, 0:128]
                    P.copy(o_, b2[0:1, 0:128])
                    P.dma(DV(dr["p_shift"][l:l + 1, i * 128:(i + 1) * 128]), o_)
            r_ = [F[7][:, :tb], F[8][:, :tb]]
            k_ = [F[9][:, :tb], F[10][:, :tb]]
            v_ = [F[11][:, :tb], F[12][:, :tb]]
            lin = F[13]
            li_b = B[0]
            P.act(li_b[0:32, :tb], lin[0:32, :tb], AF.Tanh)
            P.copy(li_b[32:64, :tb], lin[32:64, :tb], eng="act")
            P.act(li_b[64:128, :tb], lin[64:128, :tb], AF.Sigmoid)
            AT, BT, RT, KT, VT, EC, Gt, BON = [], [], [], [], [], [], [], []
            for j in range(2):
                cs_ = slice(j * 128, (j + 1) * 128)
                lw = F[0 + j][:, :tb]; a_ = F[2 + j][:, :tb]; g_ = F[4 + j][:, :tb]
                P.mm(b0[:, 0:tb], lora[0:32, l, cs_], li_b[0:32, :tb])
                P.act(lw, b0[:, 0:tb], AF.Sigmoid, bias=col(l, "w0", j))
                P.ts(lw, lw, -0.6065306597126334, None, ALU.mult)
                P.mm(b1[:, 0:tb], lora[32:64, l, cs_], li_b[32:64, :tb])
                P.act(a_, b1[:, 0:tb], AF.Sigmoid, bias=col(l, "a0", j))
                P.mm(b0[:, 0:tb], lora[64:128, l, cs_], li_b[64:128, :tb])
                P.copy(g_, b0[:, 0:tb], eng="act")
                kk = F[14 + j][:, :tb]; t1 = F[16 + j][:, :tb]; bon = F[18 + j][:, :tb]
                P.ts(kk, k_[j], col(l, "k_k", j), None, ALU.mult)
                tq = F[6][:, :tb]
                P.act(tq, kk, AF.Square)
                P.mm(b7[:, 0:tb], bd_f, tq)
                P.act(tq, b7[:, 0:tb], AF.Sqrt)
                P.ts(tq, tq, 1e-12, None, ALU.max)
                P.recip(tq, tq)
                P.tt(kk, kk, tq, ALU.mult)
                P.ts(t1, a_, col(l, "k_a", j), omka[:, l, j:j + 1], ALU.mult, ALU.add)
                P.tt(t1, k_[j], t1, ALU.mult)
                P.tt(bon, r_[j], t1, ALU.mult)
                P.ts(bon, bon, col(l, "r_k", j), None, ALU.mult)
                P.mm(b7[:, 0:tb], bd_f, bon)
                P.tt(bon, b7[:, 0:tb], v_[j], ALU.mult)
                if g.sample:
                    c_ = lw
                else:
                    c_ = F[20 + j][:, :tb]
                    P.scan(c_, reset[:, :tb], lw, 0.0, ALU.mult, ALU.add)
                ec = F[22 + j][:, :tb] if j == 0 else F[13][:, :tb]
                P.act(ec, c_, AF.Exp)
                P.tt(tq, c_, lw, ALU.subtract)
                P.act(tq, tq, AF.Exp)
                P.stt(B[1 + j][:, :tb], kk, -1.0, tq, ALU.mult, ALU.mult)
                P.act(tq, c_, AF.Exp, scale=-1.0)
                P.tt(kk, kk, a_, ALU.mult)
                P.tt(B[3 + j][:, :tb], kk, tq, ALU.mult)
                P.tt(B[7 + j][:, :tb], t1, tq, ALU.mult)
                P.tt(B[5 + j][:, :tb], r_[j], ec, ALU.mult)
                P.copy(B[9 + j][:, :tb], v_[j], eng="pool")
                AT.append(B[1 + j][:, :tb]); BT.append(B[3 + j][:, :tb]); RT.append(B[5 + j][:, :tb])
                KT.append(B[7 + j][:, :tb]); VT.append(B[9 + j][:, :tb]); EC.append(ec); Gt.append(g_); BON.append(bon)
            TT, SC, SCb = tok["TT"], tok["SC"], tok["SCb"]
            Ya, Yb = tok["Y"], tok["Y2"]
            PPa, PPb, IP, ATp, Ut = tok["PP0"], tok["PP1"], tok["IP"], tok["ATp"], tok["Ut"]
            r3 = lambda bank, off, n: bank.with_ap(bank.ap[:, off:off + n * 64].rearrange("p (j c) -> p j c", c=64))
            flush_out("m")
            for c in range(g.nch):
                cs = slice(c * C, (c + 1) * C)
                si = c if g.sample else 0
                if g.sample:
                    P.copy(Sb[:, :, 0, :], S[:, :, si, :], eng="act")
                for ti, src in enumerate((VT, KT, BT, AT)):
                    for h in range(4):
                        j = h // 2; hb = (h % 2) * 64
                        P.mm(b3[hb:hb + C, ti * 128 + j * 64:ti * 128 + (j + 1) * 64], src[j][hp(h), cs], ident_b[hp(h), hp(h)])
                P.copy(p4(TT), b3[:, 0:512], eng="act")
                Vk, Kk, Bk, Ak = TT[:, 0], TT[:, 1], TT[:, 2], TT[:, 3]
                for ti, (lx, rx) in enumerate(((BT, AT), (AT, BT), (KT, AT), (KT, RT))):
                    for h in range(4):
                        j = h // 2; hb = (h % 2) * 64
                        P.mm(b2[hb:hb + C, ti * 128 + j * 64:ti * 128 + j * 64 + C], lx[j][hp(h), cs], rx[j][hp(h), cs])
                for h in range(4):
                    j = h // 2; hb = (h % 2) * 64
                    P.mm(b0[hb:hb + C, j * 64:j * 64 + C], BT[j][hp(h), cs], RT[j][hp(h), cs])
                b2v = b2.with_ap(b2.ap.rearrange("p (a j c) -> p a j c", a=4, j=2)[:, :, :, 0:C])
                rmv = rmask.with_ap(rmask.ap.rearrange("p (a j c) -> p a j c", a=4, j=2)[:, :, :, 0:C])
                P.tt(SC[:, :, :, 0:C], b2v, rmv, ALU.mult)
                P.tt(SCb[:, :, 0:C], r3(b0, 0, 2)[:, :, 0:C], mkP(0, C), ALU.mult)
                Mm, Mt, Aak, Ark = SC[:, 0], SC[:, 1], SC[:, 2], SC[:, 3]
                for h in range(4):
                    j = h // 2; hb = (h % 2) * 64
                    P.mm(b1[hb:hb + C, j * 128:j * 128 + 64], Aak[hb:hb + C, j, 0:C], Vk[hb:hb + C, j, :])
                b1v = b1.with_ap(b1.ap[:, 0:256].rearrange("p (j c) -> p j c", j=2))
                P.copy(Ya[:, :, 0:64], b1v[:, :, 0:64], eng="act")
                P.copy(Ya[:, :, 64:128], Ak, eng="pool")
                Ycur, Ynxt = Ya, Yb
                eyv = eyeP.with_ap(eyeP.ap.rearrange("p (j c) -> p j c", j=2))
                IPs = [IP, ATp]
                Pcur = SC
                if nlev > 0:
                    P.tt(IPs[0][:, :, 0:C], Mm[:, :, 0:C], eyv[:, :, 0:C], ALU.add)
                for lev in range(nlev):
                    ipc = IPs[lev % 2]
                    for h in range(4):
                        j = h // 2; hb = (h % 2) * 64
                        P.mm(b1[hb:hb + C, j * 128:(j + 1) * 128], ipc[hb:hb + C, j, 0:C], Ycur[hb:hb + C, j, :])
                    if lev < nlev - 1:
                        for h in range(4):
                            j = h // 2; hb = (h % 2) * 64
                            P.mm(b2[hb:hb + C, j * 64:j * 64 + C], Pcur[hb:hb + C, 1, j, 0:C], Pcur[hb:hb + C, 0, j, 0:C])
                            P.mm(b2[hb:hb + C, 128 + j * 64:128 + j * 64 + C], Pcur[hb:hb + C, 0, j, 0:C], Pcur[hb:hb + C, 1, j, 0:C])
                        P.tt(IPs[(lev + 1) % 2], r3(b2, 0, 2), eyv, ALU.add)
                    P.copy(Ynxt, b1v, eng="act")
                    Ycur, Ynxt = Ynxt, Ycur
                    if lev < nlev - 2:
                        Pn = PPa if Pcur is not PPa else PPb
                        P.copy(Pn.with_ap(Pn.ap.rearrange("p a j c -> p (a j c)")), b2[:, 0:256])
                        Pcur = Pn
                for h in range(4):
                    j = h // 2; hb = (h % 2) * 64
                    P.mm(b7[hp(h), j * 64:j * 64 + C], Ycur[hb:hb + C, j, 64:128], ident_b[hb:hb + C, hb:hb + C])
                P.copy(ATp[:, :, 0:C], r3(b7, 0, 2)[:, :, 0:C], eng="act")
                for h in range(4):
                    j = h // 2; hb = (h % 2) * 64
                    P.mm(b0[hb:hb + C, 128 + j * 64:128 + (j + 1) * 64], ATp[hp(h), j, 0:C], Sb[hp(h), j, 0, :])
                P.tt(Ut, r3(b0, 128, 2), Ycur[:, :, 0:64], ALU.add)
                for h in range(4):
                    j = h // 2; hb = (h % 2) * 64
                    P.mm(bO[j][hp(h), cs], Vk[hb:hb + C, j, :], Ark[hb:hb + C, j, 0:C], start=True, stop=False)
                    P.mm(bO[j][hp(h), cs], Ut[hb:hb + C, j, :], SCb[hb:hb + C, j, 0:C], start=False, stop=False)
                    P.mm(bO[j][hp(h), cs], Sb[hp(h), j, 0, :], RT[j][hp(h), cs], start=False, stop=True)
                for h in range(4):
                    j = h // 2; hb = (h % 2) * 64
                    P.mm(b7[hp(h), 128 + j * 64:128 + (j + 1) * 64], Kk[hb:hb + C, j, :], Vk[hb:hb + C, j, :], start=True, stop=False)
                    P.mm(b7[hp(h), 128 + j * 64:128 + (j + 1) * 64], Bk[hb:hb + C, j, :], Ut[hb:hb + C, j, :], start=False, stop=True)
                ce = (c + 1) * C - 1
                for j in range(2):
                    tm = tok["tmpS"][:, j, 0:64]
                    P.tt(tm, S[:, j, si, :], b7[:, 128 + j * 64:128 + (j + 1) * 64], ALU.add)
                    P.ts(S[:, j, si, :], tm, EC[j][:, ce:ce + 1], None, ALU.mult)
                    if not g.sample:
                        P.act(Sb[:, j, 0, :], tm, AF.Copy, scale=EC[j][:, ce:ce + 1])
            mixed = [B[11][:, :tb], B[12][:, :tb]]

            def ep_r(j):
                o = F[7 + j][:, :tb]; xc = F[9 + j][:, :tb]; t_ = F[11 + j][:, :tb]
                P.copy(o, bO[j][:, :tb], eng="act")
                yield
                P.mm(b7[:, j * 256:j * 256 + tb], bd_f, o)
                P.stt(xc, b7[:, j * 256:j * 256 + tb], -1.0 / 64.0, o, ALU.mult, ALU.add)
                yield
                P.act(t_, xc, AF.Square)
                yield
                yield from rms_rstd_g(t_, [t_], 1.0 / 64.0, 64e-5, bd_f, j * 256)
                P.tt(xc, xc, t_, ALU.mult)
                yield
                P.ts(xc, xc, col(l, "ln_g", j), col(l, "ln_b", j), ALU.mult, ALU.add)
                yield
                P.tt(xc, xc, BON[j], ALU.add)
                yield
                P.tt(B[11 + j][:, :tb], xc, Gt[j], ALU.mult)
                yield
            interleave([ep_r(0), ep_r(1)])
            defer_out("m", g, b, wo, mixed)
        flush_out("m")
        state_io_T(g, l, S, "s_wkv" if g.sample else "p_wkv", False)
    X["rwkv"] = rwkv

def run_rest2(X):
    P = X["P"]; dr = X["dr"]; DV = X["DV"]; F = X["F"]; B = X["B"]; bk = X["bk"]; b4 = X["b4"]
    bO = X["bO"]; bD = X["bD"]; tok = X["tok"]; col = X["col"]; pp = X["pp"]; lora = X["lora"]
    ident_f = X["ident_f"]; ident_b = X["ident_b"]; bd_f = X["bd_f"]; ones_f = X["ones_f"]; ones_b = X["ones_b"]
    eye16 = X["eye16"]; proj = X["proj"]; add_out = X["add_out"]; rms_rstd = X["rms_rstd"]
    load_unit = X["load_unit"]; load_wo = X["load_wo"]
    groups = X["groups"]; gp = X["gp"]; gs = X["gs"]; norm_stage = X["norm_stage"]
    b0, b1, b2, b3, _, b5, b6, b7 = bk
    es = X["es"]; nc = X["nc"]

    P._aoff = X["X_common_off"]
    KT = P.al("KTm", [128, 8, 256], BF16)
    Vm = P.al("Vm", [128, 2, 1024], BF16)
    qTm = P.al("qTm", [128, 8, 256], BF16)
    mnT = qTm
    E = [P.al("Eexp%d" % i, [128, TPB], BF16) for i in range(2)]
    OT = P.al("OTatt", [128, 8, TPB], BF16)
    QTt = P.al("QTatt", [128, 2, TPB], BF16)
    kcbs = [P.al("kcb%d" % i, [128, 1024], BF16) for i in range(2)]
    vcb = [P.al("vcb%d" % i, [128, 2, 1024], BF16) for i in range(2)]
    pTm = P.al("pTm", [128, 2, 16], BF16)
    pb_s = P.al("pb_s", [16, 1024], BF16)
    os_s = P.al("os_s", [16, 1024], BF16)
    X["pT_s"] = P.al("pT_s", [128, 8, 16], BF16)
    print("arena ATT end", P._aoff * 4)

    def mem_kv(l, Wk, Wv, part):
        for rt in (range(2) if part == 0 else []):
            for kc in range(8):
                st = F[kc]
                P.dma(st[:, 0:128], DV(dr["memp"][rt * 128:(rt + 1) * 128, kc * 128:(kc + 1) * 128]))
                bank = bk[kc % 2]
                P.tr(bank[:, 0:128], st[:, 0:128], ident_f)
                P.copy(F[8 + kc][:, 0:128], bank[:, 0:128], eng=("act" if kc % 2 else "dve"))
            sqs = []
            for kc in range(8):
                P.act(F[kc][:, 0:128], F[8 + kc][:, 0:128], AF.Square)
                sqs.append(F[kc][:, 0:128])
            rs = F[16][:, 0:128]
            rms_rstd(rs, sqs, 1.0 / 1024, 1e-6, ones_f)
            for kc in range(8):
                P.stt(mnT[:, kc, rt * 128:(rt + 1) * 128], F[8 + kc][:, 0:128], col(l, "g_mem", kc), rs, ALU.mult, ALU.mult)
        for dc in (range(8) if part == 0 else []):
            bank = bk[dc % 2]
            for kc in range(8):
                P.mm(bank[:, 0:256], Wk[:, kc, dc * 128:(dc + 1) * 128], mnT[:, kc, :], start=(kc == 0), stop=(kc == 7))
            P.copy(KT[:, dc, :], bank[:, 0:256], eng=("act" if dc % 2 else "dve"))
        for which, W_, dn in (((0, Wk, "p_mk"),) if part == 0 else ((1, Wv, "p_mv"),)):
            for mt in range(2):
                for half in range(2):
                    bank = bk[half]
                    for kc in range(8):
                        P.mm(bank[:, 0:512], mnT[:, kc, mt * 128:(mt + 1) * 128], W_[:, kc, half * 512:(half + 1) * 512],
                             start=(kc == 0), stop=(kc == 7))
                    for q4 in range(4):
                        o_ = F[half * 4 + q4]
                        P.copy(o_[:, 0:128], bank[:, q4 * 128:(q4 + 1) * 128], eng=("act" if q4 % 2 else "dve"))
                        P.dma(DV(dr[dn][l, mt * 128:(mt + 1) * 128, half * 512 + q4 * 128:half * 512 + (q4 + 1) * 128]), o_[:, 0:128])
                    if which == 1:
                        P.copy(Vm[:, mt, half * 512:(half + 1) * 512], bank[:, 0:512], eng="act")

    def xattn_prompt(l, Wq, Wo):
        g = gp
        tb = g.TB
        for b in range(g.nblk):
            for h in range(4):
                for dc in range(2):
                    bank = bk[dc]
                    for kc in range(8):
                        P.mm(bank[:, 0:tb], Wq[:, kc, h * 256 + dc * 128:h * 256 + (dc + 1) * 128], g.xn(kc, b),
                             start=(kc == 0), stop=(kc == 7))
                    P.copy(QTt[:, dc, :], bank[:, 0:tb], eng=("act" if dc else "dve"))
                for mj in range(2):
                    bank = bk[2 + mj]
                    for dc in range(2):
                        P.mm(bank[:, 0:tb], KT[:, h * 2 + dc, mj * 128:(mj + 1) * 128], QTt[:, dc, :], start=(dc == 0), stop=(dc == 1))
                    P.act(E[mj], bank[:, 0:tb], AF.Exp, scale=1.0 / 16.0)
                for mj in range(2):
                    P.mm(b7[:, 0:tb], ones_b, E[mj], start=(mj == 0), stop=(mj == 1))
                rd = F[0][:, :tb]
                P.recip(rd, b7[:, 0:tb])
                for dc in range(2):
                    bank = bO[dc]
                    for mj in range(2):
                        P.mm(bank[:, 0:tb], Vm[:, mj, h * 256 + dc * 128:h * 256 + (dc + 1) * 128], E[mj], start=(mj == 0), stop=(mj == 1))
                    P.tt(OT[:, h * 2 + dc, :], bank[:, 0:tb], rd, ALU.mult)
            for mc in range(8):
                bank = bk[mc % 2]
                for kc in range(8):
                    P.mm(bank[:, 0:tb], Wo[:, kc, mc * 128:(mc + 1) * 128], OT[:, kc, :], start=(kc == 0), stop=(kc == 7))
                P.tt(g.resid(mc, b), g.resid(mc, b), bank[:, 0:tb], ALU.add)

    def xattn_sample(l, Wq, Wo):
        g = gs
        e16 = eye16.with_ap(eye16.ap.rearrange("p (b j) -> p b j", j=16))
        for dc in range(8):
            bank = bk[dc % 2]
            for kc in range(8):
                P.mm(bank[:, 0:16], Wq[:, kc, dc * 128:(dc + 1) * 128], g.xn(kc, 0), start=(kc == 0), stop=(kc == 7))
            q_ = F[dc][:, 0:16]
            P.copy(q_, bank[:, 0:16], eng=("act" if dc % 2 else "dve"))
            qv = qTm.with_ap(qTm.ap[:, dc, :].rearrange("p (b j) -> p b j", j=16))
            P.tt(qv, q_.with_ap(q_.ap.unsqueeze(1).to_broadcast([128, 16, 16])), e16, ALU.mult)
        for b_ in range(16):
            vb = vcb[b_ % 2]
            for mt in range(2):
                kcb = kcbs[mt]
                P.dma(kcb, DV(dr["ck"][l, b_, mt * 128:(mt + 1) * 128, :]), eng="pool")
                for dc in range(8):
                    P.tr(b4[:, dc * 128:(dc + 1) * 128], kcb[:, dc * 128:(dc + 1) * 128], ident_b)
                P.copy(KT[:, :, mt * 128:(mt + 1) * 128], b4.with_ap(b4.ap[:, 0:1024].rearrange("p (c m) -> p c m", c=8)),
                       eng=("act" if mt else "dve"))
            for h in range(4):
                bank = bk[h]
                for dc in range(2):
                    P.mm(bank[0:16, 0:256], qTm[:, h * 2 + dc, b_ * 16:(b_ + 1) * 16], KT[:, h * 2 + dc, :],
                         start=(b_ == 0 and dc == 0), stop=(b_ == 15 and dc == 1))
        for h in range(4):
            P.act(F[1 + h][0:16, 0:256], bk[h][0:16, 0:256], AF.Exp, scale=1.0 / 16.0)
            P.reduce(F[0][0:16, h:h + 1], F[1 + h][0:16, 0:256], ALU.add)
        P.recip(F[0][0:16, 4:8], F[0][0:16, 0:4])
        for h in range(4):
            P.ts(pb_s[:, h * 256:(h + 1) * 256], F[1 + h][0:16, 0:256], F[0][0:16, 4 + h:5 + h], None, ALU.mult)
        for b_ in range(16):
            vb = vcb[b_ % 2]
            for mt in range(2):
                P.dma(vb[:, mt, :], DV(dr["cv"][l, b_, mt * 128:(mt + 1) * 128, :]), eng="pool")
            for h in range(4):
                if b_ == 0:
                    for mt in range(2):
                        P.tr(b4[:, (h * 2 + mt) * 16:(h * 2 + mt + 1) * 16], pb_s[:, h * 256 + mt * 128:h * 256 + (mt + 1) * 128], ident_b[0:16, 0:16])
            if b_ == 0:
                P.copy(X["pT_s"], b4.with_ap(b4.ap[:, 0:128].rearrange("p (c j) -> p c j", j=16)))
            pT = X["pT_s"]
            for h in range(4):
                bank = bk[h]
                for mt in range(2):
                    pm_ = pTm[:, mt, :]
                    P.tt(pm_, pT[:, h * 2 + mt, :], e16[:, b_, :], ALU.mult, eng="dve")
                    P.mm(bank[0:16, 0:256], pm_, vb[:, mt, h * 256:(h + 1) * 256],
                         start=(b_ == 0 and mt == 0), stop=(b_ == 15 and mt == 1))
        for h in range(4):
            P.copy(os_s[:, h * 256:(h + 1) * 256], bk[h][0:16, 0:256], eng=("act" if h % 2 else "dve"))
        for kc in range(8):
            P.tr(b4[:, kc * 16:(kc + 1) * 16], os_s[:, kc * 128:(kc + 1) * 128], ident_b[0:16, 0:16])
        P.copy(OT[:, :, 0:16], b4.with_ap(b4.ap[:, 0:128].rearrange("p (c j) -> p c j", j=16)))
        for mc in range(8):
            bank = bk[mc % 2]
            for kc in range(8):
                P.mm(bank[:, 0:16], Wo[:, kc, mc * 128:(mc + 1) * 128], OT[:, kc, 0:16], start=(kc == 0), stop=(kc == 7))
            P.tt(g.resid(mc, 0), g.resid(mc, 0), bank[:, 0:16], ALU.add)

    def ffn_eighth(l, g, W):
        tb = g.TB
        for b in range(g.nblk):
            for fc in range(4):
                bank = bk[fc % 2]
                for kc in range(8):
                    P.mm(bank[:, 0:tb], W[:, kc, fc * 128:(fc + 1) * 128], g.xn(kc, b), start=(kc == 0), stop=(kc == 7))
                r_ = F[fc % 4][:, :tb]
                P.act(r_, bank[:, 0:tb], AF.Relu)
                P.tt(OT[:, fc, 0:tb], r_, r_, ALU.mult, eng=("pool" if fc % 2 else "dve"))
            for mc in range(8):
                bank = bk[2 + mc % 2]
                for kc in range(4):
                    P.mm(bank[:, 0:tb], W[:, 2 * kc + mc // 4, 512 + (mc % 4) * 128:512 + (mc % 4 + 1) * 128], OT[:, kc, 0:tb],
                         start=(kc == 0), stop=(kc == 3))
                P.tt(g.resid(mc, b), g.resid(mc, b), bank[:, 0:tb], ALU.add)

    def ffn_quarter(l, g, W1, W2):
        tb = g.TB
        for b in range(g.nblk):
            for fc in range(8):
                bank = bk[fc % 2]
                for kc in range(8):
                    P.mm(bank[:, 0:tb], W1[:, kc, fc * 128:(fc + 1) * 128], g.xn(kc, b), start=(kc == 0), stop=(kc == 7))
                r_ = F[fc % 4][:, :tb]
                P.act(r_, bank[:, 0:tb], AF.Relu)
                P.tt(OT[:, fc, 0:tb], r_, r_, ALU.mult, eng=("pool" if fc % 2 else "dve"))
            for mc in range(8):
                bank = bk[2 + mc % 2]
                for kc in range(8):
                    P.mm(bank[:, 0:tb], W2[:, kc, mc * 128:(mc + 1) * 128], OT[:, kc, 0:tb], start=(kc == 0), stop=(kc == 7))
                P.tt(g.resid(mc, b), g.resid(mc, b), bank[:, 0:tb], ALU.add)

    X["load_x"](gp, dr["xp"])
    X["load_x"](gs, dr["xs"])
    mixers = [("rwkv", RW0, 896, 0), ("hgrn", HG0, 1024, 256), ("ssm", SS0, 1028, 512), ("mlstm", ML0, 1032, 768)]
    units = X["units"]; need = X["need"]; wo_list = X["wo_list"]; need_wo = X["need_wo"]
    uidx = {}
    for l in range(NL):
        for name, c0, ncols, r0 in mixers:
            uidx[(l, name)] = len(units); units.append(("w", (dr["w_in"][l, :, c0:c0 + ncols], ncols)))
            wo_list.append((l, r0))
        for n_ in ("wk", "wv", "wq", "wo"):
            uidx[(l, n_)] = len(units); units.append(("w", (dr[n_][l], 1024)))
        for qd in range(4):
            uidx[(l, "f1", qd)] = len(units); units.append(("w", (dr["ff_w1"][l, :, qd * 1024:(qd + 1) * 1024], 1024)))
            uidx[(l, "f2", qd)] = len(units); units.append(("w", (dr["ff_w2"][l, qd * 1024:(qd + 1) * 1024, :], 1024)))
    for l in range(NL):
        X["layer_params"](l)
        for g in groups:
            norm_stage(g, l, "g_mix")
        for mi, (name, c0, ncols, r0) in enumerate(mixers):
            (W,) = need([uidx[(l, name)]])
            wo = need_wo(l * 4 + mi)
            for g in groups:
                X[name](l, g, W, wo)
        P.barrier()
        (Wk,) = need([uidx[(l, "wk")]])
        for g in groups:
            norm_stage(g, l, "g_x")
        mem_kv(l, Wk, None, 0)
        (Wv,) = need([uidx[(l, "wv")]])
        mem_kv(l, None, Wv, 1)
        Wq, Wo = need([uidx[(l, "wq")], uidx[(l, "wo")]])
        xattn_prompt(l, Wq, Wo)
        xattn_sample(l, Wq, Wo)
        for g in groups:
            norm_stage(g, l, "g_ff")
        for qd in range(4):
            W1, W2 = need([uidx[(l, "f1", qd)], uidx[(l, "f2", qd)]])
            for g in groups:
                ffn_quarter(l, g, W1, W2)
        P.barrier()
    X["store_y"](gp, dr["yp"], 0)
    X["store_y"](gs, dr["ys"], 0)

def run_rest(X):
    P = X["P"]; dr = X["dr"]; DV = X["DV"]; F = X["F"]; B = X["B"]; bk = X["bk"]; b4 = X["b4"]
    bO = X["bO"]; bD = X["bD"]; tok = X["tok"]; col = X["col"]; pp = X["pp"]; lora = X["lora"]
    ident_f = X["ident_f"]; ident_b = X["ident_b"]; bd_f = X["bd_f"]; ones_f = X["ones_f"]; ones_b = X["ones_b"]
    mk = X["mk"]; eye4v = X["eye4v"]; reset = X["reset"]; sel = X["sel"]; r2c = X["r2c"]; eye16 = X["eye16"]
    v4 = X["v4"]; hp = X["hp"]; proj = X["proj"]; add_out = X["add_out"]; rms_rstd = X["rms_rstd"]
    head_norm = X["head_norm"]; state_io_T = X["state_io_T"]; state_io_N = X["state_io_N"]
    load_unit = X["load_unit"]; load_wo = X["load_wo"]; lbt = X["lbt"]; nA = X["nA"]; omka = X["omka"]
    groups = X["groups"]; gp = X["gp"]; gs = X["gs"]; norm_stage = X["norm_stage"]
    flush_out = X["flush_out"]; defer_out = X["defer_out"]
    head_norm_g = X["head_norm_g"]; interleave = X["interleave"]
    b0, b1, b2, b3, _, b5, b6, b7 = bk
    cw = lambda l, j, i: pp[:, l, PC["conv_w"] + j * 6 + i:PC["conv_w"] + j * 6 + i + 1]

    def ssm(l, g, W, wo):
        tb, C = g.TB, g.C
        S, Sb = g.S_s, g.Sb_s
        if g.sample:
            for b_ in range(16):
                for j in range(2):
                    bank = bk[(b_ * 2 + j) % 2]
                    for hh in range(2):
                        t_ = F[(b_ * 4 + j * 2 + hh) % 8][0:64, 0:128]
                        P.dma(t_, DV(dr["st_ssm"][l, b_, 2 * j + hh]))
                        P.tr(bank[:, hh * 64:hh * 64 + 64], t_, ident_f[0:64, 0:64])
                    P.copy(S[:, b_, 2 * j:2 * j + 2, :], bank.with_ap(bank.ap[:, 0:128].rearrange("p (h c) -> p h c", h=2)),
                           eng=("act" if j else "dve"))
            for i in range(6):
                st = F[4 + i % 2][0:48, 0:128]
                P.dma(st, DV(dr["st_conv"][l, :, :, i * 128:(i + 1) * 128].rearrange("b j c -> (b j) c")))
                P.tr(b0[:, 0:48], st, ident_f[0:48, 0:48])
                P.copy(gs.hist_c[:, i, :, 0:3], b0.with_ap(b0.ap[:, 0:48].rearrange("p (b j) -> p b j", j=3)))
            P.dma(DV(dr["s_conv"][l, :, 0:2, :]), DV(dr["st_conv"][l, :, 1:3, :]))
        else:
            P.memset(S, 0.0)
            P.memset(Sb, 0.0)
            P.memset(gp.hist_c, 0.0)
        for b in range(g.nblk):
            for j in range(2):
                ps = proj(g, b, W, j * 128, 128, bk[j % 2]); P.act(F[0 + j][:, :tb], ps, AF.Silu)
            for i in range(6):
                ps = proj(g, b, W, 256 + i * 128, 128, bk[i % 2])
                if g.sample:
                    P.copy(F[2 + i][:, :tb], ps, eng=("act" if i % 2 else "dve"))
                else:
                    P.copy(F[2 + i][:, 3:3 + tb], ps, eng=("act" if i % 2 else "dve"))
                    P.copy(F[2 + i][:, 0:3], gp.hist_c[:, i, 0:3], eng="pool")
            ps = proj(g, b, W, 1024, 4, b0); P.copy(F[8][0:4, :tb], ps)
            def conv_g(i):
                acc = F[9 + i][:, :tb]
                if g.sample:
                    cur = F[2 + i][:, :tb]
                    P.ts(acc, cur, cw(l, 3, i), col(l, "conv_b", i), ALU.mult, ALU.add)
                    yield
                    for j in range(3):
                        P.stt(acc, gs.hist_c[:, i, :, j], cw(l, j, i), acc, ALU.mult, ALU.add)
                        yield
                else:
                    U = F[2 + i]
                    P.ts(acc, U[:, 3:3 + tb], cw(l, 3, i), col(l, "conv_b", i), ALU.mult, ALU.add)
                    yield
                    for j in range(3):
                        P.stt(acc, U[:, j:j + tb], cw(l, j, i), acc, ALU.mult, ALU.add)
                        yield
                    P.copy(gp.hist_c[:, i, 0:3], U[:, tb:tb + 3], eng="pool")
                if i < 2:
                    P.act(acc, acc, AF.Silu)
                else:
                    P.act(B[6 + (i - 2)][:, :tb] if i < 4 else B[8 + (i - 4)][:, :tb], acc, AF.Silu)
                yield
            interleave([conv_g(i) for i in range(6)])
            if g.sample:
                for i in range(6):
                    P.tr(b1[0:16, 0:128], F[2 + i][:, 0:16], ident_f)
                    o_ = F[15][0:16, 0:128]
                    P.copy(o_, b1[0:16, 0:128])
                    P.dma(DV(dr["s_conv"][l, :, 2, i * 128:(i + 1) * 128]), o_)
            elif b == g.nblk - 1:
                for i in range(6):
                    P.tr(b1[0:3, 0:128], F[2 + i][:, tb:tb + 3], ident_f)
                    o_ = F[15][0:3, 0:128]
                    P.copy(o_, b1[0:3, 0:128])
                    P.dma(DV(dr["p_conv"][l, :, i * 128:(i + 1) * 128]), o_)
            QT = [B[2 + h][:, :tb] for h in range(4)]

            def dt_g(h):
                dt = F[15 + h][:, :tb]; cum = F[19 + h][:, :tb]; tmp = F[2 + h][:, :tb]
                pb_ = (b7 if h < 2 else b2)[:, (h % 2) * 256:(h % 2) * 256 + tb]
                P.mm(pb_, sel[0:4, (2 + h) * 128:(3 + h) * 128], F[8][0:4, :tb])
                yield
                P.act(tmp, pb_, AF.Exp, bias=col(l, "dt_b", h))
                yield
                P.act(dt, tmp, AF.Ln, bias=ones_f[:, 0:1])
                yield
                P.ts(tmp, dt, nA[:, l, h:h + 1], None, ALU.mult)
                yield
                if g.sample:
                    P.copy(cum, tmp, eng="pool")
                else:
                    P.scan(cum, reset[:, :tb], tmp, 0.0, ALU.mult, ALU.add)
                yield
                P.act(dt, dt, AF.Ln)
                yield
                P.act(tmp, cum, AF.Exp)
                yield
                ecv = tmp.with_ap(tmp.ap.rearrange("p (c k) -> p c k", k=C)[:, :, C - 1])
                P.copy(tok["ECe"][:, h, 0:g.nch], ecv, eng="pool")
                P.tt(B[2 + h][:, :tb], B[8 + h // 2][:, :tb], tmp, ALU.mult)
                yield
                P.tt(dt[0:2, :], dt[0:2, :], cum[0:2, :], ALU.subtract)
                yield
                P.ts(dt[0:2, :], dt[0:2, :], r2c[:, 0:1], r2c[:, 1:2], ALU.mult, ALU.add)
                yield
                P.ts(cum[0:2, :], cum[0:2, :], r2c[:, 2:3], r2c[:, 3:4], ALU.mult, ALU.add)
                yield
            interleave([dt_g(h) for h in range(4)])
            for j in range(2):
                P.copy(B[0 + j][:, :tb], F[9 + j][:, :tb], eng="act")
            flush_out("m")
            for c in range(g.nch):
                cs = slice(c * C, (c + 1) * C)
                si = c if g.sample else 0
                for j in range(2):
                    P.tr(b4[0:C, j * 128:(j + 1) * 128], B[0 + j][:, cs], ident_b)
                    P.tr(b4[0:C, 256 + j * 128:256 + (j + 1) * 128], B[6 + j][:, cs], ident_b)
                P.copy(tok["Ktok"][0:C, 0:256], b4[0:C, 0:256], eng="act")
                P.copy(tok["T2"][0:C, 0:256], b4[0:C, 256:512])
                for h in range(4):
                    P.mm(b7[0:C, h * 64:h * 64 + C], F[15 + h][0:2, cs], F[19 + h][0:2, cs])
                LT = tok["LT"][0:C, :, 0:C]
                P.tt(LT, v4(b7, 0, C), mk(2, C), ALU.add)
                P.act(LT, LT, AF.Exp)
                bA = bk[2 + c % 2]
                for gg in range(2):
                    P.mm(bA[0:C, gg * 64:gg * 64 + C], B[6 + gg][:, cs], B[8 + gg][:, cs])
                for gg in range(2):
                    src = bA.with_ap(bA.ap[0:C, gg * 64:gg * 64 + C].unsqueeze(1).to_broadcast([C, 2, C]))
                    P.tt(tok["AT"][0:C, 2 * gg:2 * gg + 2, 0:C], src, tok["LT"][0:C, 2 * gg:2 * gg + 2, 0:C], ALU.mult)
                Xt = tok["Ktok"].with_ap(tok["Ktok"].ap[0:C, 0:256].rearrange("p (h c) -> p h c", h=4))
                wend = tok["LT"].with_ap(tok["LT"].ap[0:C, :, C - 1:C].to_broadcast([C, 4, 64]))
                P.tt(tok["Xh"][0:C], Xt, wend, ALU.mult, eng="pool")
                if g.sample:
                    P.copy(Sb[:, 0], S[:, si], eng="pool")
                for h in range(4):
                    j = h // 2
                    P.mm(bO[j][hp(h), cs], tok["Ktok"][0:C, h * 64:(h + 1) * 64], tok["AT"][0:C, h, 0:C], start=True, stop=False)
                    P.mm(bO[j][hp(h), cs], Sb[:, 0, h, :], QT[h][:, cs], start=False, stop=True)
                for h in range(4):
                    gg = h // 2
                    P.mm(b0[:, h * 64:(h + 1) * 64], tok["T2"][0:C, gg * 128:(gg + 1) * 128], tok["Xh"][0:C, h, :])
                ecb = tok["ECe"].with_ap(tok["ECe"].ap[:, :, c:c + 1].to_broadcast([128, 4, 64]))
                P.tt(tok["tmpS4"], S[:, si], ecb, ALU.mult)
                P.tt(S[:, si], tok["tmpS4"], b0.with_ap(b0.ap[:, 0:256].rearrange("p (h c) -> p h c", h=4)), ALU.add)
                if not g.sample:
                    P.copy(Sb[:, 0], S[:, 0], eng="act")
            mixed = [B[10][:, :tb], B[11][:, :tb]]

            def ep_s(j):
                y = F[2 + j][:, :tb]; t_ = F[4 + j][:, :tb]
                P.stt(y, F[9 + j][:, :tb], col(l, "ssm_D", j), bO[j][:, :tb], ALU.mult, ALU.add)
                yield
                P.tt(y, y, F[0 + j][:, :tb], ALU.mult)
                yield
                yield from head_norm_g(y, t_, 128.0, 1e-6, ones_f, j * 256)
                P.stt(B[10 + j][:, :tb], y, col(l, "ssm_g", j), t_, ALU.mult, ALU.mult)
                yield
            interleave([ep_s(0), ep_s(1)])
            defer_out("m", g, b, wo, mixed)
        flush_out("m")
        for si in range(g.nS):
            for j in range(2):
                bank = bk[(si * 2 + j) % 2]
                for hh in range(2):
                    P.mm(bank[hh * 64:hh * 64 + 64, 0:128], S[:, si, 2 * j + hh, :], ident_f)
                t_ = F[(si * 2 + j) % 8][:, 0:128]
                P.copy(t_, bank[:, 0:128], eng=("act" if j else "dve"))
                if g.sample:
                    P.dma(DV(dr["s_ssm"][l, si, 2 * j:2 * j + 2].rearrange("h p n -> (h p) n")), t_)
                else:
                    P.dma(DV(dr["p_ssm"][l, 2 * j:2 * j + 2].rearrange("h p n -> (h p) n")), t_)

    X["ssm"] = ssm
    run_rest2(X)

def run_all(X):
    P = X["P"]; dr = X["dr"]; DV = X["DV"]; F = X["F"]; B = X["B"]; bk = X["bk"]; b4 = X["b4"]
    bO = X["bO"]; bD = X["bD"]; tok = X["tok"]; col = X["col"]; pp = X["pp"]; lora = X["lora"]
    ident_f = X["ident_f"]; ident_b = X["ident_b"]; bd_f = X["bd_f"]; ones_f = X["ones_f"]; ones_b = X["ones_b"]
    mk = X["mk"]; eye4v = X["eye4v"]; reset = X["reset"]; sel = X["sel"]; r2c = X["r2c"]; eye16 = X["eye16"]
    v4 = X["v4"]; hp = X["hp"]; proj = X["proj"]; add_out = X["add_out"]; rms_rstd = X["rms_rstd"]
    head_norm = X["head_norm"]; lin_core = X["lin_core"]; state_io_T = X["state_io_T"]; state_io_N = X["state_io_N"]
    load_unit = X["load_unit"]; load_wo = X["load_wo"]; lbt = X["lbt"]; nA = X["nA"]; omka = X["omka"]
    groups = X["groups"]; gp = X["gp"]; gs = X["gs"]; norm_stage = X["norm_stage"]
    flush_out = X["flush_out"]; defer_out = X["defer_out"]
    head_norm_g = X["head_norm_g"]; interleave = X["interleave"]
    b0, b1, b2, b3, _, b5, b6, b7 = bk

    def load_x(g, src):
        nrt = max(1, g.T // 128)
        for rt in range(nrt):
            rows = min(128, g.T)
            for kc in range(8):
                st = F[(rt * 8 + kc) % 8]
                P.dma(st[0:rows, 0:128], DV(src[rt * 128:rt * 128 + rows, kc * 128:(kc + 1) * 128]))
                bank = bk[kc % 2]
                P.tr(bank[:, 0:rows], st[0:rows, 0:128], ident_f[0:rows, 0:rows])
                blk = (rt * 128) // g.TB
                off = (rt * 128) % g.TB
                P.copy(g.resid(kc, blk)[:, off:off + rows], bank[:, 0:rows], eng=("act" if kc % 2 else "dve"))

    def store_y(g, dst, l):
        for b in range(g.nblk):
            tb = g.TB
            sqs = []
            for kc in range(8):
                P.act(F[kc][:, :tb], g.resid(kc, b), AF.Square)
                sqs.append(F[kc][:, :tb])
            rs = F[8][:, :tb]
            rms_rstd(rs, sqs, 1.0 / 1024, 1e-6, ones_f)
            for kc in range(8):
                P.stt(F[9 + kc][:, :tb], g.resid(kc, b), col(0, "g_fin", kc), rs, ALU.mult, ALU.mult)
            for sub in range(max(1, tb // 128)):
                rows = min(128, tb)
                for kc in range(8):
                    bank = bk[kc % 2]
                    P.tr(bank[0:rows, 0:128], F[9 + kc][:, sub * 128:sub * 128 + rows], ident_f)
                    o_ = F[kc % 8][0:rows, 0:128]
                    P.copy(o_, bank[0:rows, 0:128], eng=("act" if kc % 2 else "dve"))
                    r0 = b * tb + sub * 128
                    P.dma(DV(dst[r0:r0 + rows, kc * 128:(kc + 1) * 128]), o_)

    def layer_params(l):
        for j in range(2):
            e0 = F[0][:, 0:1]; e1 = F[0][:, 1:2]; sm = F[0][:, 2:3]
            P.act(e0, col(l, "lbl0", j), AF.Exp)
            P.act(e1, col(l, "lbl1", j), AF.Exp)
            P.tt(sm, e0, e1, ALU.add)
            P.recip(sm, sm)
            lbv = lbt[:, l, j, 0:1]
            if l == 0:
                P.tt(lbv, e0, e0, ALU.subtract)
            else:
                P.tt(lbv, e1, sm, ALU.mult)
            P.ts(lbt[:, l, j, 1:2], lbv, -1.0, 1.0, ALU.mult, ALU.add)
            P.ts(lbt[:, l, j, 2:3], lbv, 1.0, -1.0, ALU.mult, ALU.add)
            P.ts(omka[:, l, j:j + 1], col(l, "k_a", j), -1.0, 1.0, ALU.mult, ALU.add)
        P.act(nA[:, l, :], pp[:, l, PC["A_log"]:PC["A_log"] + 4], AF.Exp)
        P.ts(nA[:, l, :], nA[:, l, :], -1.0, None, ALU.mult)

    def hgrn(l, g, W, wo):
        tb, C = g.TB, g.C
        S, Sb = g.S_h, g.Sb_h
        if g.sample:
            state_io_N(g, l, S, "st_hgrn", True)
        else:
            P.memset(S, 0.0)
            P.memset(Sb, 0.0)
        for b in range(g.nblk):
            for j in range(2):
                ps = proj(g, b, W, 0 + j * 128, 128, bk[j % 2]); P.act(F[0 + j][:, :tb], ps, AF.Silu)
            for j in range(2):
                ps = proj(g, b, W, 768 + j * 128, 128, bk[j % 2]); P.act(F[4 + j][:, :tb], ps, AF.Silu)
            for j in range(2):
                ps = proj(g, b, W, 256 + j * 128, 128, bk[j % 2]); P.act(F[2 + j][:, :tb], ps, AF.Sigmoid)
            for j in range(2):
                ps = proj(g, b, W, 512 + j * 128, 128, bk[j % 2]); P.act(B[0 + j][:, :tb], ps, AF.Copy)
            QT = [B[2][:, :tb], B[3][:, :tb]]; KT = [B[4][:, :tb], B[5][:, :tb]]; VT = [B[0][:, :tb], B[1][:, :tb]]
            EE = [F[10][:, :tb], F[11][:, :tb]]

            def prep_h(j):
                sig = F[2 + j][:, :tb]; fg = F[6 + j][:, :tb]
                P.ts(fg, sig, lbt[:, l, j, 1:2], lbt[:, l, j, 0:1], ALU.mult, ALU.add)
                yield
                P.act(fg, fg, AF.Ln)
                yield
                if g.sample:
                    bb = fg
                else:
                    bb = F[8 + j][:, :tb]
                    P.scan(bb, reset[:, :tb], fg, 0.0, ALU.mult, ALU.add)
                    yield
                eb = F[10 + j][:, :tb]; enb = F[12 + j][:, :tb]
                P.act(eb, bb, AF.Exp)
                yield
                P.act(enb, bb, AF.Exp, scale=-1.0)
                yield
                P.ts(sig, sig, lbt[:, l, j, 2:3], lbt[:, l, j, 1:2], ALU.mult, ALU.add)
                yield
                P.tt(B[2 + j][:, :tb], F[0 + j][:, :tb], eb, ALU.mult)
                yield
                P.tt(B[4 + j][:, :tb], sig, enb, ALU.mult, eng="pool")
                yield
            interleave([prep_h(0), prep_h(1)])
            flush_out("m")
            lin_core(g, QT, KT, VT, EE, S, Sb, 64, False)
            mixed = [B[6][:, :tb], B[7][:, :tb]]

            def ep_h(j):
                o = F[14 + j][:, :tb]; t_ = F[16 + j][:, :tb]
                P.copy(o, bO[j][:, :tb], eng="act")
                yield
                yield from head_norm_g(o, t_, 64.0, 1e-6, bd_f, j * 256)
                P.stt(o, o, col(l, "hg_g", j), t_, ALU.mult, ALU.mult)
                yield
                P.tt(B[6 + j][:, :tb], o, F[4 + j][:, :tb], ALU.mult)
                yield
            interleave([ep_h(0), ep_h(1)])
            defer_out("m", g, b, wo, mixed)
        flush_out("m")
        if g.sample:
            state_io_N(g, l, S, "s_hgrn", False)
        else:
            state_io_N(g, l, S, "p_hgrn", False)

    def mlstm(l, g, W, wo):
        tb, C = g.TB, g.C
        S, Sb = g.S_m, g.Sb_m
        if g.sample:
            for j in range(2):
                for hh in range(2):
                    P.dma(gs.m0[hh * 64:(hh + 1) * 64, j, :],
                          DV(dr["st_m"][l, :, 2 * j + hh:2 * j + hh + 1].rearrange("b h -> h b").to_broadcast([64, 16])))
            em0 = F[20][:, 0:32]
            P.act(em0, gs.m0.with_ap(gs.m0.ap.rearrange("p a b -> p (a b)")), AF.Exp)
            Sc = S.with_ap(S.ap[:, :, :, 0:64])
            state_io_T(g, l, Sc, "st_C", True)
            n0 = F[21][:, 0:32]
            for j in range(2):
                P.dma(n0[:, j * 16:(j + 1) * 16], DV(dr["st_n"][l, :, 2 * j:2 * j + 2, :].rearrange("b h k -> (h k) b")),
                      allow_slow_non_contiguous=True)
            for j in range(2):
                e_ = em0[:, j * 16:(j + 1) * 16]
                P.tt(S[:, j, :, 0:64], Sc[:, j, :, :], e_.with_ap(e_.ap.unsqueeze(2).to_broadcast([128, 16, 64])), ALU.mult)
                nn = F[22][:, 0:16]
                P.tt(nn, n0[:, j * 16:(j + 1) * 16], e_, ALU.mult)
                P.copy(S[:, j, :, 64:128], nn.with_ap(nn.ap.unsqueeze(2).to_broadcast([128, 16, 64])))
        else:
            P.memset(S, 0.0)
            P.memset(Sb, 0.0)
            P.memset(gp.carry, 0.0)
        P.memset(F[23][:, :tb], 1.0)
        P.memset(tok["VtP"], 1.0)
        for b in range(g.nblk):
            for j in range(2):
                ps = proj(g, b, W, 0 + j * 128, 128, bk[j % 2]); P.act(F[0 + j][:, :tb], ps, AF.Copy)
                ps = proj(g, b, W, 256 + j * 128, 128, bk[j % 2]); P.copy(F[2 + j][:, :tb], ps)
                ps = proj(g, b, W, 512 + j * 128, 128, bk[j % 2]); P.act(B[0 + j][:, :tb], ps, AF.Copy)
                ps = proj(g, b, W, 776 + j * 128, 128, bk[j % 2]); P.act(F[4 + j][:, :tb], ps, AF.Sigmoid)
            ps = proj(g, b, W, 768, 4, b0); P.copy(F[6][0:4, :tb], ps)
            ps = proj(g, b, W, 772, 4, b1); P.copy(F[7][0:4, :tb], ps)
            QT = [B[2][:, :tb], B[3][:, :tb]]; KT = [B[4][:, :tb], B[5][:, :tb]]; VT = [B[0][:, :tb], B[1][:, :tb]]
            EE = [F[12][:, :tb], F[13][:, :tb]]; MM = [F[14][:, :tb], F[15][:, :tb]]; ENM = [F[10][:, :tb], F[11][:, :tb]]

            def prep_m(j):
                li = F[8 + j][:, :tb]; lf = F[10 + j][:, :tb]
                o7 = slice(j * 256, j * 256 + tb)
                P.mm(b7[:, o7], sel[0:4, j * 128:(j + 1) * 128], F[6][0:4, :tb])
                P.mm(b2[:, o7], sel[0:4, j * 128:(j + 1) * 128], F[7][0:4, :tb])
                yield
                P.act(li, b7[:, o7], AF.Identity, bias=col(l, "ml_ib", j))
                yield
                P.act(lf, b2[:, o7], AF.Identity, bias=col(l, "ml_fb", j))
                yield
                P.act(lf, lf, AF.Exp, scale=-1.0)
                yield
                P.act(lf, lf, AF.Ln, bias=ones_f[:, 0:1])
                yield
                P.ts(lf, lf, -1.0, None, ALU.mult)
                yield
                m_ = F[14 + j][:, :tb]; g_ = F[16 + j][:, :tb]
                bb = F[12 + j][:, :tb]
                if g.sample:
                    P.copy(bb, lf, eng="pool")
                    yield
                    P.tt(m_, bb, gs.m0[:, j, :], ALU.add)
                    yield
                    P.tt(m_, m_, li, ALU.max)
                    yield
                else:
                    P.scan(bb, reset[:, :tb], lf, 0.0, ALU.mult, ALU.add)
                    yield
                    Bg = m_
                    P.scan(Bg, F[23][:, :tb], lf, gp.carry[:, j, 0:1], ALU.mult, ALU.add)
                    yield
                    P.copy(gp.carry[:, j, 0:1], Bg[:, tb - 1:tb])
                    ug = g_
                    P.tt(ug, li, Bg, ALU.subtract)
                    yield
                    Gg = F[18 + j][:, :tb]
                    P.scan(Gg, ug, ug, gp.carry[:, j, 1:2], ALU.max, ALU.max)
                    yield
                    P.copy(gp.carry[:, j, 1:2], Gg[:, tb - 1:tb])
                    P.tt(m_, Bg, Gg, ALU.add)
                    yield
                P.tt(g_, m_, bb, ALU.subtract)
                yield
                P.tt(li, li, bb, ALU.subtract)
                yield
                eg = F[18 + j][:, :tb]; eu = F[20 + j][:, :tb]
                P.act(eg, g_, AF.Exp, scale=-1.0)
                yield
                P.act(eu, li, AF.Exp)
                yield
                P.act(bb, bb, AF.Exp)
                yield
                P.act(lf, m_, AF.Exp, scale=-1.0)
                yield
                P.tt(B[2 + j][:, :tb], F[0 + j][:, :tb], eg, ALU.mult)
                yield
                P.stt(B[4 + j][:, :tb], F[2 + j][:, :tb], 0.125, eu, ALU.mult, ALU.mult)
                yield
            interleave([prep_m(0), prep_m(1)])
            flush_out("m")
            lin_core(g, QT, KT, VT, EE, S, Sb, 128, True)
            mixed = [B[6][:, :tb], B[7][:, :tb]]

            def ep_m(j):
                d_ = F[6 + j][:, :tb]; h_ = F[0 + j][:, :tb]; t_ = F[2 + j][:, :tb]
                P.act(d_, bD[j][:, :tb], AF.Abs)
                yield
                P.tt(d_, d_, ENM[j], ALU.max)
                yield
                P.recip(d_, d_)
                yield
                P.tt(h_, bO[j][:, :tb], d_, ALU.mult)
                yield
                yield from head_norm_g(h_, t_, 64.0, 1e-6, bd_f, j * 256)
                P.stt(h_, h_, col(l, "ml_g", j), t_, ALU.mult, ALU.mult)
                yield
                P.tt(B[6 + j][:, :tb], h_, F[4 + j][:, :tb], ALU.mult)
                yield
            interleave([ep_m(0), ep_m(1)])
            defer_out("m", g, b, wo, mixed)
            if b == g.nblk - 1:
                flush_out("m")
                Sc = S.with_ap(S.ap[:, :, :, 0:64])
                for j in range(2):
                    if g.sample:
                        e_ = ENM[j]
                        P.tt(Sc[:, j, :, :], S[:, j, :, 0:64], e_.with_ap(e_.ap.unsqueeze(2).to_broadcast([128, 16, 64])), ALU.mult)
                        nn = F[22][:, 0:16]
                        P.tt(nn, S[:, j, :, 64], e_, ALU.mult)
                        P.dma(DV(dr["s_n"][l, :, 2 * j:2 * j + 2, :].rearrange("b h k -> (h k) b")), nn, allow_slow_non_contiguous=True)
                        for hh in range(2):
                            P.dma(DV(dr["s_m"][l, :, 2 * j + hh:2 * j + hh + 1].rearrange("b h -> h b")), MM[j][hh * 64:hh * 64 + 1, :],
                                  allow_slow_non_contiguous=True)
                    else:
                        e_ = ENM[j][:, tb - 1:tb]
                        P.ts(Sc[:, j, 0, :], S[:, j, 0, 0:64], e_, None, ALU.mult)
                        nn = F[22][:, 0:1]
                        P.ts(nn, S[:, j, 0, 64:65], e_, None, ALU.mult)
                        P.dma(DV(dr["p_n"][l, 2 * j:2 * j + 2, :].rearrange("h (k o) -> (h k) o", o=1)), nn)
                        for hh in range(2):
                            P.dma(DV(dr["p_m"][l:l + 1, 2 * j + hh:2 * j + hh + 1]), MM[j][hh * 64:hh * 64 + 1, tb - 1:tb])
                state_io_T(g, l, Sc, "s_C" if g.sample else "p_C", False)

    X["hgrn"] = hgrn
    (rwkv_full if "rwkv_full" in globals() else rwkv_stub)(X)
    X["mlstm"] = mlstm
    X["load_x"] = load_x
    X["store_y"] = store_y
    X["layer_params"] = layer_params
    run_rest(X)

class Grp:
    pass


def build_program(dbg=None):
    nc = bass.Bass("TRN2", target_bir_lowering=False)
    dr = {}

    def din(n, shape):
        dr[n] = nc.dram_tensor(n, list(shape), F32, kind="ExternalInput").ap()

    def dout(n, shape):
        dr[n] = nc.dram_tensor(n, list(shape), F32, kind="ExternalOutput").ap()

    din("xp", [2048, 1024]); din("xs", [16, 1024]); din("memp", [256, 1024])
    din("st_wkv", [2, 16, 4, 64, 64]); din("st_shift", [2, 16, 896]); din("st_hgrn", [2, 16, 4, 64, 64])
    din("st_ssm", [2, 16, 4, 64, 128]); din("st_conv", [2, 16, 3, 768]); din("st_C", [2, 16, 4, 64, 64])
    din("st_n", [2, 16, 4, 64]); din("st_m", [2, 16, 4])
    din("ck", [2, 16, 256, 1024]); din("cv", [2, 16, 256, 1024])
    din("w_in", [2, 1024, 3980]); din("w_out", [2, 1024, 1024])
    for n in ("wq", "wk", "wv", "wo"):
        din(n, [2, 1024, 1024])
    din("ff_w1", [2, 1024, 4096]); din("ff_w2", [2, 4096, 1024])
    din("lora", [2, 128, 256]); din("pp", [2, 128, NPC])
    din("c_ident", [128, 128]); din("c_bd", [128, 128]); din("c_mask", [1, 64, 256]); din("c_rmask", [128, 512]); din("c_eyeP", [128, 128]); din("c_maskP", [1, 128, 128]);
    din("c_reset", [128, TPB]); din("c_sel", [4, 768]); din("c_r2", [2, 4]); din("c_eye16", [128, 256])
    dout("yp", [2048, 1024]); dout("ys", [16, 1024])
    dout("p_wkv", [2, 4, 64, 64]); dout("p_shift", [2, 896]); dout("p_hgrn", [2, 4, 64, 64]); dout("p_ssm", [2, 4, 64, 128])
    dout("p_conv", [2, 3, 768]); dout("p_C", [2, 4, 64, 64]); dout("p_n", [2, 4, 64]); dout("p_m", [2, 4])
    dout("p_mk", [2, 256, 1024]); dout("p_mv", [2, 256, 1024])
    dout("s_wkv", [2, 16, 4, 64, 64]); dout("s_shift", [2, 16, 896]); dout("s_hgrn", [2, 16, 4, 64, 64])
    dout("s_ssm", [2, 16, 4, 64, 128]); dout("s_conv", [2, 16, 3, 768]); dout("s_C", [2, 16, 4, 64, 64])
    dout("s_n", [2, 16, 4, 64]); dout("s_m", [2, 16, 4])

    es = ExitStack()
    with es:
        P = Prog(nc, es)
        DV = lambda ap: V(ap, ())

        ident_f = P.sb("ident_f", [128, 128], F32)
        ident_b = P.sb("ident_b", [128, 128], BF16)
        bd_f = P.sb("bd_f", [128, 128], F32)
        ones_f = P.sb("ones_f", [128, 128], F32)
        ones_b = P.sb("ones_b", [128, 128], BF16)
        masks = P.sb("masks", [64, 1, 256], F32)
        rmask = P.sb("rmask", [128, 512], BF16)
        eyeP = P.sb("eyeP", [128, 128], BF16)
        hist_r = P.sb("hist_r", [128, 8], F32)
        prev_r = P.sb("prev_r", [128, 7, 16], F32)
        S_rp = P.sb("S_rp", [128, 2, 1, 64], F32)
        Sb_rp = P.sb("Sb_rp", [128, 2, 1, 64], BF16)
        maskP = P.sb("maskP", [128, 1, 128], F32)
        reset = P.sb("reset", [128, TPB], F32)
        sel = P.sb("sel", [4, 768], F32)
        r2c = P.sb("r2c", [2, 4], F32)
        eye16 = P.sb("eye16", [128, 256], BF16)
        pp = P.sb("pp", [128, 2, NPC], F32)
        lora = P.sb("lora", [128, 2, 256], BF16)
        P.dma(ident_f, DV(dr["c_ident"]))
        P.dma(bd_f, DV(dr["c_bd"]))
        P.dma(masks, DV(dr["c_mask"].rearrange("m p c -> p m c")))
        P.dma(maskP, DV(dr["c_maskP"].rearrange("m p c -> p m c")))
        P.dma(rmask, DV(dr["c_rmask"]), eng="pool")
        P.dma(eyeP, DV(dr["c_eyeP"]), eng="pool")
        P.dma(reset, DV(dr["c_reset"]))
        P.dma(sel, DV(dr["c_sel"]))
        P.dma(r2c, DV(dr["c_r2"]))
        P.dma(eye16, DV(dr["c_eye16"]), eng="pool")
        P.dma(pp, DV(dr["pp"].rearrange("l p c -> p l c")))
        P.dma(lora, DV(dr["lora"].rearrange("l p c -> p l c")), eng="pool")
        epsc = P.sb("epsc", [128, 2], F32)
        P.memset(epsc[:, 0:1], 1e-6)
        P.memset(epsc[:, 1:2], 64e-5)
        P.copy(ident_b, ident_f)
        P.memset(ones_f, 1.0)
        P.memset(ones_b, 1.0)

        def mk(i, C):
            i = 0
            return masks.with_ap(masks.ap[0:C, i, :].rearrange("p (h c) -> p h c", h=4)[:, :, 0:C])
        def mkP(i, C):
            return maskP.with_ap(maskP.ap[:, i, :].rearrange("p (j c) -> p j c", j=2)[:, :, 0:C])
        eye4v = None

        bk = [P.ps("bk%d" % i, [128, 512], F32) for i in range(4)]
        b4 = P.ps("bk4", [128, 1024], BF16)
        bk += [b4] + [P.ps("bk%d" % i, [128, 512], F32) for i in range(5, 8)]
        b0, b1, b2, b3, _, b5, b6, b7 = bk
        bO = [b5, b6]
        bD = [b0, b1]

        def v4(bank, off, C):
            return bank.with_ap(bank.ap[0:C, off:off + 256].rearrange("p (h c) -> p h c", h=4)[:, :, 0:C])

        ring = [P.sb("ring%d" % i, [128, 8, 1040], BF16) for i in range(2)]
        P.arena_init(66 * 1024 - 128)
        F = [P.al("F%d" % i, [128, TPB + 4], F32) for i in range(NF)]
        B = [P.al("B%d" % i, [128, TPB], BF16) for i in range(NB)]
        X_common_off = P._aoff
        wor = [P.al("wor0", [128, 2, 1024], BF16)] * 2
        rstate = {"r": 0, "w": 0}

        units = []
        issued = set()

        def unit_tile(i):
            return ring[i % 2]

        def issue(i):
            if i in issued or i >= len(units):
                return
            issued.add(i)
            kind, a = units[i]
            t = ring[i % 2]
            if kind == "w":
                src_ap, ncols = a
                P.dma(t[:, :, 0:ncols], DV(src_ap.rearrange("(c p) n -> p c n", p=128)), eng="pool")
            elif kind == "ffn":
                l, e = a
                P.dma(t[:, :, 0:512], DV(dr["ff_w1"][l, :, e * 512:(e + 1) * 512].rearrange("(c p) n -> p c n", p=128)), eng="pool")
                for h_ in range(2):
                    dst = t.with_ap(t.ap[:, :, 512:1024].rearrange("p (c h) n -> p c h n", h=2)[:, :, h_, :])
                    P.dma(dst, DV(dr["ff_w2"][l, e * 512:(e + 1) * 512, h_ * 512:(h_ + 1) * 512].rearrange("(c p) n -> p c n", p=128)), eng="pool")
            elif kind == "wo":
                pass

        def need(idxs):
            for i in idxs:
                issue(i)
            nxt = max(idxs) + 1
            if nxt < len(units) and (nxt - 2) not in idxs:
                issue(nxt)
            return [ring[i % 2] for i in idxs]

        wo_list = []
        wo_issued = set()

        def issue_wo(i):
            if i in wo_issued or i >= len(wo_list):
                return
            wo_issued.add(i)
            l, r0 = wo_list[i]
            P.dma(wor[i % 2], DV(dr["w_out"][l, r0:r0 + 256, :].rearrange("(c p) n -> p c n", p=128)), eng="pool")

        def need_wo(i):
            issue_wo(i)
            return wor[i % 2]

        load_unit = None
        load_wo = None

        def mkgrp(name, T, TB, C):
            g = Grp()
            g.name, g.T, g.TB, g.C = name, T, TB, C
            g.nblk = T // TB
            g.nch = TB // C
            g.sample = (C == 1)
            rt = es.enter_context(nc.sbuf_tensor("sb_resid_" + name, [128, 8, T], F32))
            xt = es.enter_context(nc.sbuf_tensor("sb_xn_" + name, [128, 8, T], BF16))
            g.rreg = [[Region("r%s%d_%d" % (name, k, b)) for b in range(g.nblk)] for k in range(8)]
            g.xreg = [[Region("x%s%d_%d" % (name, k, b)) for b in range(g.nblk)] for k in range(8)]
            g.resid = lambda k, b: V(rt[:, k, b * TB:(b + 1) * TB], (g.rreg[k][b],))
            g.xn = lambda k, b: V(xt[:, k, b * TB:(b + 1) * TB], (g.xreg[k][b],))
            g.nS = 16 if g.sample else 1
            return g

        gp = mkgrp("p", 2048, TPB, CH)
        gs = mkgrp("s", 16, 16, 1)
        groups = [gp, gs]

        g = gp
        g.S_h = P.al("S_hp", [128, 2, 1, 64], F32); g.Sb_h = P.al("Sb_hp", [128, 2, 1, 64], BF16)
        g.S_m = P.al("S_mp", [128, 2, 1, 128], F32); g.Sb_m = P.al("Sb_mp", [128, 2, 1, 128], BF16)
        g.S_s = P.al("S_sp", [128, 1, 4, 64], F32); g.Sb_s = P.al("Sb_sp", [128, 1, 4, 64], BF16)
        Ssh = P.al("Ssh", [128, 4096], F32)
        SbC = P.al("SbC", [128, 2, 1, 128], BF16)
        SbC4 = P.al("SbC4", [128, 1, 4, 64], BF16)
        g = gs
        g.S_h = Ssh.with_ap(Ssh.ap[:, 0:2048].rearrange("p (a b c) -> p a b c", a=2, b=16, c=64))
        g.Sb_h = SbC.with_ap(SbC.ap[:, :, :, 0:64])
        g.S_m = Ssh.with_ap(Ssh.ap.rearrange("p (a b c) -> p a b c", a=2, b=16, c=128))
        g.Sb_m = SbC
        g.S_s = Ssh.with_ap(Ssh.ap.rearrange("p (a b c) -> p a b c", a=16, b=4, c=64))
        g.Sb_s = SbC4
        gp.hist_c = P.al("hist_c", [128, 6, 4], F32)
        gp.carry = P.al("carry_ml", [128, 2, 2], F32)
        gs.hist_c = P.al("hist_cs", [128, 6, 16, 3], F32)
        gs.m0 = P.al("m0_s", [128, 2, 16], F32)
        gp.S_r, gp.Sb_r = S_rp, Sb_rp
        gs.S_r = Ssh.with_ap(Ssh.ap[:, 0:2048].rearrange("p (a b c) -> p a b c", a=2, b=16, c=64))
        gs.Sb_r = SbC.with_ap(SbC.ap[:, :, :, 0:64])
        gp.hist_r = hist_r
        gs.prev_r = prev_r
        tok = {
               "TT": P.al("TTr", [128, 4, 2, 64], BF16), "SC": P.al("SCr", [128, 4, 2, 64], BF16),
               "SCb": P.al("SCbr", [128, 2, 64], BF16), "Y": P.al("Yr", [128, 2, 128], BF16),
               "Y2": P.al("Y2r", [128, 2, 128], BF16), "PP0": P.al("PP0r", [128, 2, 2, 64], BF16),
               "PP1": P.al("PP1r", [128, 2, 2, 64], BF16), "IP": P.al("IPr", [128, 2, 64], BF16),
               "ATp": P.al("ATpr", [128, 2, 64], BF16), "Ut": P.al("Utr", [128, 2, 64], BF16),

               "AT": P.al("ATm", [64, 4, 64], BF16),
               "LT": P.al("LTm", [64, 4, 64], F32), "Xh": P.al("Xhat", [64, 4, 64], BF16),
               "ECe": P.al("ECend", [128, 4, 16], F32), "tmpS": P.al("tmpS", [128, 2, 128], F32),
               }
        ttf = tok["TT"].ap.rearrange("p a j c -> p (a j c)")
        tok["KtP"] = tok["TT"].with_ap(ttf[:, 0:128].rearrange("p (j c) -> p j c", j=2))
        tok["VtP"] = tok["TT"].with_ap(ttf[:, 128:384].rearrange("p (j c) -> p j c", j=2))
        tok["ATP"] = tok["TT"].with_ap(ttf[:, 384:512].rearrange("p (j c) -> p j c", j=2))
        tok["tmpS4"] = tok["tmpS"].with_ap(tok["tmpS"].ap.rearrange("p a c -> p (a c)").rearrange("p (h c) -> p h c", h=4))
        tok["Ktok"] = tok["TT"].with_ap(tok["TT"].ap.rearrange("p a j c -> p (a j c)")[0:64, :])
        tok["T2"] = tok["SC"].with_ap(tok["SC"].ap.rearrange("p a j c -> p (a j c)")[0:64, :])
        print("arena MIX end", P._aoff * 4)
        P._aoff = X_common_off
        lbt = P.sb("lbt", [128, 2, 2, 3], F32)
        nA = P.sb("nA", [128, 2, 4], F32)
        omka = P.sb("omka", [128, 2, 2], F32)

        col = lambda l, name, j=0: pp[:, l, PC[name] + j:PC[name] + j + 1]

        def hp(h):
            return slice((h % 2) * 64, (h % 2) * 64 + 64)

        def load_x(g, src, nrow_tiles):
            for rtile in range(nrow_tiles):
                rows = min(128, g.T)
                xt = F[rtile % 2 * 8:rtile % 2 * 8 + 8]
                st = P.sb("xstage%d" % (rtile % 2), [128, 1024], F32) if rtile < 2 and not hasattr(g, "_st%d" % (rtile % 2)) else None
                if st is not None:
                    setattr(g, "_st%d" % (rtile % 2), st)
                st = getattr(g, "_st%d" % (rtile % 2))
                P.dma(st[0:rows, :], DV(src[rtile * 128:rtile * 128 + rows, :]))
                for kc in range(8):
                    bank = bk[kc % 2]
                    P.tr(bank[:, 0:rows], st[0:rows, kc * 128:(kc + 1) * 128], ident_f[0:rows, 0:rows])
                    blk = (rtile * 128) // g.TB
                    off = (rtile * 128) % g.TB
                    dst = g.resid(kc, blk)
                    P.copy(dst[:, off:off + rows], bank[:, 0:rows], eng=("act" if kc % 2 else "dve"))

        def rms_rstd(dst, srcs, scale, eps, lhs):
            n = srcs[0].ap.shape[-1]
            for i, s_ in enumerate(srcs):
                P.mm(b7[:, 0:n], lhs, s_, start=(i == 0), stop=(i == len(srcs) - 1))
            P.ts(dst, b7[:, 0:n], scale, eps, ALU.mult, ALU.add)
            P.act(dst, dst, AF.Sqrt)
            P.recip(dst, dst)

        def rms_rstd_g(dst, srcs, scale, eps, lhs, off=0):
            n = srcs[0].ap.shape[-1]
            for i, s_ in enumerate(srcs):
                P.mm(b7[:, off:off + n], lhs, s_, start=(i == 0), stop=(i == len(srcs) - 1))
            yield
            P.ts(dst, b7[:, off:off + n], scale, eps, ALU.mult, ALU.add)
            yield
            P.act(dst, dst, AF.Sqrt)
            yield
            P.recip(dst, dst)
            yield

        def head_norm_g(o, tmp, div, eps, lhs, off=0):
            P.act(tmp, o, AF.Square)
            yield
            yield from rms_rstd_g(tmp, [tmp], 1.0 / div, eps, lhs, off)

        def interleave(gens):
            gens = list(gens)
            while gens:
                for g_ in list(gens):
                    try:
                        next(g_)
                    except StopIteration:
                        gens.remove(g_)

        def norm_stage(g, l, gname, xn_fn=None):
            xn_fn = xn_fn or g.xn
            for b in range(g.nblk):
                tb = g.TB
                sqs = []
                for kc in range(8):
                    sq = F[kc]
                    P.act(sq[:, :tb], g.resid(kc, b), AF.Square)
                    sqs.append(sq[:, :tb])
                rs = F[8][:, :tb]
                rms_rstd(rs, sqs, 1.0 / 1024, 1e-6, ones_f)
                for kc in range(8):
                    P.stt(xn_fn(kc, b), g.resid(kc, b), col(l, gname, kc), rs, ALU.mult, ALU.mult,
                          eng=("dve" if kc % 2 else "pool"))

        def proj(g, b, W, c0, M, bank):
            tb = g.TB
            for kc in range(8):
                P.mm(bank[0:M, 0:tb], W[:, kc, c0:c0 + M], g.xn(kc, b), start=(kc == 0), stop=(kc == 7))
            return bank[0:M, 0:tb]

        def add_out(g, b, wo, mixed):
            tb = g.TB
            for mc in range(8):
                bank = bk[mc % 2]
                for kk, m_ in enumerate(mixed):
                    P.mm(bank[:, 0:tb], wo[:, kk, mc * 128:(mc + 1) * 128], m_, start=(kk == 0), stop=(kk == len(mixed) - 1))
                P.tt(g.resid(mc, b), g.resid(mc, b), bank[:, 0:tb], ALU.add)

        def head_norm(o, tmp, div, eps, lhs):
            P.tt(tmp, o, o, ALU.mult, eng="pool")
            rms_rstd(tmp, [tmp], 1.0 / div, eps, lhs)

        def state_io_T(g, l, S, dname, load):
            st = F[0:8]
            for j in range(2):
                for b_ in range(g.nS):
                    if g.sample:
                        dsl = dr[dname][l, b_, 2 * j:2 * j + 2].rearrange("h a c -> (h a) c")
                    else:
                        dsl = dr[dname][l, 2 * j:2 * j + 2].rearrange("h a c -> (h a) c")
                    t_ = st[b_ % 8][:, 0:64]
                    bank = bk[b_ % 2]
                    if load:
                        P.dma(t_, DV(dsl))
                        for hh in range(2):
                            sl = slice(hh * 64, hh * 64 + 64)
                            P.mm(bank[sl, 0:64], t_[sl, :], ident_f[sl, sl])
                        P.copy(S[:, j, b_, :], bank[:, 0:64], eng=("act" if b_ % 2 else "dve"))
                    else:
                        for hh in range(2):
                            sl = slice(hh * 64, hh * 64 + 64)
                            P.mm(bank[sl, 0:64], S[sl, j, b_, :], ident_f[sl, sl])
                        P.copy(t_, bank[:, 0:64], eng=("act" if b_ % 2 else "dve"))
                        P.dma(DV(dsl), t_)

        def state_io_N(g, l, S, dname, load):
            for j in range(2):
                if g.sample:
                    dsl = dr[dname][l, :, 2 * j:2 * j + 2].rearrange("b h k v -> (h k) b v")
                else:
                    dsl = dr[dname][l, 2 * j:2 * j + 2].rearrange("h k v -> (h k) v")
                sv = S[:, j, :, :] if g.sample else S[:, j, 0, :]
                if load:
                    P.dma(sv, DV(dsl))
                else:
                    P.dma(DV(dsl), sv)

        def lin_core(g, QT, KT, VT, eend, S, Sb, dvv, den):
            C, tb = g.C, g.TB
            Kt, Vt, AT = tok["KtP"], tok["VtP"], tok["ATP"]
            p2 = lambda bank, off: bank.with_ap(bank.ap[:, off:off + 128].rearrange("p (j c) -> p j c", j=2))
            for c in range(g.nch):
                cs = slice(c * C, (c + 1) * C)
                si = c if g.sample else 0
                for h in range(4):
                    j = h // 2
                    hb = (h % 2) * 64
                    P.mm(b3[hb:hb + C, j * 64:(j + 1) * 64], KT[j][hp(h), cs], ident_b[hp(h), hp(h)])
                    P.mm(b7[hb:hb + C, 256 + j * 64:256 + (j + 1) * 64], VT[j][hp(h), cs], ident_b[hp(h), hp(h)])
                P.copy(Kt, p2(b3, 0), eng="act")
                P.copy(Vt[:, :, 0:64], p2(b7, 256))
                if g.sample:
                    P.copy(Sb[:, :, 0, :], S[:, :, si, :], eng="act")
                for h in range(4):
                    j = h // 2
                    hb = (h % 2) * 64
                    P.mm(b2[hb:hb + C, j * 64:j * 64 + C], KT[j][hp(h), cs], QT[j][hp(h), cs])
                P.tt(AT[:, :, 0:C], p2(b2, 0)[:, :, 0:C], mkP(0, C), ALU.mult)
                for h in range(4):
                    j = h // 2
                    hb = (h % 2) * 64
                    P.mm(bO[j][hp(h), cs], Vt[hb:hb + C, j, 0:64], AT[hb:hb + C, j, 0:C], start=True, stop=False)
                    P.mm(bO[j][hp(h), cs], Sb[hp(h), j, 0, 0:64], QT[j][hp(h), cs], start=False, stop=True)
                    if den:
                        P.mm(bD[j][hp(h), cs], Vt[hb:hb + C, j, 64:128], AT[hb:hb + C, j, 0:C], start=True, stop=False)
                        P.mm(bD[j][hp(h), cs], Sb[hp(h), j, 0, 64:128], QT[j][hp(h), cs], start=False, stop=True)
                for h in range(4):
                    j = h // 2
                    hb = (h % 2) * 64
                    P.mm(b7[hp(h), j * 128:j * 128 + dvv], Kt[hb:hb + C, j, :], Vt[hb:hb + C, j, 0:dvv])
                ce = (c + 1) * C - 1
                for j in range(2):
                    tm = tok["tmpS"][:, j, 0:dvv]
                    P.tt(tm, S[:, j, si, :], b7[:, j * 128:j * 128 + dvv], ALU.add)
                    P.ts(S[:, j, si, :], tm, eend[j][:, ce:ce + 1], None, ALU.mult)
                    if not g.sample:
                        P.act(Sb[:, j, 0, :], tm, AF.Copy, scale=eend[j][:, ce:ce + 1])

        pend = {}

        def flush_out(key):
            a = pend.pop(key, None)
            if a is not None:
                add_out(*a)

        def defer_out(key, g, b, wo, mixed):
            flush_out(key)
            pend[key] = (g, b, wo, mixed)

        ctx = dict(rms_rstd_g=rms_rstd_g, head_norm_g=head_norm_g, interleave=interleave, flush_out=flush_out, defer_out=defer_out, P=P, dr=dr, DV=DV, F=F, B=B, bk=bk, b4=b4, bO=bO, bD=bD, tok=tok, col=col, pp=pp, lora=lora,
                   ident_f=ident_f, ident_b=ident_b, bd_f=bd_f, ones_f=ones_f, ones_b=ones_b, mk=mk, mkP=mkP, eye4v=eye4v, rmask=rmask, eyeP=eyeP,
                   reset=reset, sel=sel, r2c=r2c, eye16=eye16, v4=v4, hp=hp, proj=proj, add_out=add_out,
                   rms_rstd=rms_rstd, head_norm=head_norm, lin_core=lin_core, state_io_T=state_io_T,
                   state_io_N=state_io_N, load_unit=load_unit, load_wo=load_wo, units=units, need=need, wo_list=wo_list, need_wo=need_wo, epsc=epsc, lbt=lbt, nA=nA, omka=omka,
                   X_common_off=X_common_off, groups=groups, gp=gp, gs=gs, norm_stage=norm_stage, load_x=load_x, nc=nc, es=es)
        run_all(ctx)
        P.build()
    return nc

_NC_CACHE = {}


def kernel(**inp):
    f = np.float32
    inp = {k: np.asarray(v) for k, v in inp.items()}
    if "nc" not in _NC_CACHE:
        _NC_CACHE["nc"] = build_program()
    nc = _NC_CACHE["nc"]
    consts = make_consts()
    pp = pack_params(inp)
    lora = np.concatenate([inp["rwkv_w_up"], inp["rwkv_a_up"], inp["rwkv_g_up"]], axis=1).astype(f)
    shared = {"w_in": inp["w_in"], "w_out": inp["w_out"], "wq": inp["xattn_wq"], "wk": inp["xattn_wk"],
              "wv": inp["xattn_wv"], "wo": inp["xattn_wo"], "ff_w1": inp["ff_w1"], "ff_w2": inp["ff_w2"],
              "lora": lora, "pp": pp}
    shared.update(consts)
    shared = {k: np.ascontiguousarray(v, dtype=f) for k, v in shared.items()}
    in_maps = []
    for c in range(8):
        sl = slice(16 * c, 16 * c + 16)
        m = dict(shared)
        m["xp"] = np.ascontiguousarray(inp["x_prompt"][c], f)
        m["xs"] = np.ascontiguousarray(inp["x_sample"][sl, 0, :], f)
        m["memp"] = np.ascontiguousarray(inp["mem_prompt"][c], f)
        m["st_wkv"] = np.ascontiguousarray(inp["state_rwkv_wkv"][:, sl], f)
        m["st_shift"] = np.ascontiguousarray(inp["state_rwkv_shift"][:, sl], f)
        m["st_hgrn"] = np.ascontiguousarray(inp["state_hgrn"][:, sl], f)
        m["st_ssm"] = np.ascontiguousarray(inp["state_ssm"][:, sl], f)
        m["st_conv"] = np.ascontiguousarray(inp["state_ssm_conv"][:, sl], f)
        m["st_C"] = np.ascontiguousarray(inp["state_mlstm_C"][:, sl], f)
        m["st_n"] = np.ascontiguousarray(inp["state_mlstm_n"][:, sl], f)
        m["st_m"] = np.ascontiguousarray(inp["state_mlstm_m"][:, sl], f)
        m["ck"] = np.ascontiguousarray(inp["cache_mem_k"][:, sl].reshape(2, 16, 256, 1024), f)
        m["cv"] = np.ascontiguousarray(inp["cache_mem_v"][:, sl].reshape(2, 16, 256, 1024), f)
        in_maps.append(m)
    res = run_bass_kernel_spmd(nc, in_maps, core_ids=list(range(8)))
    R = res.results
    cat0 = lambda n: np.stack([np.asarray(R[c][n]) for c in range(8)], axis=0)
    y_prompt = cat0("yp").astype(f)
    y_sample = np.concatenate([np.asarray(R[c]["ys"]) for c in range(8)], axis=0)[:, None, :].astype(f)
    outs = [y_prompt, y_sample]
    for n in ("p_wkv", "p_shift", "p_hgrn", "p_ssm", "p_conv", "p_C", "p_n", "p_m"):
        outs.append(np.stack([np.asarray(R[c][n]) for c in range(8)], axis=1).astype(f))
    for n in ("p_mk", "p_mv"):
        outs.append(np.stack([np.asarray(R[c][n]) for c in range(8)], axis=1).reshape(2, 8, 256, 4, 256).astype(f))
    for n in ("s_wkv", "s_shift", "s_hgrn", "s_ssm", "s_conv", "s_C", "s_n", "s_m"):
        outs.append(np.concatenate([np.asarray(R[c][n]) for c in range(8)], axis=1).astype(f))
    return tuple(outs)
```

```python
import numpy as np
import concourse.bass as bass
import concourse.mybir as mybir
from concourse.bass_utils import run_bass_kernel_spmd
from contextlib import ExitStack

F32 = mybir.dt.float32
BF16 = mybir.dt.bfloat16
AF = mybir.ActivationFunctionType
ALU = mybir.AluOpType
AX = mybir.AxisListType


class Region:
    __slots__ = ("name", "lw", "rd", "excl")

    def __init__(self, name):
        self.name = name
        self.excl = False
        self.lw = None
        self.rd = {}


class V:
    __slots__ = ("ap", "regs")

    def __init__(self, ap, regs):
        self.ap = ap
        self.regs = regs

    def __getitem__(self, idx):
        return V(self.ap[idx], self.regs)

    def with_ap(self, ap):
        return V(ap, self.regs)


class EngQ:
    def __init__(self, name, sem):
        self.name = name
        self.sem = sem
        self.count = 0
        self.ops = []
        self.known = {}
        self.known_dma = {}
        self.marks = set()


class Prog:
    NDMA = 12

    def __init__(self, nc, es):
        self.nc = nc
        self.es = es
        self.q = {}
        for n in ("pe", "dve", "act", "pool", "sp"):
            self.q[n] = EngQ(n, es.enter_context(nc.semaphore("s_" + n)))
        self.dsem = {}
        for n in ("sp", "pool", "act"):
            self.dsem[n] = [[es.enter_context(nc.semaphore("d_%s%d" % (n, i))), 0]
                            for i in range(self.NDMA)]
        self.dnext = {"sp": 0, "pool": 0, "act": 0}
        self.nregion = 0
        self.ninstr = 0

    def sb(self, name, shape, dt):
        t = self.es.enter_context(self.nc.sbuf_tensor("sb_" + name, list(shape), dt))
        return V(t[:], (Region(name),))

    def ps(self, name, shape, dt=F32):
        t = self.es.enter_context(self.nc.psum_tensor("ps_" + name, list(shape), dt))
        r = Region(name)
        r.excl = True
        return V(t[:], (r,))

    def dram(self, ap, name):
        return V(ap, (Region(name),))

    def _deps(self, eng, reads, writes):
        deps = []
        for v in reads:
            for r in v.regs:
                if r.lw is not None:
                    deps.append(r.lw)
        for v in writes:
            for r in v.regs:
                if r.lw is not None:
                    deps.append(r.lw)
                for en, tk in r.rd.items():
                    if en != eng:
                        deps.append(tk)
        return deps

    def emit(self, eng, fn, reads, writes, dma=False):
        Q = self.q[eng]
        ex = [v for v in reads if any(r.excl for r in v.regs)]
        if ex:
            writes = list(writes) + ex
        deps = self._deps(eng, reads, writes)
        waits = {}
        for tk in deps:
            if tk[0] == "c":
                _, en, idx = tk
                if en == eng and eng == "pe":
                    continue
                if en == eng and not dma:
                    pass
                if Q.known.get(en, 0) >= idx:
                    continue
                key = ("c", en)
                if waits.get(key, 0) < idx:
                    waits[key] = idx
            else:
                _, qn, si, val = tk
                if Q.known_dma.get((qn, si), 0) >= val:
                    continue
                key = ("d", qn, si)
                if waits.get(key, 0) < val:
                    waits[key] = val
        wl = []
        for key, val in waits.items():
            if key[0] == "c":
                wl.append(("c", key[1], val))
                self.q[key[1]].marks.add(val)
                Q.known[key[1]] = val
            else:
                wl.append((self.dsem[key[1]][key[2]][0], val))
                Q.known_dma[(key[1], key[2])] = val
        if dma:
            i = self.dnext[eng]
            self.dnext[eng] = (i + 1) % self.NDMA
            ent = self.dsem[eng][i]
            prev = ent[1]
            if prev > 0 and Q.known_dma.get((eng, i), 0) < prev:
                wl.append((ent[0], prev))
                Q.known_dma[(eng, i)] = prev
            ent[1] = prev + 16
            tok = ("d", eng, i, ent[1])
            sem, inc = ent[0], 16
        else:
            Q.count += 1
            tok = ("c", eng, Q.count)
            sem, inc = Q.sem, 1

        myidx = None if dma else Q.count

        def op(e, wl=wl, fn=fn, sem=sem, inc=inc, myidx=myidx, Q=Q):
            for w in wl:
                if w[0] == "c":
                    e.wait_ge(self.q[w[1]].sem, self.q[w[1]].rank[w[2]])
                else:
                    e.wait_ge(w[0], w[1])
            ins = fn(e)
            if myidx is None or myidx in Q.rank:
                ins.then_inc(sem, inc)

        Q.ops.append(op)
        self.ninstr += 1
        for v in reads:
            for r in v.regs:
                r.rd[eng if not dma else "dma_" + eng + str(tok[2])] = tok
        for v in writes:
            for r in v.regs:
                r.lw = tok
                r.rd = {}
        return tok

    def barrier(self):
        snap = {n: q.count for n, q in self.q.items()}
        dsn = {(qn, i): ent[1] for qn, lst in self.dsem.items() for i, ent in enumerate(lst) if ent[1] > 0}
        for n, Q in self.q.items():
            wl = []
            for en, cnt in snap.items():
                if en != n and cnt > 0 and Q.known.get(en, 0) < cnt:
                    wl.append(("c", en, cnt)); self.q[en].marks.add(cnt); Q.known[en] = cnt
            for (qn, i), val in dsn.items():
                if Q.known_dma.get((qn, i), 0) < val:
                    wl.append((self.dsem[qn][i][0], val)); Q.known_dma[(qn, i)] = val

            def op(e, wl=wl):
                for w in wl:
                    if w[0] == "c":
                        e.wait_ge(self.q[w[1]].sem, self.q[w[1]].rank[w[2]])
                    else:
                        e.wait_ge(w[0], w[1])
            Q.ops.append(op)

    def arena_init(self, nbytes):
        self._arena = self.es.enter_context(self.nc.sbuf_tensor("sb_arena", [128, nbytes // 4], F32))
        self._aoff = 0
        self._awords = nbytes // 4
        self._an = 0

    def arena_reset(self):
        self.barrier()
        self._aoff = 0

    def al(self, name, shape, dt):
        shape = list(shape)
        n = 1
        for s_ in shape[1:]:
            n *= s_
        words = n if dt == F32 else (n + 1) // 2
        assert self._aoff + words <= self._awords, ("arena overflow", name, self._aoff, words, self._awords)
        ap = self._arena[0:shape[0], self._aoff:self._aoff + words]
        self._aoff += words
        if dt != F32:
            ap = ap.bitcast(dt)[:, 0:n]
        if len(shape) > 2:
            names = " ".join("d%d" % i for i in range(1, len(shape)))
            kw = {"d%d" % i: shape[i] for i in range(1, len(shape))}
            ap = ap.rearrange("p (%s) -> p %s" % (names, names), **kw)
        self._an += 1
        return V(ap, (Region(name),))

    def dma(self, out, in_, eng="sp", **kw):
        kw.setdefault("allow_slow_non_contiguous", True)
        return self.emit(eng, lambda e: e.dma_start(out=out.ap, in_=in_.ap, **kw),
                         [in_], [out], dma=True)

    def mm(self, out, lhsT, rhs, start=True, stop=True, **kw):
        return self.emit("pe", lambda e: e.matmul(out.ap, lhsT.ap, rhs.ap, start=start, stop=stop, **kw),
                         [lhsT, rhs], [out])

    def tr(self, out, in_, ident):
        return self.emit("pe", lambda e: e.transpose(out.ap, in_.ap, ident.ap), [in_, ident], [out])

    def act(self, out, in_, func, bias=None, scale=None, accum_out=None, eng="act"):
        st = getattr(self, "actset", None)
        if False:
            pass
        elif func == AF.Sigmoid:
            self.actset = "SIG"
        elif func == AF.Silu:
            self.actset = "SILU"
        elif func == AF.Tanh:
            if st not in ("SIG", "SILU", "EXPO"):
                self.actset = "EXPO"
        elif func == AF.Sqrt:
            self.actset = "SQRT"
        reads = [in_]
        kw = {}
        if bias is not None:
            if isinstance(bias, V):
                reads.append(bias); kw["bias"] = bias.ap
            else:
                kw["bias"] = bias
        if scale is not None:
            if isinstance(scale, V):
                reads.append(scale); kw["scale"] = scale.ap
            else:
                kw["scale"] = scale
        writes = [out]
        if accum_out is not None:
            writes.append(accum_out); kw["accum_out"] = accum_out.ap
        return self.emit(eng, lambda e: e.activation(out.ap, in_.ap, func, **kw), reads, writes)

    def tt(self, out, a, b, op, eng="dve"):
        return self.emit(eng, lambda e: e.tensor_tensor(out.ap, a.ap, b.ap, op), [a, b], [out])

    def ts(self, out, a, s1, s2, op0, op1=None, eng="dve", accum_out=None):
        reads = [a]
        s1a = s1.ap if isinstance(s1, V) else s1
        s2a = s2.ap if isinstance(s2, V) else s2
        if isinstance(s1, V): reads.append(s1)
        if isinstance(s2, V): reads.append(s2)
        writes = [out]
        kw = {}
        if accum_out is not None:
            writes.append(accum_out); kw["accum_out"] = accum_out.ap
        if op1 is None:
            return self.emit(eng, lambda e: e.tensor_single_scalar(out.ap, a.ap, s1a, op0), reads, writes)
        return self.emit(eng, lambda e: e.tensor_scalar(out.ap, a.ap, s1a, s2a, op0, op1, **kw), reads, writes)

    def stt(self, out, a, s, b, op0, op1, eng="dve"):
        if eng == "pool":
            eng = "dve"
        reads = [a, b]
        sa = s.ap if isinstance(s, V) else s
        if isinstance(s, V): reads.append(s)
        return self.emit(eng, lambda e: e.scalar_tensor_tensor(out.ap, a.ap, sa, b.ap, op0, op1), reads, [out])

    def copy(self, out, in_, eng="dve"):
        if eng == "act":
            return self.emit("act", lambda e: e.copy(out.ap, in_.ap), [in_], [out])
        return self.emit(eng, lambda e: e.tensor_copy(out.ap, in_.ap), [in_], [out])

    def memset(self, out, val, eng="dve"):
        return self.emit(eng, lambda e: e.memset(out.ap, val), [], [out])

    def scan(self, out, d0, d1, init, op0, op1, eng="dve"):
        reads = [d0, d1]
        ia = init.ap if isinstance(init, V) else init
        if isinstance(init, V): reads.append(init)
        return self.emit(eng, lambda e: e.tensor_tensor_scan(out.ap, d0.ap, d1.ap, ia, op0, op1), reads, [out])

    def reduce(self, out, in_, op, axis=None, eng="dve"):
        axis = axis or AX.X
        return self.emit(eng, lambda e: e.tensor_reduce(out.ap, in_.ap, axis, op), [in_], [out])

    def recip(self, out, in_):
        return self.emit("dve", lambda e: e.reciprocal(out.ap, in_.ap), [in_], [out])

    def build(self):
        nc = self.nc
        for Q in self.q.values():
            Q.rank = {idx: i + 1 for i, idx in enumerate(sorted(Q.marks))}
        print("marked incs", {n: len(Q.rank) for n, Q in self.q.items()})
        finals = []
        for qn, lst in self.dsem.items():
            for s, val in lst:
                if val > 0:
                    finals.append((s, val))
        with nc.Block() as block:
            @block.tensor
            def _(e):
                for op in self.q["pe"].ops:
                    op(e)

            @block.vector
            def _(e):
                for op in self.q["dve"].ops:
                    op(e)

            @block.scalar
            def _(e):
                for op in self.q["act"].ops:
                    op(e)

            @block.gpsimd
            def _(e):
                for op in self.q["pool"].ops:
                    op(e)

            @block.sync
            def _(e):
                for op in self.q["sp"].ops:
                    op(e)
                for s, val in finals:
                    e.wait_ge(s, val)

NL = 2
RW0, HG0, SS0, ML0 = 0, 896, 1920, 2948
TPB = 256
CH = 64
NF, NB = 24, 14


def _mkcols():
    cols = {}
    cur = [0]

    def add(n, k):
        cols[n] = cur[0]
        cur[0] += k
    for n in ("g_mix", "g_x", "g_ff", "g_mem", "g_fin"):
        add(n, 8)
    add("mu", 7)
    for n in ("w0", "a0", "k_k", "k_a", "r_k", "ln_g", "ln_b", "lbl0", "lbl1", "hg_g", "ssm_g", "ml_g",
              "ssm_D", "ml_ib", "ml_fb"):
        add(n, 2)
    add("conv_w", 24)
    add("conv_b", 6)
    add("dt_b", 4)
    add("A_log", 4)
    return cols, cur[0]


PC, NPC = _mkcols()


def pack_params(inp):
    f = np.float32
    pp = np.zeros((NL, 128, NPC), f)
    for l in range(NL):
        def put(name, arr, k):
            pp[l, :, PC[name]:PC[name] + k] = np.asarray(arr, f).reshape(k, 128).T
        put("g_mix", inp["norm_mix_g"][l], 8)
        put("g_x", inp["norm_x_g"][l], 8)
        put("g_ff", inp["norm_ff_g"][l], 8)
        put("g_mem", inp["norm_mem_g"][l], 8)
        put("g_fin", inp["final_norm_g"], 8)
        put("mu", inp["rwkv_mu"][l], 7)
        put("w0", inp["rwkv_w0"][l], 2)
        put("a0", inp["rwkv_a0"][l], 2)
        put("k_k", inp["rwkv_k_k"][l], 2)
        put("k_a", inp["rwkv_k_a"][l], 2)
        put("r_k", inp["rwkv_r_k"][l], 2)
        put("ln_g", inp["rwkv_ln_g"][l], 2)
        put("ln_b", inp["rwkv_ln_b"][l], 2)
        put("lbl0", inp["hgrn_lb_logits"][0], 2)
        put("lbl1", inp["hgrn_lb_logits"][1], 2)
        put("hg_g", inp["hgrn_norm_g"][l], 2)
        put("ssm_g", inp["ssm_norm_g"][l], 2)
        put("ml_g", inp["mlstm_norm_g"][l], 2)
        put("ssm_D", np.repeat(inp["ssm_D"][l], 64), 2)
        put("ml_ib", np.repeat(inp["mlstm_i_bias"][l], 64), 2)
        put("ml_fb", np.repeat(inp["mlstm_f_bias"][l], 64), 2)
        cw = np.asarray(inp["ssm_conv_w"][l], f)
        for j in range(4):
            pp[l, :, PC["conv_w"] + j * 6:PC["conv_w"] + j * 6 + 6] = cw[j].reshape(6, 128).T
        put("conv_b", inp["ssm_conv_b"][l], 6)
        pp[l, :, PC["dt_b"]:PC["dt_b"] + 4] = np.asarray(inp["ssm_dt_bias"][l], f)[None, :]
        pp[l, :, PC["A_log"]:PC["A_log"] + 4] = np.asarray(inp["ssm_A_log"][l], f)[None, :]
    return pp


def make_consts():
    f = np.float32
    c = {}
    c["c_ident"] = np.eye(128, dtype=f)
    bd = np.zeros((128, 128), f)
    bd[:64, :64] = 1
    bd[64:, 64:] = 1
    c["c_bd"] = bd
    s = np.arange(64)[:, None]
    t = np.arange(64)[None, :]
    incl = (s <= t).astype(f)
    strict = (s < t).astype(f)
    strictT = (s > t).astype(f)
    neg = np.where(s <= t, 0.0, -30000.0).astype(f)
    c["c_mask"] = np.stack([np.tile(m[:, None, :], (1, 4, 1)).reshape(64, 256) for m in (neg,)])
    mp = lambda m: np.tile(m, (2, 2))
    c["c_rmask"] = np.concatenate([mp(strict), mp(strictT), mp(strict), mp(incl)], axis=1)
    c["c_eyeP"] = mp(np.eye(64, dtype=f))
    c["c_maskP"] = np.stack([np.tile(m, (2, 2)) for m in (incl,)])
    rs = np.ones((128, TPB), f)
    rs[:, ::CH] = 0
    c["c_reset"] = rs
    sel = np.zeros((4, 6, 128), f)
    for j in range(2):
        for p in range(128):
            sel[2 * j + p // 64, j, p] = 1
    for h in range(4):
        sel[h, 2 + h, :] = 1
    c["c_sel"] = sel.reshape(4, 768)
    r2 = np.zeros((2, 4), f)
    r2[:, 0] = (0, 1)
    r2[:, 1] = (1, 0)
    r2[:, 2] = (1, 0)
    r2[:, 3] = (0, 1)
    c["c_r2"] = r2
    c["c_eye16"] = np.tile(np.eye(16, dtype=f)[None], (128, 1, 1)).reshape(128, 256)
    return c

def rwkv_stub(X):
    P = X["P"]; dr = X["dr"]; DV = X["DV"]; F = X["F"]; bk = X["bk"]; proj = X["proj"]; ident_f = X["ident_f"]

    def rwkv(l, g, W, wo):
        tb = g.TB
        for b in range(g.nblk):
            last = (b == g.nblk - 1)
            if not (g.sample or last):
                continue
            for i in range(7):
                ps = proj(g, b, W, i * 128, 128, bk[i % 2])
                P.copy(F[i][:, 1:1 + tb], ps, eng=("act" if i % 2 else "dve"))
                if g.sample:
                    P.tr(bk[2][0:16, 0:128], F[i][:, 1:17], ident_f)
                    o_ = F[8][0:16, 0:128]
                    P.copy(o_, bk[2][0:16, 0:128])
                    P.dma(DV(dr["s_shift"][l, :, i * 128:(i + 1) * 128]), o_)
                else:
                    P.tr(bk[2][0:1, 0:128], F[i][:, tb:tb + 1], ident_f)
                    o_ = F[8][0:1, 0:128]
                    P.copy(o_, bk[2][0:1, 0:128])
                    P.dma(DV(dr["p_shift"][l:l + 1, i * 128:(i + 1) * 128]), o_)
    X["rwkv"] = rwkv


def rwkv_full(X):
    P = X["P"]; dr = X["dr"]; DV = X["DV"]; F = X["F"]; B = X["B"]; bk = X["bk"]; b4 = X["b4"]
    bO = X["bO"]; tok = X["tok"]; col = X["col"]; pp = X["pp"]; lora = X["lora"]
    ident_f = X["ident_f"]; ident_b = X["ident_b"]; bd_f = X["bd_f"]; ones_f = X["ones_f"]
    reset = X["reset"]; hp = X["hp"]; proj = X["proj"]; add_out = X["add_out"]; rms_rstd = X["rms_rstd"]
    state_io_T = X["state_io_T"]; omka = X["omka"]; rmask = X["rmask"]; eyeP = X["eyeP"]; mkP = X["mkP"]
    gp = X["gp"]; gs = X["gs"]
    flush_out = X["flush_out"]; defer_out = X["defer_out"]
    rms_rstd_g = X["rms_rstd_g"]; interleave = X["interleave"]
    b0, b1, b2, b3, _, b5, b6, b7 = bk
    import math

    def rwkv(l, g, W, wo):
        tb, C = g.TB, g.C
        S, Sb = g.S_r, g.Sb_r
        nlev = int(round(math.log2(C)))
        if g.sample:
            state_io_T(g, l, S, "st_wkv", True)
            for i in range(7):
                st = F[i][0:16, 0:128]
                P.dma(st, DV(dr["st_shift"][l, :, i * 128:(i + 1) * 128]))
                P.tr(b0[:, 0:16], st, ident_f[0:16, 0:16])
                P.copy(gs.prev_r[:, i, :], b0[:, 0:16])
        else:
            P.memset(S, 0.0)
            P.memset(Sb, 0.0)
            P.memset(gp.hist_r, 0.0)
        p4 = lambda t_: t_.with_ap(t_.ap.rearrange("p a j c -> p (a j c)"))
        for b in range(g.nblk):
            last = (b == g.nblk - 1)
            for i in range(7):
                ps = proj(g, b, W, i * 128, 128, bk[i % 2])
                PPi = F[i]
                P.copy(PPi[:, 1:1 + tb], ps, eng=("act" if i % 2 else "dve"))
                if g.sample:
                    prev = gs.prev_r[:, i, :]
                else:
                    P.copy(PPi[:, 0:1], gp.hist_r[:, i:i + 1], eng="pool")
                    P.copy(gp.hist_r[:, i:i + 1], PPi[:, tb:tb + 1], eng="pool")
                    prev = PPi[:, 0:tb]
                pm = F[7 + i][:, :tb]
                P.tt(pm, prev, PPi[:, 1:1 + tb], ALU.subtract)
                P.stt(pm, pm, col(l, "mu", i), PPi[:, 1:1 + tb], ALU.mult, ALU.add)
                if g.sample:
                    P.tr(b2[0:16, 0:128], PPi[:, 1:17], ident_f)
                    o_ = F[23][0:16, 0:128]
                    P.copy(o_, b2[0:16, 0:128])
                    P.dma(DV(dr["s_shift"][l, :, i * 128:(i + 1) * 128]), o_)
                elif last:
                    P.tr(b2[0:1, 0:128], PPi[:, tb:tb + 1], ident_f)
                    o_ = F[23][0:1, 0:128]
                    P.copy(o_, b2[0:1, 0:128])
                    P.dma(DV(dr["p_shift"][l:l + 1, i * 128:(i + 1) * 128]), o_)
            r_ = [F[7][:, :tb], F[8][:, :tb]]
            k_ = [F[9][:, :tb], F[10][:, :tb]]
            v_ = [F[11][:, :tb], F[12][:, :tb]]
            lin = F[13]
            li_b = B[0]
            P.act(li_b[0:32, :tb], lin[0:32, :tb], AF.Tanh)
            P.copy(li_b[32:64, :tb], lin[32:64, :tb], eng="act")
            P.act(li_b[64:128, :tb], lin[64:128, :tb], AF.Sigmoid)
            AT, BT, RT, KT, VT, EC, Gt, BON = [], [], [], [], [], [], [], []
            for j in range(2):
                cs_ = slice(j * 128, (j + 1) * 128)
                lw = F[0 + j][:, :tb]; a_ = F[2 + j][:, :tb]; g_ = F[4 + j][:, :tb]
                P.mm(b0[:, 0:tb], lora[0:32, l, cs_], li_b[0:32, :tb])
                P.act(lw, b0[:, 0:tb], AF.Sigmoid, bias=col(l, "w0", j))
                P.ts(lw, lw, -0.6065306597126334, None, ALU.mult)
                P.mm(b1[:, 0:tb], lora[32:64, l, cs_], li_b[32:64, :tb])
                P.act(a_, b1[:, 0:tb], AF.Sigmoid, bias=col(l, "a0", j))
                P.mm(b0[:, 0:tb], lora[64:128, l, cs_], li_b[64:128, :tb])
                P.copy(g_, b0[:, 0:tb], eng="act")
                kk = F[14 + j][:, :tb]; t1 = F[16 + j][:, :tb]; bon = F[18 + j][:, :tb]
                P.ts(kk, k_[j], col(l, "k_k", j), None, ALU.mult)
                tq = F[6][:, :tb]
                P.act(tq, kk, AF.Square)
                P.mm(b7[:, 0:tb], bd_f, tq)
                P.act(tq, b7[:, 0:tb], AF.Sqrt)
                P.ts(tq, tq, 1e-12, None, ALU.max)
                P.recip(tq, tq)
                P.tt(kk, kk, tq, ALU.mult)
                P.ts(t1, a_, col(l, "k_a", j), omka[:, l, j:j + 1], ALU.mult, ALU.add)
                P.tt(t1, k_[j], t1, ALU.mult)
                P.tt(bon, r_[j], t1, ALU.mult)
                P.ts(bon, bon, col(l, "r_k", j), None, ALU.mult)
                P.mm(b7[:, 0:tb], bd_f, bon)
                P.tt(bon, b7[:, 0:tb], v_[j], ALU.mult)
                if g.sample:
                    c_ = lw
                else:
                    c_ = F[20 + j][:, :tb]
                    P.scan(c_, reset[:, :tb], lw, 0.0, ALU.mult, ALU.add)
                ec = F[22 + j][:, :tb] if j == 0 else F[13][:, :tb]
                P.act(ec, c_, AF.Exp)
                P.tt(tq, c_, lw, ALU.subtract)
                P.act(tq, tq, AF.Exp)
                P.stt(B[1 + j][:, :tb], kk, -1.0, tq, ALU.mult, ALU.mult)
                P.act(tq, c_, AF.Exp, scale=-1.0)
                P.tt(kk, kk, a_, ALU.mult)
                P.tt(B[3 + j][:, :tb], kk, tq, ALU.mult)
                P.tt(B[7 + j][:, :tb], t1, tq, ALU.mult)
                P.tt(B[5 + j][:, :tb], r_[j], ec, ALU.mult)
                P.copy(B[9 + j][:, :tb], v_[j], eng="pool")
                AT.append(B[1 + j][:, :tb]); BT.append(B[3 + j][:, :tb]); RT.append(B[5 + j][:, :tb])
                KT.append(B[7 + j][:, :tb]); VT.append(B[9 + j][:, :tb]); EC.append(ec); Gt.append(g_); BON.append(bon)
            TT, SC, SCb = tok["TT"], tok["SC"], tok["SCb"]
            Ya, Yb = tok["Y"], tok["Y2"]
            PPa, PPb, IP, ATp, Ut = tok["PP0"], tok["PP1"], tok["IP"], tok["ATp"], tok["Ut"]
            r3 = lambda bank, off, n: bank.with_ap(bank.ap[:, off:off + n * 64].rearrange("p (j c) -> p j c", c=64))
            flush_out("m")
            for c in range(g.nch):
                cs = slice(c * C, (c + 1) * C)
                si = c if g.sample else 0
                if g.sample:
                    P.copy(Sb[:, :, 0, :], S[:, :, si, :], eng="act")
                for ti, src in enumerate((VT, KT, BT, AT)):
                    for h in range(4):
                        j = h // 2; hb = (h % 2) * 64
                        P.mm(b3[hb:hb + C, ti * 128 + j * 64:ti * 128 + (j + 1) * 64], src[j][hp(h), cs], ident_b[hp(h), hp(h)])
                P.copy(p4(TT), b3[:, 0:512], eng="act")
                Vk, Kk, Bk, Ak = TT[:, 0], TT[:, 1], TT[:, 2], TT[:, 3]
                for ti, (lx, rx) in enumerate(((BT, AT), (AT, BT), (KT, AT), (KT, RT))):
                    for h in range(4):
                        j = h // 2; hb = (h % 2) * 64
                        P.mm(b2[hb:hb + C, ti * 128 + j * 64:ti * 128 + j * 64 + C], lx[j][hp(h), cs], rx[j][hp(h), cs])
                for h in range(4):
                    j = h // 2; hb = (h % 2) * 64
                    P.mm(b0[hb:hb + C, j * 64:j * 64 + C], BT[j][hp(h), cs], RT[j][hp(h), cs])
                b2v = b2.with_ap(b2.ap.rearrange("p (a j c) -> p a j c", a=4, j=2)[:, :, :, 0:C])
                rmv = rmask.with_ap(rmask.ap.rearrange("p (a j c) -> p a j c", a=4, j=2)[:, :, :, 0:C])
                P.tt(SC[:, :, :, 0:C], b2v, rmv, ALU.mult)
                P.tt(SCb[:, :, 0:C], r3(b0, 0, 2)[:, :, 0:C], mkP(0, C), ALU.mult)
                Mm, Mt, Aak, Ark = SC[:, 0], SC[:, 1], SC[:, 2], SC[:, 3]
                for h in range(4):
                    j = h // 2; hb = (h % 2) * 64
                    P.mm(b1[hb:hb + C, j * 128:j * 128 + 64], Aak[hb:hb + C, j, 0:C], Vk[hb:hb + C, j, :])
                b1v = b1.with_ap(b1.ap[:, 0:256].rearrange("p (j c) -> p j c", j=2))
                P.copy(Ya[:, :, 0:64], b1v[:, :, 0:64], eng="act")
                P.copy(Ya[:, :, 64:128], Ak, eng="pool")
                Ycur, Ynxt = Ya, Yb
                eyv = eyeP.with_ap(eyeP.ap.rearrange("p (j c) -> p j c", j=2))
                IPs = [IP, ATp]
                Pcur = SC
                if nlev > 0:
                    P.tt(IPs[0][:, :, 0:C], Mm[:, :, 0:C], eyv[:, :, 0:C], ALU.add)
                for lev in range(nlev):
                    ipc = IPs[lev % 2]
                    for h in range(4):
                        j = h // 2; hb = (h % 2) * 64
                        P.mm(b1[hb:hb + C, j * 128:(j + 1) * 128], ipc[hb:hb + C, j, 0:C], Ycur[hb:hb + C, j, :])
                    if lev < nlev - 1:
                        for h in range(4):
                            j = h // 2; hb = (h % 2) * 64
                            P.mm(b2[hb:hb + C, j * 64:j * 64 + C], Pcur[hb:hb + C, 1, j, 0:C], Pcur[hb:hb + C, 0, j, 0:C])
                            P.mm(b2[hb:hb + C, 128 + j * 64:128 + j * 64 + C], Pcur[hb:hb + C, 0, j, 0:C], Pcur[hb:hb + C, 1, j, 0:C])
                        P.tt(IPs[(lev + 1) % 2], r3(b2, 0, 2), eyv, ALU.add)
                    P.copy(Ynxt, b1v, eng="act")
                    Ycur, Ynxt = Ynxt, Ycur
                    if lev < nlev - 2:
                        Pn = PPa if Pcur is not PPa else PPb
                        P.copy(Pn.with_ap(Pn.ap.rearrange("p a j c -> p (a j c)")), b2[:, 0:256])
                        Pcur = Pn
                for h in range(4):
                    j = h // 2; hb = (h % 2) * 64
                    P.mm(b7[hp(h), j * 64:j * 64 + C], Ycur[hb:hb + C, j, 64:128], ident_b[hb:hb + C, hb:hb + C])
                P.copy(ATp[:, :, 0:C], r3(b7, 0, 2)[:, :, 0:C], eng="act")
                for h in range(4):
                    j = h // 2; hb = (h % 2) * 64
                    P.mm(b0[hb:hb + C, 128 + j * 64:128 + (j + 1) * 64], ATp[hp(h), j, 0:C], Sb[hp(h), j, 0, :])
                P.tt(Ut, r3(b0, 128, 2), Ycur[:, :, 0:64], ALU.add)
                for h in range(4):
                    j = h // 2; hb = (h % 2) * 64
                    P.mm(bO[j][hp(h), cs], Vk[hb:hb + C, j, :], Ark[hb:hb + C, j, 0:C], start=True, stop=False)
                    P.mm(bO[j][hp(h), cs], Ut[hb:hb + C, j, :], SCb[hb:hb + C, j, 0:C], start=False, stop=False)
                    P.mm(bO[j][hp(h), cs], Sb[hp(h), j, 0, :], RT[j][hp(h), cs], start=False, stop=True)
                for h in range(4):
                    j = h // 2; hb = (h % 2) * 64
                    P.mm(b7[hp(h), 128 + j * 64:128 + (j + 1) * 64], Kk[hb:hb + C, j, :], Vk[hb:hb + C, j, :], start=True, stop=False)
                    P.mm(b7[hp(h), 128 + j * 64:128 + (j + 1) * 64], Bk[hb:hb + C, j, :], Ut[hb:hb + C, j, :], start=False, stop=True)
                ce = (c + 1) * C - 1
                for j in range(2):
                    tm = tok["tmpS"][:, j, 0:64]
                    P.tt(tm, S[:, j, si, :], b7[:, 128 + j * 64:128 + (j + 1) * 64], ALU.add)
                    P.ts(S[:, j, si, :], tm, EC[j][:, ce:ce + 1], None, ALU.mult)
                    if not g.sample:
                        P.act(Sb[:, j, 0, :], tm, AF.Copy, scale=EC[j][:, ce:ce + 1])
            mixed = [B[11][:, :tb], B[12][:, :tb]]

            def ep_r(j):
                o = F[7 + j][:, :tb]; xc = F[9 + j][:, :tb]; t_ = F[11 + j][:, :tb]
                P.copy(o, bO[j][:, :tb], eng="act")
                yield
                P.mm(b7[:, j * 256:j * 256 + tb], bd_f, o)
                P.stt(xc, b7[:, j * 256:j * 256 + tb], -1.0 / 64.0, o, ALU.mult, ALU.add)
                yield
                P.act(t_, xc, AF.Square)
                yield
                yield from rms_rstd_g(t_, [t_], 1.0 / 64.0, 64e-5, bd_f, j * 256)
                P.tt(xc, xc, t_, ALU.mult)
                yield
                P.ts(xc, xc, col(l, "ln_g", j), col(l, "ln_b", j), ALU.mult, ALU.add)
                yield
                P.tt(xc, xc, BON[j], ALU.add)
                yield
                P.tt(B[11 + j][:, :tb], xc, Gt[j], ALU.mult)
                yield
            interleave([ep_r(0), ep_r(1)])
            defer_out("m", g, b, wo, mixed)
        flush_out("m")
        state_io_T(g, l, S, "s_wkv" if g.sample else "p_wkv", False)
    X["rwkv"] = rwkv

def run_rest2(X):
    P = X["P"]; dr = X["dr"]; DV = X["DV"]; F = X["F"]; B = X["B"]; bk = X["bk"]; b4 = X["b4"]
    bO = X["bO"]; bD = X["bD"]; tok = X["tok"]; col = X["col"]; pp = X["pp"]; lora = X["lora"]
    ident_f = X["ident_f"]; ident_b = X["ident_b"]; bd_f = X["bd_f"]; ones_f = X["ones_f"]; ones_b = X["ones_b"]
    eye16 = X["eye16"]; proj = X["proj"]; add_out = X["add_out"]; rms_rstd = X["rms_rstd"]
    load_unit = X["load_unit"]; load_wo = X["load_wo"]
    groups = X["groups"]; gp = X["gp"]; gs = X["gs"]; norm_stage = X["norm_stage"]
    b0, b1, b2, b3, _, b5, b6, b7 = bk
    es = X["es"]; nc = X["nc"]

    P._aoff = X["X_common_off"]
    KT = P.al("KTm", [128, 8, 256], BF16)
    Vm = P.al("Vm", [128, 2, 1024], BF16)
    qTm = P.al("qTm", [128, 8, 256], BF16)
    mnT = qTm
    E = [P.al("Eexp%d" % i, [128, TPB], BF16) for i in range(2)]
    OT = P.al("OTatt", [128, 8, TPB], BF16)
    QTt = P.al("QTatt", [128, 2, TPB], BF16)
    kcbs = [P.al("kcb%d" % i, [128, 1024], BF16) for i in range(2)]
    vcb = [P.al("vcb%d" % i, [128, 2, 1024], BF16) for i in range(2)]
    pTm = P.al("pTm", [128, 2, 16], BF16)
    pb_s = P.al("pb_s", [16, 1024], BF16)
    os_s = P.al("os_s", [16, 1024], BF16)
    X["pT_s"] = P.al("pT_s", [128, 8, 16], BF16)
    print("arena ATT end", P._aoff * 4)

    def mem_kv(l, Wk, Wv, part):
        for rt in (range(2) if part == 0 else []):
            for kc in range(8):
                st = F[kc]
                P.dma(st[:, 0:128], DV(dr["memp"][rt * 128:(rt + 1) * 128, kc * 128:(kc + 1) * 128]))
                bank = bk[kc % 2]
                P.tr(bank[:, 0:128], st[:, 0:128], ident_f)
                P.copy(F[8 + kc][:, 0:128], bank[:, 0:128], eng=("act" if kc % 2 else "dve"))
            sqs = []
            for kc in range(8):
                P.act(F[kc][:, 0:128], F[8 + kc][:, 0:128], AF.Square)
                sqs.append(F[kc][:, 0:128])
            rs = F[16][:, 0:128]
            rms_rstd(rs, sqs, 1.0 / 1024, 1e-6, ones_f)
            for kc in range(8):
                P.stt(mnT[:, kc, rt * 128:(rt + 1) * 128], F[8 + kc][:, 0:128], col(l, "g_mem", kc), rs, ALU.mult, ALU.mult)
        for dc in (range(8) if part == 0 else []):
            bank = bk[dc % 2]
            for kc in range(8):
                P.mm(bank[:, 0:256], Wk[:, kc, dc * 128:(dc + 1) * 128], mnT[:, kc, :], start=(kc == 0), stop=(kc == 7))
            P.copy(KT[:, dc, :], bank[:, 0:256], eng=("act" if dc % 2 else "dve"))
        for which, W_, dn in (((0, Wk, "p_mk"),) if part == 0 else ((1, Wv, "p_mv"),)):
            for mt in range(2):
                for half in range(2):
                    bank = bk[half]
                    for kc in range(8):
                        P.mm(bank[:, 0:512], mnT[:, kc, mt * 128:(mt + 1) * 128], W_[:, kc, half * 512:(half + 1) * 512],
                             start=(kc == 0), stop=(kc == 7))
                    for q4 in range(4):
                        o_ = F[half * 4 + q4]
                        P.copy(o_[:, 0:128], bank[:, q4 * 128:(q4 + 1) * 128], eng=("act" if q4 % 2 else "dve"))
                        P.dma(DV(dr[dn][l, mt * 128:(mt + 1) * 128, half * 512 + q4 * 128:half * 512 + (q4 + 1) * 128]), o_[:, 0:128])
                    if which == 1:
                        P.copy(Vm[:, mt, half * 512:(half + 1) * 512], bank[:, 0:512], eng="act")

    def xattn_prompt(l, Wq, Wo):
        g = gp
        tb = g.TB
        for b in range(g.nblk):
            for h in range(4):
                for dc in range(2):
                    bank = bk[dc]
                    for kc in range(8):
                        P.mm(bank[:, 0:tb], Wq[:, kc, h * 256 + dc * 128:h * 256 + (dc + 1) * 128], g.xn(kc, b),
                             start=(kc == 0), stop=(kc == 7))
                    P.copy(QTt[:, dc, :], bank[:, 0:tb], eng=("act" if dc else "dve"))
                for mj in range(2):
                    bank = bk[2 + mj]
                    for dc in range(2):
                        P.mm(bank[:, 0:tb], KT[:, h * 2 + dc, mj * 128:(mj + 1) * 128], QTt[:, dc, :], start=(dc == 0), stop=(dc == 1))
                    P.act(E[mj], bank[:, 0:tb], AF.Exp, scale=1.0 / 16.0)
                for mj in range(2):
                    P.mm(b7[:, 0:tb], ones_b, E[mj], start=(mj == 0), stop=(mj == 1))
                rd = F[0][:, :tb]
                P.recip(rd, b7[:, 0:tb])
                for dc in range(2):
                    bank = bO[dc]
                    for mj in range(2):
                        P.mm(bank[:, 0:tb], Vm[:, mj, h * 256 + dc * 128:h * 256 + (dc + 1) * 128], E[mj], start=(mj == 0), stop=(mj == 1))
                    P.tt(OT[:, h * 2 + dc, :], bank[:, 0:tb], rd, ALU.mult)
            for mc in range(8):
                bank = bk[mc % 2]
                for kc in range(8):
                    P.mm(bank[:, 0:tb], Wo[:, kc, mc * 128:(mc + 1) * 128], OT[:, kc, :], start=(kc == 0), stop=(kc == 7))
                P.tt(g.resid(mc, b), g.resid(mc, b), bank[:, 0:tb], ALU.add)

    def xattn_sample(l, Wq, Wo):
        g = gs
        e16 = eye16.with_ap(eye16.ap.rearrange("p (b j) -> p b j", j=16))
        for dc in range(8):
            bank = bk[dc % 2]
            for kc in range(8):
                P.mm(bank[:, 0:16], Wq[:, kc, dc * 128:(dc + 1) * 128], g.xn(kc, 0), start=(kc == 0), stop=(kc == 7))
            q_ = F[dc][:, 0:16]
            P.copy(q_, bank[:, 0:16], eng=("act" if dc % 2 else "dve"))
            qv = qTm.with_ap(qTm.ap[:, dc, :].rearrange("p (b j) -> p b j", j=16))
            P.tt(qv, q_.with_ap(q_.ap.unsqueeze(1).to_broadcast([128, 16, 16])), e16, ALU.mult)
        for b_ in range(16):
            vb = vcb[b_ % 2]
            for mt in range(2):
                kcb = kcbs[mt]
                P.dma(kcb, DV(dr["ck"][l, b_, mt * 128:(mt + 1) * 128, :]), eng="pool")
                for dc in range(8):
                    P.tr(b4[:, dc * 128:(dc + 1) * 128], kcb[:, dc * 128:(dc + 1) * 128], ident_b)
                P.copy(KT[:, :, mt * 128:(mt + 1) * 128], b4.with_ap(b4.ap[:, 0:1024].rearrange("p (c m) -> p c m", c=8)),
                       eng=("act" if mt else "dve"))
            for h in range(4):
                bank = bk[h]
                for dc in range(2):
                    P.mm(bank[0:16, 0:256], qTm[:, h * 2 + dc, b_ * 16:(b_ + 1) * 16], KT[:, h * 2 + dc, :],
                         start=(b_ == 0 and dc == 0), stop=(b_ == 15 and dc == 1))
        for h in range(4):
            P.act(F[1 + h][0:16, 0:256], bk[h][0:16, 0:256], AF.Exp, scale=1.0 / 16.0)
            P.reduce(F[0][0:16, h:h + 1], F[1 + h][0:16, 0:256], ALU.add)
        P.recip(F[0][0:16, 4:8], F[0][0:16, 0:4])
        for h in range(4):
            P.ts(pb_s[:, h * 256:(h + 1) * 256], F[1 + h][0:16, 0:256], F[0][0:16, 4 + h:5 + h], None, ALU.mult)
        for b_ in range(16):
            vb = vcb[b_ % 2]
            for mt in range(2):
                P.dma(vb[:, mt, :], DV(dr["cv"][l, b_, mt * 128:(mt + 1) * 128, :]), eng="pool")
            for h in range(4):
                if b_ == 0:
                    for mt in range(2):
                        P.tr(b4[:, (h * 2 + mt) * 16:(h * 2 + mt + 1) * 16], pb_s[:, h * 256 + mt * 128:h * 256 + (mt + 1) * 128], ident_b[0:16, 0:16])
            if b_ == 0:
                P.copy(X["pT_s"], b4.with_ap(b4.ap[:, 0:128].rearrange("p (c j) -> p c j", j=16)))
            pT = X["pT_s"]
            for h in range(4):
                bank = bk[h]
                for mt in range(2):
                    pm_ = pTm[:, mt, :]
                    P.tt(pm_, pT[:, h * 2 + mt, :], e16[:, b_, :], ALU.mult, eng="dve")
                    P.mm(bank[0:16, 0:256], pm_, vb[:, mt, h * 256:(h + 1) * 256],
                         start=(b_ == 0 and mt == 0), stop=(b_ == 15 and mt == 1))
        for h in range(4):
            P.copy(os_s[:, h * 256:(h + 1) * 256], bk[h][0:16, 0:256], eng=("act" if h % 2 else "dve"))
        for kc in range(8):
            P.tr(b4[:, kc * 16:(kc + 1) * 16], os_s[:, kc * 128:(kc + 1) * 128], ident_b[0:16, 0:16])
        P.copy(OT[:, :, 0:16], b4.with_ap(b4.ap[:, 0:128].rearrange("p (c j) -> p c j", j=16)))
        for mc in range(8):
            bank = bk[mc % 2]
            for kc in range(8):
                P.mm(bank[:, 0:16], Wo[:, kc, mc * 128:(mc + 1) * 128], OT[:, kc, 0:16], start=(kc == 0), stop=(kc == 7))
            P.tt(g.resid(mc, 0), g.resid(mc, 0), bank[:, 0:16], ALU.add)

    def ffn_eighth(l, g, W):
        tb = g.TB
        for b in range(g.nblk):
            for fc in range(4):
                bank = bk[fc % 2]
                for kc in range(8):
                    P.mm(bank[:, 0:tb], W[:, kc, fc * 128:(fc + 1) * 128], g.xn(kc, b), start=(kc == 0), stop=(kc == 7))
                r_ = F[fc % 4][:, :tb]
                P.act(r_, bank[:, 0:tb], AF.Relu)
                P.tt(OT[:, fc, 0:tb], r_, r_, ALU.mult, eng=("pool" if fc % 2 else "dve"))
            for mc in range(8):
                bank = bk[2 + mc % 2]
                for kc in range(4):
                    P.mm(bank[:, 0:tb], W[:, 2 * kc + mc // 4, 512 + (mc % 4) * 128:512 + (mc % 4 + 1) * 128], OT[:, kc, 0:tb],
                         start=(kc == 0), stop=(kc == 3))
                P.tt(g.resid(mc, b), g.resid(mc, b), bank[:, 0:tb], ALU.add)

    def ffn_quarter(l, g, W1, W2):
        tb = g.TB
        for b in range(g.nblk):
            for fc in range(8):
                bank = bk[fc % 2]
                for kc in range(8):
                    P.mm(bank[:, 0:tb], W1[:, kc, fc * 128:(fc + 1) * 128], g.xn(kc, b), start=(kc == 0), stop=(kc == 7))
                r_ = F[fc % 4][:, :tb]
                P.act(r_, bank[:, 0:tb], AF.Relu)
                if fc % 2:
                    P.act(OT[:, fc, 0:tb], r_, AF.Square)
                else:
                    P.tt(OT[:, fc, 0:tb], r_, r_, ALU.mult)
            for mc in range(8):
                bank = bk[2 + mc % 2]
                for kc in range(8):
                    P.mm(bank[:, 0:tb], W2[:, kc, mc * 128:(mc + 1) * 128], OT[:, kc, 0:tb], start=(kc == 0), stop=(kc == 7))
                P.tt(g.resid(mc, b), g.resid(mc, b), bank[:, 0:tb], ALU.add)

    X["load_x"](gp, dr["xp"])
    X["load_x"](gs, dr["xs"])
    mixers = [("rwkv", RW0, 896, 0), ("hgrn", HG0, 1024, 256), ("ssm", SS0, 1028, 512), ("mlstm", ML0, 1032, 768)]
    units = X["units"]; need = X["need"]; wo_list = X["wo_list"]; need_wo = X["need_wo"]
    uidx = {}
    for l in range(NL):
        for name, c0, ncols, r0 in mixers:
            uidx[(l, name)] = len(units); units.append(("w", (dr["w_in"][l, :, c0:c0 + ncols], ncols)))
            wo_list.append((l, r0))
        for n_ in ("wk", "wv", "wq", "wo"):
            uidx[(l, n_)] = len(units); units.append(("w", (dr[n_][l], 1024)))
        for qd in range(4):
            uidx[(l, "f1", qd)] = len(units); units.append(("w", (dr["ff_w1"][l, :, qd * 1024:(qd + 1) * 1024], 1024)))
            uidx[(l, "f2", qd)] = len(units); units.append(("w", (dr["ff_w2"][l, qd * 1024:(qd + 1) * 1024, :], 1024)))
    for l in range(NL):
        X["layer_params"](l)
        for g in groups:
            norm_stage(g, l, "g_mix")
        for mi, (name, c0, ncols, r0) in enumerate(mixers):
            (W,) = need([uidx[(l, name)]])
            wo = need_wo(l * 4 + mi)
            for g in groups:
                X[name](l, g, W, wo)
        P.barrier()
        (Wk,) = need([uidx[(l, "wk")]])
        for g in groups:
            norm_stage(g, l, "g_x")
        mem_kv(l, Wk, None, 0)
        (Wv,) = need([uidx[(l, "wv")]])
        mem_kv(l, None, Wv, 1)
        Wq, Wo = need([uidx[(l, "wq")], uidx[(l, "wo")]])
        xattn_prompt(l, Wq, Wo)
        xattn_sample(l, Wq, Wo)
        for g in groups:
            norm_stage(g, l, "g_ff")
        for qd in range(4):
            W1, W2 = need([uidx[(l, "f1", qd)], uidx[(l, "f2", qd)]])
            for g in groups:
                ffn_quarter(l, g, W1, W2)
        P.barrier()
    X["store_y"](gp, dr["yp"], 0)
    X["store_y"](gs, dr["ys"], 0)

def run_rest(X):
    P = X["P"]; dr = X["dr"]; DV = X["DV"]; F = X["F"]; B = X["B"]; bk = X["bk"]; b4 = X["b4"]
    bO = X["bO"]; bD = X["bD"]; tok = X["tok"]; col = X["col"]; pp = X["pp"]; lora = X["lora"]
    ident_f = X["ident_f"]; ident_b = X["ident_b"]; bd_f = X["bd_f"]; ones_f = X["ones_f"]; ones_b = X["ones_b"]
    mk = X["mk"]; eye4v = X["eye4v"]; reset = X["reset"]; sel = X["sel"]; r2c = X["r2c"]; eye16 = X["eye16"]
    v4 = X["v4"]; hp = X["hp"]; proj = X["proj"]; add_out = X["add_out"]; rms_rstd = X["rms_rstd"]
    head_norm = X["head_norm"]; state_io_T = X["state_io_T"]; state_io_N = X["state_io_N"]
    load_unit = X["load_unit"]; load_wo = X["load_wo"]; lbt = X["lbt"]; nA = X["nA"]; omka = X["omka"]
    groups = X["groups"]; gp = X["gp"]; gs = X["gs"]; norm_stage = X["norm_stage"]
    flush_out = X["flush_out"]; defer_out = X["defer_out"]
    head_norm_g = X["head_norm_g"]; interleave = X["interleave"]
    b0, b1, b2, b3, _, b5, b6, b7 = bk
    cw = lambda l, j, i: pp[:, l, PC["conv_w"] + j * 6 + i:PC["conv_w"] + j * 6 + i + 1]

    def ssm(l, g, W, wo):
        tb, C = g.TB, g.C
        S, Sb = g.S_s, g.Sb_s
        if g.sample:
            for b_ in range(16):
                for j in range(2):
                    bank = bk[(b_ * 2 + j) % 2]
                    for hh in range(2):
                        t_ = F[(b_ * 4 + j * 2 + hh) % 8][0:64, 0:128]
                        P.dma(t_, DV(dr["st_ssm"][l, b_, 2 * j + hh]))
                        P.tr(bank[:, hh * 64:hh * 64 + 64], t_, ident_f[0:64, 0:64])
                    P.copy(S[:, b_, 2 * j:2 * j + 2, :], bank.with_ap(bank.ap[:, 0:128].rearrange("p (h c) -> p h c", h=2)),
                           eng=("act" if j else "dve"))
            for i in range(6):
                st = F[4 + i % 2][0:48, 0:128]
                P.dma(st, DV(dr["st_conv"][l, :, :, i * 128:(i + 1) * 128].rearrange("b j c -> (b j) c")))
                P.tr(b0[:, 0:48], st, ident_f[0:48, 0:48])
                P.copy(gs.hist_c[:, i, :, 0:3], b0.with_ap(b0.ap[:, 0:48].rearrange("p (b j) -> p b j", j=3)))
            P.dma(DV(dr["s_conv"][l, :, 0:2, :]), DV(dr["st_conv"][l, :, 1:3, :]))
        else:
            P.memset(S, 0.0)
            P.memset(Sb, 0.0)
            P.memset(gp.hist_c, 0.0)
        for b in range(g.nblk):
            for j in range(2):
                ps = proj(g, b, W, j * 128, 128, bk[j % 2]); P.act(F[0 + j][:, :tb], ps, AF.Silu)
            for i in range(6):
                ps = proj(g, b, W, 256 + i * 128, 128, bk[i % 2])
                if g.sample:
                    P.copy(F[2 + i][:, :tb], ps, eng=("act" if i % 2 else "dve"))
                else:
                    P.copy(F[2 + i][:, 3:3 + tb], ps, eng=("act" if i % 2 else "dve"))
                    P.copy(F[2 + i][:, 0:3], gp.hist_c[:, i, 0:3], eng="pool")
            ps = proj(g, b, W, 1024, 4, b0); P.copy(F[8][0:4, :tb], ps)
            def conv_g(i):
                acc = F[9 + i][:, :tb]
                if g.sample:
                    cur = F[2 + i][:, :tb]
                    P.ts(acc, cur, cw(l, 3, i), col(l, "conv_b", i), ALU.mult, ALU.add)
                    yield
                    for j in range(3):
                        P.stt(acc, gs.hist_c[:, i, :, j], cw(l, j, i), acc, ALU.mult, ALU.add)
                        yield
                else:
                    U = F[2 + i]
                    P.ts(acc, U[:, 3:3 + tb], cw(l, 3, i), col(l, "conv_b", i), ALU.mult, ALU.add)
                    yield
                    for j in range(3):
                        P.stt(acc, U[:, j:j + tb], cw(l, j, i), acc, ALU.mult, ALU.add)
                        yield
                    P.copy(gp.hist_c[:, i, 0:3], U[:, tb:tb + 3], eng="pool")
                if i < 2:
                    P.act(acc, acc, AF.Silu)
                else:
                    P.act(B[6 + (i - 2)][:, :tb] if i < 4 else B[8 + (i - 4)][:, :tb], acc, AF.Silu)
                yield
            interleave([conv_g(i) for i in range(6)])
            if g.sample:
                for i in range(6):
                    P.tr(b1[0:16, 0:128], F[2 + i][:, 0:16], ident_f)
                    o_ = F[15][0:16, 0:128]
                    P.copy(o_, b1[0:16, 0:128])
                    P.dma(DV(dr["s_conv"][l, :, 2, i * 128:(i + 1) * 128]), o_)
            elif b == g.nblk - 1:
                for i in range(6):
                    P.tr(b1[0:3, 0:128], F[2 + i][:, tb:tb + 3], ident_f)
                    o_ = F[15][0:3, 0:128]
                    P.copy(o_, b1[0:3, 0:128])
                    P.dma(DV(dr["p_conv"][l, :, i * 128:(i + 1) * 128]), o_)
            QT = [B[2 + h][:, :tb] for h in range(4)]

            def dt_g(h):
                dt = F[15 + h][:, :tb]; cum = F[19 + h][:, :tb]; tmp = F[2 + h][:, :tb]
                pb_ = (b7 if h < 2 else b2)[:, (h % 2) * 256:(h % 2) * 256 + tb]
                P.mm(pb_, sel[0:4, (2 + h) * 128:(3 + h) * 128], F[8][0:4, :tb])
                yield
                P.act(tmp, pb_, AF.Exp, bias=col(l, "dt_b", h))
                yield
                P.act(dt, tmp, AF.Ln, bias=ones_f[:, 0:1])
                yield
                P.ts(tmp, dt, nA[:, l, h:h + 1], None, ALU.mult)
                yield
                if g.sample:
                    P.copy(cum, tmp, eng="pool")
                else:
                    P.scan(cum, reset[:, :tb], tmp, 0.0, ALU.mult, ALU.add)
                yield
                P.act(dt, dt, AF.Ln)
                yield
                P.act(tmp, cum, AF.Exp)
                yield
                ecv = tmp.with_ap(tmp.ap.rearrange("p (c k) -> p c k", k=C)[:, :, C - 1])
                P.copy(tok["ECe"][:, h, 0:g.nch], ecv, eng="pool")
                P.tt(B[2 + h][:, :tb], B[8 + h // 2][:, :tb], tmp, ALU.mult)
                yield
                P.tt(dt[0:2, :], dt[0:2, :], cum[0:2, :], ALU.subtract)
                yield
                P.ts(dt[0:2, :], dt[0:2, :], r2c[:, 0:1], r2c[:, 1:2], ALU.mult, ALU.add)
                yield
                P.ts(cum[0:2, :], cum[0:2, :], r2c[:, 2:3], r2c[:, 3:4], ALU.mult, ALU.add)
                yield
            interleave([dt_g(h) for h in range(4)])
            for j in range(2):
                P.copy(B[0 + j][:, :tb], F[9 + j][:, :tb], eng="act")
            flush_out("m")
            for c in range(g.nch):
                cs = slice(c * C, (c + 1) * C)
                si = c if g.sample else 0
                for j in range(2):
                    P.tr(b4[0:C, j * 128:(j + 1) * 128], B[0 + j][:, cs], ident_b)
                    P.tr(b4[0:C, 256 + j * 128:256 + (j + 1) * 128], B[6 + j][:, cs], ident_b)
                P.copy(tok["Ktok"][0:C, 0:256], b4[0:C, 0:256], eng="act")
                P.copy(tok["T2"][0:C, 0:256], b4[0:C, 256:512])
                for h in range(4):
                    P.mm(b7[0:C, h * 64:h * 64 + C], F[15 + h][0:2, cs], F[19 + h][0:2, cs])
                LT = tok["LT"][0:C, :, 0:C]
                P.tt(LT, v4(b7, 0, C), mk(2, C), ALU.add)
                P.act(LT, LT, AF.Exp)
                bA = bk[2 + c % 2]
                for gg in range(2):
                    P.mm(bA[0:C, gg * 64:gg * 64 + C], B[6 + gg][:, cs], B[8 + gg][:, cs])
                for gg in range(2):
                    src = bA.with_ap(bA.ap[0:C, gg * 64:gg * 64 + C].unsqueeze(1).to_broadcast([C, 2, C]))
                    P.tt(tok["AT"][0:C, 2 * gg:2 * gg + 2, 0:C], src, tok["LT"][0:C, 2 * gg:2 * gg + 2, 0:C], ALU.mult)
                Xt = tok["Ktok"].with_ap(tok["Ktok"].ap[0:C, 0:256].rearrange("p (h c) -> p h c", h=4))
                wend = tok["LT"].with_ap(tok["LT"].ap[0:C, :, C - 1:C].to_broadcast([C, 4, 64]))
                P.tt(tok["Xh"][0:C], Xt, wend, ALU.mult, eng="pool")
                if g.sample:
                    P.copy(Sb[:, 0], S[:, si], eng="pool")
                for h in range(4):
                    j = h // 2
                    P.mm(bO[j][hp(h), cs], tok["Ktok"][0:C, h * 64:(h + 1) * 64], tok["AT"][0:C, h, 0:C], start=True, stop=False)
                    P.mm(bO[j][hp(h), cs], Sb[:, 0, h, :], QT[h][:, cs], start=False, stop=True)
                for h in range(4):
                    gg = h // 2
                    P.mm(b0[:, h * 64:(h + 1) * 64], tok["T2"][0:C, gg * 128:(gg + 1) * 128], tok["Xh"][0:C, h, :])
                ecb = tok["ECe"].with_ap(tok["ECe"].ap[:, :, c:c + 1].to_broadcast([128, 4, 64]))
                P.tt(tok["tmpS4"], S[:, si], ecb, ALU.mult)
                P.tt(S[:, si], tok["tmpS4"], b0.with_ap(b0.ap[:, 0:256].rearrange("p (h c) -> p h c", h=4)), ALU.add)
                if not g.sample:
                    P.copy(Sb[:, 0], S[:, 0], eng="act")
            mixed = [B[10][:, :tb], B[11][:, :tb]]

            def ep_s(j):
                y = F[2 + j][:, :tb]; t_ = F[4 + j][:, :tb]
                P.stt(y, F[9 + j][:, :tb], col(l, "ssm_D", j), bO[j][:, :tb], ALU.mult, ALU.add)
                yield
                P.tt(y, y, F[0 + j][:, :tb], ALU.mult)
                yield
                yield from head_norm_g(y, t_, 128.0, 1e-6, ones_f, j * 256)
                P.stt(B[10 + j][:, :tb], y, col(l, "ssm_g", j), t_, ALU.mult, ALU.mult)
                yield
            interleave([ep_s(0), ep_s(1)])
            defer_out("m", g, b, wo, mixed)
        flush_out("m")
        for si in range(g.nS):
            for j in range(2):
                bank = bk[(si * 2 + j) % 2]
                for hh in range(2):
                    P.mm(bank[hh * 64:hh * 64 + 64, 0:128], S[:, si, 2 * j + hh, :], ident_f)
                t_ = F[(si * 2 + j) % 8][:, 0:128]
                P.copy(t_, bank[:, 0:128], eng=("act" if j else "dve"))
                if g.sample:
                    P.dma(DV(dr["s_ssm"][l, si, 2 * j:2 * j + 2].rearrange("h p n -> (h p) n")), t_)
                else:
                    P.dma(DV(dr["p_ssm"][l, 2 * j:2 * j + 2].rearrange("h p n -> (h p) n")), t_)

    X["ssm"] = ssm
    run_rest2(X)

def run_all(X):
    P = X["P"]; dr = X["dr"]; DV = X["DV"]; F = X["F"]; B = X["B"]; bk = X["bk"]; b4 = X["b4"]
    bO = X["bO"]; bD = X["bD"]; tok = X["tok"]; col = X["col"]; pp = X["pp"]; lora = X["lora"]
    ident_f = X["ident_f"]; ident_b = X["ident_b"]; bd_f = X["bd_f"]; ones_f = X["ones_f"]; ones_b = X["ones_b"]
    mk = X["mk"]; eye4v = X["eye4v"]; reset = X["reset"]; sel = X["sel"]; r2c = X["r2c"]; eye16 = X["eye16"]
    v4 = X["v4"]; hp = X["hp"]; proj = X["proj"]; add_out = X["add_out"]; rms_rstd = X["rms_rstd"]
    head_norm = X["head_norm"]; lin_core = X["lin_core"]; state_io_T = X["state_io_T"]; state_io_N = X["state_io_N"]
    load_unit = X["load_unit"]; load_wo = X["load_wo"]; lbt = X["lbt"]; nA = X["nA"]; omka = X["omka"]
    groups = X["groups"]; gp = X["gp"]; gs = X["gs"]; norm_stage = X["norm_stage"]
    flush_out = X["flush_out"]; defer_out = X["defer_out"]
    head_norm_g = X["head_norm_g"]; interleave = X["interleave"]
    b0, b1, b2, b3, _, b5, b6, b7 = bk

    def load_x(g, src):
        nrt = max(1, g.T // 128)
        for rt in range(nrt):
            rows = min(128, g.T)
            for kc in range(8):
                st = F[(rt * 8 + kc) % 8]
                P.dma(st[0:rows, 0:128], DV(src[rt * 128:rt * 128 + rows, kc * 128:(kc + 1) * 128]))
                bank = bk[kc % 2]
                P.tr(bank[:, 0:rows], st[0:rows, 0:128], ident_f[0:rows, 0:rows])
                blk = (rt * 128) // g.TB
                off = (rt * 128) % g.TB
                P.copy(g.resid(kc, blk)[:, off:off + rows], bank[:, 0:rows], eng=("act" if kc % 2 else "dve"))

    def store_y(g, dst, l):
        for b in range(g.nblk):
            tb = g.TB
            sqs = []
            for kc in range(8):
                P.act(F[kc][:, :tb], g.resid(kc, b), AF.Square)
                sqs.append(F[kc][:, :tb])
            rs = F[8][:, :tb]
            rms_rstd(rs, sqs, 1.0 / 1024, 1e-6, ones_f)
            for kc in range(8):
                P.stt(F[9 + kc][:, :tb], g.resid(kc, b), col(0, "g_fin", kc), rs, ALU.mult, ALU.mult)
            for sub in range(max(1, tb // 128)):
                rows = min(128, tb)
                for kc in range(8):
                    bank = bk[kc % 2]
                    P.tr(bank[0:rows, 0:128], F[9 + kc][:, sub * 128:sub * 128 + rows], ident_f)
                    o_ = F[kc % 8][0:rows, 0:128]
                    P.copy(o_, bank[0:rows, 0:128], eng=("act" if kc % 2 else "dve"))
                    r0 = b * tb + sub * 128
                    P.dma(DV(dst[r0:r0 + rows, kc * 128:(kc + 1) * 128]), o_)

    def layer_params(l):
        for j in range(2):
            e0 = F[0][:, 0:1]; e1 = F[0][:, 1:2]; sm = F[0][:, 2:3]
            P.act(e0, col(l, "lbl0", j), AF.Exp)
            P.act(e1, col(l, "lbl1", j), AF.Exp)
            P.tt(sm, e0, e1, ALU.add)
            P.recip(sm, sm)
            lbv = lbt[:, l, j, 0:1]
            if l == 0:
                P.tt(lbv, e0, e0, ALU.subtract)
            else:
                P.tt(lbv, e1, sm, ALU.mult)
            P.ts(lbt[:, l, j, 1:2], lbv, -1.0, 1.0, ALU.mult, ALU.add)
            P.ts(lbt[:, l, j, 2:3], lbv, 1.0, -1.0, ALU.mult, ALU.add)
            P.ts(omka[:, l, j:j + 1], col(l, "k_a", j), -1.0, 1.0, ALU.mult, ALU.add)
        P.act(nA[:, l, :], pp[:, l, PC["A_log"]:PC["A_log"] + 4], AF.Exp)
        P.ts(nA[:, l, :], nA[:, l, :], -1.0, None, ALU.mult)

    def hgrn(l, g, W, wo):
        tb, C = g.TB, g.C
        S, Sb = g.S_h, g.Sb_h
        if g.sample:
            state_io_N(g, l, S, "st_hgrn", True)
        else:
            P.memset(S, 0.0)
            P.memset(Sb, 0.0)
        for b in range(g.nblk):
            for j in range(2):
                ps = proj(g, b, W, 0 + j * 128, 128, bk[j % 2]); P.act(F[0 + j][:, :tb], ps, AF.Silu)
            for j in range(2):
                ps = proj(g, b, W, 768 + j * 128, 128, bk[j % 2]); P.act(F[4 + j][:, :tb], ps, AF.Silu)
            for j in range(2):
                ps = proj(g, b, W, 256 + j * 128, 128, bk[j % 2]); P.act(F[2 + j][:, :tb], ps, AF.Sigmoid)
            for j in range(2):
                ps = proj(g, b, W, 512 + j * 128, 128, bk[j % 2]); P.act(B[0 + j][:, :tb], ps, AF.Copy)
            QT = [B[2][:, :tb], B[3][:, :tb]]; KT = [B[4][:, :tb], B[5][:, :tb]]; VT = [B[0][:, :tb], B[1][:, :tb]]
            EE = [F[10][:, :tb], F[11][:, :tb]]

            def prep_h(j):
                sig = F[2 + j][:, :tb]; fg = F[6 + j][:, :tb]
                P.ts(fg, sig, lbt[:, l, j, 1:2], lbt[:, l, j, 0:1], ALU.mult, ALU.add)
                yield
                P.act(fg, fg, AF.Ln)
                yield
                if g.sample:
                    bb = fg
                else:
                    bb = F[8 + j][:, :tb]
                    P.scan(bb, reset[:, :tb], fg, 0.0, ALU.mult, ALU.add)
                    yield
                eb = F[10 + j][:, :tb]; enb = F[12 + j][:, :tb]
                P.act(eb, bb, AF.Exp)
                yield
                P.act(enb, bb, AF.Exp, scale=-1.0)
                yield
                P.ts(sig, sig, lbt[:, l, j, 2:3], lbt[:, l, j, 1:2], ALU.mult, ALU.add)
                yield
                P.tt(B[2 + j][:, :tb], F[0 + j][:, :tb], eb, ALU.mult)
                yield
                P.tt(B[4 + j][:, :tb], sig, enb, ALU.mult, eng="pool")
                yield
            interleave([prep_h(0), prep_h(1)])
            flush_out("m")
            lin_core(g, QT, KT, VT, EE, S, Sb, 64, False)
            mixed = [B[6][:, :tb], B[7][:, :tb]]

            def ep_h(j):
                o = F[14 + j][:, :tb]; t_ = F[16 + j][:, :tb]
                P.copy(o, bO[j][:, :tb], eng="act")
                yield
                yield from head_norm_g(o, t_, 64.0, 1e-6, bd_f, j * 256)
                P.stt(o, o, col(l, "hg_g", j), t_, ALU.mult, ALU.mult)
                yield
                P.tt(B[6 + j][:, :tb], o, F[4 + j][:, :tb], ALU.mult)
                yield
            interleave([ep_h(0), ep_h(1)])
            defer_out("m", g, b, wo, mixed)
        flush_out("m")
        if g.sample:
            state_io_N(g, l, S, "s_hgrn", False)
        else:
            state_io_N(g, l, S, "p_hgrn", False)

    def mlstm(l, g, W, wo):
        tb, C = g.TB, g.C
        S, Sb = g.S_m, g.Sb_m
        if g.sample:
            for j in range(2):
                for hh in range(2):
                    P.dma(gs.m0[hh * 64:(hh + 1) * 64, j, :],
                          DV(dr["st_m"][l, :, 2 * j + hh:2 * j + hh + 1].rearrange("b h -> h b").to_broadcast([64, 16])))
            em0 = F[20][:, 0:32]
            P.act(em0, gs.m0.with_ap(gs.m0.ap.rearrange("p a b -> p (a b)")), AF.Exp)
            Sc = S.with_ap(S.ap[:, :, :, 0:64])
            state_io_T(g, l, Sc, "st_C", True)
            n0 = F[21][:, 0:32]
            for j in range(2):
                P.dma(n0[:, j * 16:(j + 1) * 16], DV(dr["st_n"][l, :, 2 * j:2 * j + 2, :].rearrange("b h k -> (h k) b")),
                      allow_slow_non_contiguous=True)
            for j in range(2):
                e_ = em0[:, j * 16:(j + 1) * 16]
                P.tt(S[:, j, :, 0:64], Sc[:, j, :, :], e_.with_ap(e_.ap.unsqueeze(2).to_broadcast([128, 16, 64])), ALU.mult)
                nn = F[22][:, 0:16]
                P.tt(nn, n0[:, j * 16:(j + 1) * 16], e_, ALU.mult)
                P.copy(S[:, j, :, 64:128], nn.with_ap(nn.ap.unsqueeze(2).to_broadcast([128, 16, 64])))
        else:
            P.memset(S, 0.0)
            P.memset(Sb, 0.0)
            P.memset(gp.carry, 0.0)
        P.memset(F[23][:, :tb], 1.0)
        P.memset(tok["VtP"], 1.0)
        for b in range(g.nblk):
            for j in range(2):
                ps = proj(g, b, W, 0 + j * 128, 128, bk[j % 2]); P.act(F[0 + j][:, :tb], ps, AF.Copy)
                ps = proj(g, b, W, 256 + j * 128, 128, bk[j % 2]); P.copy(F[2 + j][:, :tb], ps)
                ps = proj(g, b, W, 512 + j * 128, 128, bk[j % 2]); P.act(B[0 + j][:, :tb], ps, AF.Copy)
                ps = proj(g, b, W, 776 + j * 128, 128, bk[j % 2]); P.act(F[4 + j][:, :tb], ps, AF.Sigmoid)
            ps = proj(g, b, W, 768, 4, b0); P.copy(F[6][0:4, :tb], ps)
            ps = proj(g, b, W, 772, 4, b1); P.copy(F[7][0:4, :tb], ps)
            QT = [B[2][:, :tb], B[3][:, :tb]]; KT = [B[4][:, :tb], B[5][:, :tb]]; VT = [B[0][:, :tb], B[1][:, :tb]]
            EE = [F[12][:, :tb], F[13][:, :tb]]; MM = [F[14][:, :tb], F[15][:, :tb]]; ENM = [F[10][:, :tb], F[11][:, :tb]]

            def prep_m(j):
                li = F[8 + j][:, :tb]; lf = F[10 + j][:, :tb]
                o7 = slice(j * 256, j * 256 + tb)
                P.mm(b7[:, o7], sel[0:4, j * 128:(j + 1) * 128], F[6][0:4, :tb])
                P.mm(b2[:, o7], sel[0:4, j * 128:(j + 1) * 128], F[7][0:4, :tb])
                yield
                P.act(li, b7[:, o7], AF.Identity, bias=col(l, "ml_ib", j))
                yield
                P.act(lf, b2[:, o7], AF.Identity, bias=col(l, "ml_fb", j))
                yield
                P.act(lf, lf, AF.Exp, scale=-1.0)
                yield
                P.act(lf, lf, AF.Ln, bias=ones_f[:, 0:1])
                yield
                P.ts(lf, lf, -1.0, None, ALU.mult)
                yield
                m_ = F[14 + j][:, :tb]; g_ = F[16 + j][:, :tb]
                bb = F[12 + j][:, :tb]
                if g.sample:
                    P.copy(bb, lf, eng="pool")
                    yield
                    P.tt(m_, bb, gs.m0[:, j, :], ALU.add)
                    yield
                    P.tt(m_, m_, li, ALU.max)
                    yield
                else:
                    P.scan(bb, reset[:, :tb], lf, 0.0, ALU.mult, ALU.add)
                    yield
                    Bg = m_
                    P.scan(Bg, F[23][:, :tb], lf, gp.carry[:, j, 0:1], ALU.mult, ALU.add)
                    yield
                    P.copy(gp.carry[:, j, 0:1], Bg[:, tb - 1:tb])
                    ug = g_
                    P.tt(ug, li, Bg, ALU.subtract)
                    yield
                    Gg = F[18 + j][:, :tb]
                    P.scan(Gg, ug, ug, gp.carry[:, j, 1:2], ALU.max, ALU.max)
                    yield
                    P.copy(gp.carry[:, j, 1:2], Gg[:, tb - 1:tb])
                    P.tt(m_, Bg, Gg, ALU.add)
                    yield
                P.tt(g_, m_, bb, ALU.subtract)
                yield
                P.tt(li, li, bb, ALU.subtract)
                yield
                eg = F[18 + j][:, :tb]; eu = F[20 + j][:, :tb]
                P.act(eg, g_, AF.Exp, scale=-1.0)
                yield
                P.act(eu, li, AF.Exp)
                yield
                P.act(bb, bb, AF.Exp)
                yield
                P.act(lf, m_, AF.Exp, scale=-1.0)
                yield
                P.tt(B[2 + j][:, :tb], F[0 + j][:, :tb], eg, ALU.mult)
                yield
                P.stt(B[4 + j][:, :tb], F[2 + j][:, :tb], 0.125, eu, ALU.mult, ALU.mult)
                yield
            interleave([prep_m(0), prep_m(1)])
            flush_out("m")
            lin_core(g, QT, KT, VT, EE, S, Sb, 128, True)
            mixed = [B[6][:, :tb], B[7][:, :tb]]

            def ep_m(j):
                d_ = F[6 + j][:, :tb]; h_ = F[0 + j][:, :tb]; t_ = F[2 + j][:, :tb]
                P.act(d_, bD[j][:, :tb], AF.Abs)
                yield
                P.tt(d_, d_, ENM[j], ALU.max)
                yield
                P.recip(d_, d_)
                yield
                P.tt(h_, bO[j][:, :tb], d_, ALU.mult)
                yield
                yield from head_norm_g(h_, t_, 64.0, 1e-6, bd_f, j * 256)
                P.stt(h_, h_, col(l, "ml_g", j), t_, ALU.mult, ALU.mult)
                yield
                P.tt(B[6 + j][:, :tb], h_, F[4 + j][:, :tb], ALU.mult)
                yield
            interleave([ep_m(0), ep_m(1)])
            defer_out("m", g, b, wo, mixed)
            if b == g.nblk - 1:
                flush_out("m")
                Sc = S.with_ap(S.ap[:, :, :, 0:64])
                for j in range(2):
                    if g.sample:
                        e_ = ENM[j]
                        P.tt(Sc[:, j, :, :], S[:, j, :, 0:64], e_.with_ap(e_.ap.unsqueeze(2).to_broadcast([128, 16, 64])), ALU.mult)
                        nn = F[22][:, 0:16]
                        P.tt(nn, S[:, j, :, 64], e_, ALU.mult)
                        P.dma(DV(dr["s_n"][l, :, 2 * j:2 * j + 2, :].rearrange("b h k -> (h k) b")), nn, allow_slow_non_contiguous=True)
                        for hh in range(2):
                            P.dma(DV(dr["s_m"][l, :, 2 * j + hh:2 * j + hh + 1].rearrange("b h -> h b")), MM[j][hh * 64:hh * 64 + 1, :],
                                  allow_slow_non_contiguous=True)
                    else:
                        e_ = ENM[j][:, tb - 1:tb]
                        P.ts(Sc[:, j, 0, :], S[:, j, 0, 0:64], e_, None, ALU.mult)
                        nn = F[22][:, 0:1]
                        P.ts(nn, S[:, j, 0, 64:65], e_, None, ALU.mult)
                        P.dma(DV(dr["p_n"][l, 2 * j:2 * j + 2, :].rearrange("h (k o) -> (h k) o", o=1)), nn)
                        for hh in range(2):
                            P.dma(DV(dr["p_m"][l:l + 1, 2 * j + hh:2 * j + hh + 1]), MM[j][hh * 64:hh * 64 + 1, tb - 1:tb])
                state_io_T(g, l, Sc, "s_C" if g.sample else "p_C", False)

    X["hgrn"] = hgrn
    (rwkv_full if "rwkv_full" in globals() else rwkv_stub)(X)
    X["mlstm"] = mlstm
    X["load_x"] = load_x
    X["store_y"] = store_y
    X["layer_params"] = layer_params
    run_rest(X)

class Grp:
    pass


def build_program(dbg=None):
    nc = bass.Bass("TRN2", target_bir_lowering=False)
    dr = {}

    def din(n, shape):
        dr[n] = nc.dram_tensor(n, list(shape), F32, kind="ExternalInput").ap()

    def dout(n, shape):
        dr[n] = nc.dram_tensor(n, list(shape), F32, kind="ExternalOutput").ap()

    din("xp", [2048, 1024]); din("xs", [16, 1024]); din("memp", [256, 1024])
    din("st_wkv", [2, 16, 4, 64, 64]); din("st_shift", [2, 16, 896]); din("st_hgrn", [2, 16, 4, 64, 64])
    din("st_ssm", [2, 16, 4, 64, 128]); din("st_conv", [2, 16, 3, 768]); din("st_C", [2, 16, 4, 64, 64])
    din("st_n", [2, 16, 4, 64]); din("st_m", [2, 16, 4])
    din("ck", [2, 16, 256, 1024]); din("cv", [2, 16, 256, 1024])
    din("w_in", [2, 1024, 3980]); din("w_out", [2, 1024, 1024])
    for n in ("wq", "wk", "wv", "wo"):
        din(n, [2, 1024, 1024])
    din("ff_w1", [2, 1024, 4096]); din("ff_w2", [2, 4096, 1024])
    din("lora", [2, 128, 256]); din("pp", [2, 128, NPC])
    din("c_ident", [128, 128]); din("c_bd", [128, 128]); din("c_mask", [1, 64, 256]); din("c_rmask", [128, 512]); din("c_eyeP", [128, 128]); din("c_maskP", [1, 128, 128]);
    din("c_reset", [128, TPB]); din("c_sel", [4, 768]); din("c_r2", [2, 4]); din("c_eye16", [128, 256])
    dout("yp", [2048, 1024]); dout("ys", [16, 1024])
    dout("p_wkv", [2, 4, 64, 64]); dout("p_shift", [2, 896]); dout("p_hgrn", [2, 4, 64, 64]); dout("p_ssm", [2, 4, 64, 128])
    dout("p_conv", [2, 3, 768]); dout("p_C", [2, 4, 64, 64]); dout("p_n", [2, 4, 64]); dout("p_m", [2, 4])
    dout("p_mk", [2, 256, 1024]); dout("p_mv", [2, 256, 1024])
    dout("s_wkv", [2, 16, 4, 64, 64]); dout("s_shift", [2, 16, 896]); dout("s_hgrn", [2, 16, 4, 64, 64])
    dout("s_ssm", [2, 16, 4, 64, 128]); dout("s_conv", [2, 16, 3, 768]); dout("s_C", [2, 16, 4, 64, 64])
    dout("s_n", [2, 16, 4, 64]); dout("s_m", [2, 16, 4])

    es = ExitStack()
    with es:
        P = Prog(nc, es)
        DV = lambda ap: V(ap, ())

        ident_f = P.sb("ident_f", [128, 128], F32)
        ident_b = P.sb("ident_b", [128, 128], BF16)
        bd_f = P.sb("bd_f", [128, 128], F32)
        ones_f = P.sb("ones_f", [128, 128], F32)
        ones_b = P.sb("ones_b", [128, 128], BF16)
        masks = P.sb("masks", [64, 1, 256], F32)
        rmask = P.sb("rmask", [128, 512], BF16)
        eyeP = P.sb("eyeP", [128, 128], BF16)
        hist_r = P.sb("hist_r", [128, 8], F32)
        prev_r = P.sb("prev_r", [128, 7, 16], F32)
        S_rp = P.sb("S_rp", [128, 2, 1, 64], F32)
        Sb_rp = P.sb("Sb_rp", [128, 2, 1, 64], BF16)
        maskP = P.sb("maskP", [128, 1, 128], F32)
        reset = P.sb("reset", [128, TPB], F32)
        sel = P.sb("sel", [4, 768], F32)
        r2c = P.sb("r2c", [2, 4], F32)
        eye16 = P.sb("eye16", [128, 256], BF16)
        pp = P.sb("pp", [128, 2, NPC], F32)
        lora = P.sb("lora", [128, 2, 256], BF16)
        P.dma(ident_f, DV(dr["c_ident"]))
        P.dma(bd_f, DV(dr["c_bd"]))
        P.dma(masks, DV(dr["c_mask"].rearrange("m p c -> p m c")))
        P.dma(maskP, DV(dr["c_maskP"].rearrange("m p c -> p m c")))
        P.dma(rmask, DV(dr["c_rmask"]), eng="pool")
        P.dma(eyeP, DV(dr["c_eyeP"]), eng="pool")
        P.dma(reset, DV(dr["c_reset"]))
        P.dma(sel, DV(dr["c_sel"]))
        P.dma(r2c, DV(dr["c_r2"]))
        P.dma(eye16, DV(dr["c_eye16"]), eng="pool")
        P.dma(pp, DV(dr["pp"].rearrange("l p c -> p l c")))
        P.dma(lora, DV(dr["lora"].rearrange("l p c -> p l c")), eng="pool")
        epsc = P.sb("epsc", [128, 2], F32)
        P.memset(epsc[:, 0:1], 1e-6)
        P.memset(epsc[:, 1:2], 64e-5)
        P.copy(ident_b, ident_f)
        P.memset(ones_f, 1.0)
        P.memset(ones_b, 1.0)

        def mk(i, C):
            i = 0
            return masks.with_ap(masks.ap[0:C, i, :].rearrange("p (h c) -> p h c", h=4)[:, :, 0:C])
        def mkP(i, C):
            return maskP.with_ap(maskP.ap[:, i, :].rearrange("p (j c) -> p j c", j=2)[:, :, 0:C])
        eye4v = None

        bk = [P.ps("bk%d" % i, [128, 512], F32) for i in range(4)]
        b4 = P.ps("bk4", [128, 1024], BF16)
        bk += [b4] + [P.ps("bk%d" % i, [128, 512], F32) for i in range(5, 8)]
        b0, b1, b2, b3, _, b5, b6, b7 = bk
        bO = [b5, b6]
        bD = [b0, b1]

        def v4(bank, off, C):
            return bank.with_ap(bank.ap[0:C, off:off + 256].rearrange("p (h c) -> p h c", h=4)[:, :, 0:C])

        ring = [P.sb("ring%d" % i, [128, 8, 1040], BF16) for i in range(2)]
        P.arena_init(66 * 1024 - 128)
        F = [P.al("F%d" % i, [128, TPB + 4], F32) for i in range(NF)]
        B = [P.al("B%d" % i, [128, TPB], BF16) for i in range(NB)]
        X_common_off = P._aoff
        wor = [P.al("wor0", [128, 2, 1024], BF16)] * 2
        rstate = {"r": 0, "w": 0}

        units = []
        issued = set()

        def unit_tile(i):
            return ring[i % 2]

        def issue(i):
            if i in issued or i >= len(units):
                return
            issued.add(i)
            kind, a = units[i]
            t = ring[i % 2]
            if kind == "w":
                src_ap, ncols = a
                P.dma(t[:, :, 0:ncols], DV(src_ap.rearrange("(c p) n -> p c n", p=128)), eng="pool")
            elif kind == "ffn":
                l, e = a
                P.dma(t[:, :, 0:512], DV(dr["ff_w1"][l, :, e * 512:(e + 1) * 512].rearrange("(c p) n -> p c n", p=128)), eng="pool")
                for h_ in range(2):
                    dst = t.with_ap(t.ap[:, :, 512:1024].rearrange("p (c h) n -> p c h n", h=2)[:, :, h_, :])
                    P.dma(dst, DV(dr["ff_w2"][l, e * 512:(e + 1) * 512, h_ * 512:(h_ + 1) * 512].rearrange("(c p) n -> p c n", p=128)), eng="pool")
            elif kind == "wo":
                pass

        def need(idxs):
            for i in idxs:
                issue(i)
            nxt = max(idxs) + 1
            if nxt < len(units) and (nxt - 2) not in idxs:
                issue(nxt)
            return [ring[i % 2] for i in idxs]

        wo_list = []
        wo_issued = set()

        def issue_wo(i):
            if i in wo_issued or i >= len(wo_list):
                return
            wo_issued.add(i)
            l, r0 = wo_list[i]
            P.dma(wor[i % 2], DV(dr["w_out"][l, r0:r0 + 256, :].rearrange("(c p) n -> p c n", p=128)), eng="pool")

        def need_wo(i):
            issue_wo(i)
            return wor[i % 2]

        load_unit = None
        load_wo = None

        def mkgrp(name, T, TB, C):
            g = Grp()
            g.name, g.T, g.TB, g.C = name, T, TB, C
            g.nblk = T // TB
            g.nch = TB // C
            g.sample = (C == 1)
            rt = es.enter_context(nc.sbuf_tensor("sb_resid_" + name, [128, 8, T], F32))
            xt = es.enter_context(nc.sbuf_tensor("sb_xn_" + name, [128, 8, T], BF16))
            g.rreg = [[Region("r%s%d_%d" % (name, k, b)) for b in range(g.nblk)] for k in range(8)]
            g.xreg = [[Region("x%s%d_%d" % (name, k, b)) for b in range(g.nblk)] for k in range(8)]
            g.resid = lambda k, b: V(rt[:, k, b * TB:(b + 1) * TB], (g.rreg[k][b],))
            g.xn = lambda k, b: V(xt[:, k, b * TB:(b + 1) * TB], (g.xreg[k][b],))
            g.nS = 16 if g.sample else 1
            return g

        gp = mkgrp("p", 2048, TPB, CH)
        gs = mkgrp("s", 16, 16, 1)
        groups = [gp, gs]

        g = gp
        g.S_h = P.al("S_hp", [128, 2, 1, 64], F32); g.Sb_h = P.al("Sb_hp", [128, 2, 1, 64], BF16)
        g.S_m = P.al("S_mp", [128, 2, 1, 128], F32); g.Sb_m = P.al("Sb_mp", [128, 2, 1, 128], BF16)
        g.S_s = P.al("S_sp", [128, 1, 4, 64], F32); g.Sb_s = P.al("Sb_sp", [128, 1, 4, 64], BF16)
        Ssh = P.al("Ssh", [128, 4096], F32)
        SbC = P.al("SbC", [128, 2, 1, 128], BF16)
        SbC4 = P.al("SbC4", [128, 1, 4, 64], BF16)
        g = gs
        g.S_h = Ssh.with_ap(Ssh.ap[:, 0:2048].rearrange("p (a b c) -> p a b c", a=2, b=16, c=64))
        g.Sb_h = SbC.with_ap(SbC.ap[:, :, :, 0:64])
        g.S_m = Ssh.with_ap(Ssh.ap.rearrange("p (a b c) -> p a b c", a=2, b=16, c=128))
        g.Sb_m = SbC
        g.S_s = Ssh.with_ap(Ssh.ap.rearrange("p (a b c) -> p a b c", a=16, b=4, c=64))
        g.Sb_s = SbC4
        gp.hist_c = P.al("hist_c", [128, 6, 4], F32)
        gp.carry = P.al("carry_ml", [128, 2, 2], F32)
        gs.hist_c = P.al("hist_cs", [128, 6, 16, 3], F32)
        gs.m0 = P.al("m0_s", [128, 2, 16], F32)
        gp.S_r, gp.Sb_r = S_rp, Sb_rp
        gs.S_r = Ssh.with_ap(Ssh.ap[:, 0:2048].rearrange("p (a b c) -> p a b c", a=2, b=16, c=64))
        gs.Sb_r = SbC.with_ap(SbC.ap[:, :, :, 0:64])
        gp.hist_r = hist_r
        gs.prev_r = prev_r
        tok = {
               "TT": P.al("TTr", [128, 4, 2, 64], BF16), "SC": P.al("SCr", [128, 4, 2, 64], BF16),
               "SCb": P.al("SCbr", [128, 2, 64], BF16), "Y": P.al("Yr", [128, 2, 128], BF16),
               "Y2": P.al("Y2r", [128, 2, 128], BF16), "PP0": P.al("PP0r", [128, 2, 2, 64], BF16),
               "PP1": P.al("PP1r", [128, 2, 2, 64], BF16), "IP": P.al("IPr", [128, 2, 64], BF16),
               "ATp": P.al("ATpr", [128, 2, 64], BF16), "Ut": P.al("Utr", [128, 2, 64], BF16),

               "AT": P.al("ATm", [64, 4, 64], BF16),
               "LT": P.al("LTm", [64, 4, 64], F32), "Xh": P.al("Xhat", [64, 4, 64], BF16),
               "ECe": P.al("ECend", [128, 4, 16], F32), "tmpS": P.al("tmpS", [128, 2, 128], F32),
               }
        ttf = tok["TT"].ap.rearrange("p a j c -> p (a j c)")
        tok["KtP"] = tok["TT"].with_ap(ttf[:, 0:128].rearrange("p (j c) -> p j c", j=2))
        tok["VtP"] = tok["TT"].with_ap(ttf[:, 128:384].rearrange("p (j c) -> p j c", j=2))
        tok["ATP"] = tok["TT"].with_ap(ttf[:, 384:512].rearrange("p (j c) -> p j c", j=2))
        tok["tmpS4"] = tok["tmpS"].with_ap(tok["tmpS"].ap.rearrange("p a c -> p (a c)").rearrange("p (h c) -> p h c", h=4))
        tok["Ktok"] = tok["TT"].with_ap(tok["TT"].ap.rearrange("p a j c -> p (a j c)")[0:64, :])
        tok["T2"] = tok["SC"].with_ap(tok["SC"].ap.rearrange("p a j c -> p (a j c)")[0:64, :])
        print("arena MIX end", P._aoff * 4)
        P._aoff = X_common_off
        lbt = P.sb("lbt", [128, 2, 2, 3], F32)
        nA = P.sb("nA", [128, 2, 4], F32)
        omka = P.sb("omka", [128, 2, 2], F32)

        col = lambda l, name, j=0: pp[:, l, PC[name] + j:PC[name] + j + 1]

        def hp(h):
            return slice((h % 2) * 64, (h % 2) * 64 + 64)

        def load_x(g, src, nrow_tiles):
            for rtile in range(nrow_tiles):
                rows = min(128, g.T)
                xt = F[rtile % 2 * 8:rtile % 2 * 8 + 8]
                st = P.sb("xstage%d" % (rtile % 2), [128, 1024], F32) if rtile < 2 and not hasattr(g, "_st%d" % (rtile % 2)) else None
                if st is not None:
                    setattr(g, "_st%d" % (rtile % 2), st)
                st = getattr(g, "_st%d" % (rtile % 2))
                P.dma(st[0:rows, :], DV(src[rtile * 128:rtile * 128 + rows, :]))
                for kc in range(8):
                    bank = bk[kc % 2]
                    P.tr(bank[:, 0:rows], st[0:rows, kc * 128:(kc + 1) * 128], ident_f[0:rows, 0:rows])
                    blk = (rtile * 128) // g.TB
                    off = (rtile * 128) % g.TB
                    dst = g.resid(kc, blk)
                    P.copy(dst[:, off:off + rows], bank[:, 0:rows], eng=("act" if kc % 2 else "dve"))

        def rms_rstd(dst, srcs, scale, eps, lhs):
            n = srcs[0].ap.shape[-1]
            for i, s_ in enumerate(srcs):
                P.mm(b7[:, 0:n], lhs, s_, start=(i == 0), stop=(i == len(srcs) - 1))
            P.ts(dst, b7[:, 0:n], scale, eps, ALU.mult, ALU.add)
            P.act(dst, dst, AF.Sqrt)
            P.recip(dst, dst)

        def rms_rstd_g(dst, srcs, scale, eps, lhs, off=0):
            n = srcs[0].ap.shape[-1]
            for i, s_ in enumerate(srcs):
                P.mm(b7[:, off:off + n], lhs, s_, start=(i == 0), stop=(i == len(srcs) - 1))
            yield
            P.ts(dst, b7[:, off:off + n], scale, eps, ALU.mult, ALU.add)
            yield
            P.act(dst, dst, AF.Sqrt)
            yield
            P.recip(dst, dst)
            yield

        def head_norm_g(o, tmp, div, eps, lhs, off=0):
            P.act(tmp, o, AF.Square)
            yield
            yield from rms_rstd_g(tmp, [tmp], 1.0 / div, eps, lhs, off)

        def interleave(gens):
            gens = list(gens)
            while gens:
                for g_ in list(gens):
                    try:
                        next(g_)
                    except StopIteration:
                        gens.remove(g_)

        def norm_stage(g, l, gname, xn_fn=None):
            xn_fn = xn_fn or g.xn
            for b in range(g.nblk):
                tb = g.TB
                sqs = []
                for kc in range(8):
                    sq = F[kc]
                    P.act(sq[:, :tb], g.resid(kc, b), AF.Square)
                    sqs.append(sq[:, :tb])
                rs = F[8][:, :tb]
                rms_rstd(rs, sqs, 1.0 / 1024, 1e-6, ones_f)
                for kc in range(8):
                    P.stt(xn_fn(kc, b), g.resid(kc, b), col(l, gname, kc), rs, ALU.mult, ALU.mult,
                          eng=("dve" if kc % 2 else "pool"))

        def proj(g, b, W, c0, M, bank):
            tb = g.TB
            for kc in range(8):
                P.mm(bank[0:M, 0:tb], W[:, kc, c0:c0 + M], g.xn(kc, b), start=(kc == 0), stop=(kc == 7))
            return bank[0:M, 0:tb]

        def add_out(g, b, wo, mixed):
            tb = g.TB
            for mc in range(8):
                bank = bk[mc % 2]
                for kk, m_ in enumerate(mixed):
                    P.mm(bank[:, 0:tb], wo[:, kk, mc * 128:(mc + 1) * 128], m_, start=(kk == 0), stop=(kk == len(mixed) - 1))
                P.tt(g.resid(mc, b), g.resid(mc, b), bank[:, 0:tb], ALU.add)

        def head_norm(o, tmp, div, eps, lhs):
            P.tt(tmp, o, o, ALU.mult, eng="pool")
            rms_rstd(tmp, [tmp], 1.0 / div, eps, lhs)

        def state_io_T(g, l, S, dname, load):
            st = F[0:8]
            for j in range(2):
                for b_ in range(g.nS):
                    if g.sample:
                        dsl = dr[dname][l, b_, 2 * j:2 * j + 2].rearrange("h a c -> (h a) c")
                    else:
                        dsl = dr[dname][l, 2 * j:2 * j + 2].rearrange("h a c -> (h a) c")
                    t_ = st[b_ % 8][:, 0:64]
                    bank = bk[b_ % 2]
                    if load:
                        P.dma(t_, DV(dsl))
                        for hh in range(2):
                            sl = slice(hh * 64, hh * 64 + 64)
                            P.mm(bank[sl, 0:64], t_[sl, :], ident_f[sl, sl])
                        P.copy(S[:, j, b_, :], bank[:, 0:64], eng=("act" if b_ % 2 else "dve"))
                    else:
                        for hh in range(2):
                            sl = slice(hh * 64, hh * 64 + 64)
                            P.mm(bank[sl, 0:64], S[sl, j, b_, :], ident_f[sl, sl])
                        P.copy(t_, bank[:, 0:64], eng=("act" if b_ % 2 else "dve"))
                        P.dma(DV(dsl), t_)

        def state_io_N(g, l, S, dname, load):
            for j in range(2):
                if g.sample:
                    dsl = dr[dname][l, :, 2 * j:2 * j + 2].rearrange("b h k v -> (h k) b v")
                else:
                    dsl = dr[dname][l, 2 * j:2 * j + 2].rearrange("h k v -> (h k) v")
                sv = S[:, j, :, :] if g.sample else S[:, j, 0, :]
                if load:
                    P.dma(sv, DV(dsl))
                else:
                    P.dma(DV(dsl), sv)

        def lin_core(g, QT, KT, VT, eend, S, Sb, dvv, den):
            C, tb = g.C, g.TB
            Kt, Vt, AT = tok["KtP"], tok["VtP"], tok["ATP"]
            p2 = lambda bank, off: bank.with_ap(bank.ap[:, off:off + 128].rearrange("p (j c) -> p j c", j=2))
            for c in range(g.nch):
                cs = slice(c * C, (c + 1) * C)
                si = c if g.sample else 0
                for h in range(4):
                    j = h // 2
                    hb = (h % 2) * 64
                    P.mm(b3[hb:hb + C, j * 64:(j + 1) * 64], KT[j][hp(h), cs], ident_b[hp(h), hp(h)])
                    P.mm(b3[hb:hb + C, 128 + j * 64:128 + (j + 1) * 64], VT[j][hp(h), cs], ident_b[hp(h), hp(h)])
                P.copy(Kt, p2(b3, 0), eng="act")
                P.copy(Vt[:, :, 0:64], p2(b3, 128))
                if g.sample:
                    P.copy(Sb[:, :, 0, :], S[:, :, si, :], eng="act")
                for h in range(4):
                    j = h // 2
                    hb = (h % 2) * 64
                    P.mm(b2[hb:hb + C, j * 64:j * 64 + C], KT[j][hp(h), cs], QT[j][hp(h), cs])
                P.tt(AT[:, :, 0:C], p2(b2, 0)[:, :, 0:C], mkP(0, C), ALU.mult)
                for h in range(4):
                    j = h // 2
                    hb = (h % 2) * 64
                    P.mm(bO[j][hp(h), cs], Vt[hb:hb + C, j, 0:64], AT[hb:hb + C, j, 0:C], start=True, stop=False)
                    P.mm(bO[j][hp(h), cs], Sb[hp(h), j, 0, 0:64], QT[j][hp(h), cs], start=False, stop=True)
                    if den:
                        P.mm(bD[j][hp(h), cs], Vt[hb:hb + C, j, 64:128], AT[hb:hb + C, j, 0:C], start=True, stop=False)
                        P.mm(bD[j][hp(h), cs], Sb[hp(h), j, 0, 64:128], QT[j][hp(h), cs], start=False, stop=True)
                for h in range(4):
                    j = h // 2
                    hb = (h % 2) * 64
                    P.mm(b7[hp(h), j * 128:j * 128 + dvv], Kt[hb:hb + C, j, :], Vt[hb:hb + C, j, 0:dvv])
                ce = (c + 1) * C - 1
                for j in range(2):
                    tm = tok["tmpS"][:, j, 0:dvv]
                    P.tt(tm, S[:, j, si, :], b7[:, j * 128:j * 128 + dvv], ALU.add)
                    P.ts(S[:, j, si, :], tm, eend[j][:, ce:ce + 1], None, ALU.mult)
                    if not g.sample:
                        P.act(Sb[:, j, 0, :], tm, AF.Copy, scale=eend[j][:, ce:ce + 1])

        pend = {}

        def flush_out(key):
            a = pend.pop(key, None)
            if a is not None:
                add_out(*a)

        def defer_out(key, g, b, wo, mixed):
            flush_out(key)
            pend[key] = (g, b, wo, mixed)

        ctx = dict(rms_rstd_g=rms_rstd_g, head_norm_g=head_norm_g, interleave=interleave, flush_out=flush_out, defer_out=defer_out, P=P, dr=dr, DV=DV, F=F, B=B, bk=bk, b4=b4, bO=bO, bD=bD, tok=tok, col=col, pp=pp, lora=lora,
                   ident_f=ident_f, ident_b=ident_b, bd_f=bd_f, ones_f=ones_f, ones_b=ones_b, mk=mk, mkP=mkP, eye4v=eye4v, rmask=rmask, eyeP=eyeP,
                   reset=reset, sel=sel, r2c=r2c, eye16=eye16, v4=v4, hp=hp, proj=proj, add_out=add_out,
                   rms_rstd=rms_rstd, head_norm=head_norm, lin_core=lin_core, state_io_T=state_io_T,
                   state_io_N=state_io_N, load_unit=load_unit, load_wo=load_wo, units=units, need=need, wo_list=wo_list, need_wo=need_wo, epsc=epsc, lbt=lbt, nA=nA, omka=omka,
                   X_common_off=X_common_off, groups=groups, gp=gp, gs=gs, norm_stage=norm_stage, load_x=load_x, nc=nc, es=es)
        run_all(ctx)
        P.build()
    return nc

_NC_CACHE = {}


def kernel(**inp):
    f = np.float32
    inp = {k: np.asarray(v) for k, v in inp.items()}
    if "nc" not in _NC_CACHE:
        _NC_CACHE["nc"] = build_program()
    nc = _NC_CACHE["nc"]
    consts = make_consts()
    pp = pack_params(inp)
    lora = np.concatenate([inp["rwkv_w_up"], inp["rwkv_a_up"], inp["rwkv_g_up"]], axis=1).astype(f)
    shared = {"w_in": inp["w_in"], "w_out": inp["w_out"], "wq": inp["xattn_wq"], "wk": inp["xattn_wk"],
              "wv": inp["xattn_wv"], "wo": inp["xattn_wo"], "ff_w1": inp["ff_w1"], "ff_w2": inp["ff_w2"],
              "lora": lora, "pp": pp}
    shared.update(consts)
    shared = {k: np.ascontiguousarray(v, dtype=f) for k, v in shared.items()}
    in_maps = []
    for c in range(8):
        sl = slice(16 * c, 16 * c + 16)
        m = dict(shared)
        m["xp"] = np.ascontiguousarray(inp["x_prompt"][c], f)
        m["xs"] = np.ascontiguousarray(inp["x_sample"][sl, 0, :], f)
        m["memp"] = np.ascontiguousarray(inp["mem_prompt"][c], f)
        m["st_wkv"] = np.ascontiguousarray(inp["state_rwkv_wkv"][:, sl], f)
        m["st_shift"] = np.ascontiguousarray(inp["state_rwkv_shift"][:, sl], f)
        m["st_hgrn"] = np.ascontiguousarray(inp["state_hgrn"][:, sl], f)
        m["st_ssm"] = np.ascontiguousarray(inp["state_ssm"][:, sl], f)
        m["st_conv"] = np.ascontiguousarray(inp["state_ssm_conv"][:, sl], f)
        m["st_C"] = np.ascontiguousarray(inp["state_mlstm_C"][:, sl], f)
        m["st_n"] = np.ascontiguousarray(inp["state_mlstm_n"][:, sl], f)
        m["st_m"] = np.ascontiguousarray(inp["state_mlstm_m"][:, sl], f)
        m["ck"] = np.ascontiguousarray(inp["cache_mem_k"][:, sl].reshape(2, 16, 256, 1024), f)
        m["cv"] = np.ascontiguousarray(inp["cache_mem_v"][:, sl].reshape(2, 16, 256, 1024), f)
        in_maps.append(m)
    res = run_bass_kernel_spmd(nc, in_maps, core_ids=list(range(8)))
    R = res.results
    cat0 = lambda n: np.stack([np.asarray(R[c][n]) for c in range(8)], axis=0)
    y_prompt = cat0("yp").astype(f)
    y_sample = np.concatenate([np.asarray(R[c]["ys"]) for c in range(8)], axis=0)[:, None, :].astype(f)
    outs = [y_prompt, y_sample]
    for n in ("p_wkv", "p_shift", "p_hgrn", "p_ssm", "p_conv", "p_C", "p_n", "p_m"):
        outs.append(np.stack([np.asarray(R[c][n]) for c in range(8)], axis=1).astype(f))
    for n in ("p_mk", "p_mv"):
        outs.append(np.stack([np.asarray(R[c][n]) for c in range(8)], axis=1).reshape(2, 8, 256, 4, 256).astype(f))
    for n in ("s_wkv", "s_shift", "s_hgrn", "s_ssm", "s_conv", "s_C", "s_n", "s_m"):
        outs.append(np.concatenate([np.asarray(R[c][n]) for c in range(8)], axis=1).astype(f))
    return tuple(outs)
```
